# Optimizing a Trainium2 kernel written in Bass

```python
import math
import jax, jax.numpy as jnp
from jax import lax
import numpy as np

D_MODEL = 2048
BATCH = 16
SEQ = 2048
DEPTH = 1
DEC_BATCH = 16
DEC_SEQ = 64
PAST_LEN = 2048

CHUNK = 64
N_META = 16
EPS = 1e-6
D_MIX = D_MODEL
A_HEAD_DIM = 64
A_WIDTH = D_MIX // 2
A_HEADS = A_WIDTH // A_HEAD_DIM
DECAY_RANK = 64
AAA_RANK = 64
GATE_RANK = 160
LNX_EPS = 64e-5
B_HEAD_DIM = 64
B_WIDTH = D_MIX - A_WIDTH
B_HEADS = B_WIDTH // B_HEAD_DIM
B_GROUPS = 2
D_STATE = 128
CONV_W = 4
B_CONV_DIM = B_WIDTH + 2 * B_GROUPS * D_STATE
A_COLS = 3 * A_WIDTH + DECAY_RANK + AAA_RANK + GATE_RANK
B_COLS = B_WIDTH + B_CONV_DIM + B_HEADS
IN_COLS = A_COLS + B_COLS
D_FF = ((8 * D_MODEL // 3 + 255) // 256) * 256

kernel_name = 'hybrid_rwkv7_mamba2_stream'

F32 = jnp.float32


def rmsnorm(x, w):
    xf = x.astype(F32)
    y = xf * lax.rsqrt(jnp.mean(xf * xf, axis=-1, keepdims=True) + EPS)
    return (y * w.astype(F32)).astype(x.dtype)


def wkv7_scan(r, w, k, v, kk, ka, S0):
    def step(S, inp):
        r_t, w_t, k_t, v_t, kk_t, ka_t = inp
        sa = jnp.einsum('bhij,bhj->bhi', S, kk_t)
        S = S * w_t[:, :, None, :] - sa[..., None] * ka_t[:, :, None, :] + v_t[..., None] * k_t[:, :, None, :]
        return S, jnp.einsum('bhij,bhj->bhi', S, r_t)
    xs = tuple(jnp.moveaxis(t, 1, 0) for t in (r, w, k, v, kk, ka))
    S, ys = lax.scan(step, S0, xs)
    return jnp.moveaxis(ys, 0, 1), S


def rwkv7_mix(za, shift0, wkv0, mu, w0, w_up, a0, a_up, g_up, k_k, k_a, r_k, lnx_w, lnx_b):
    Bsz, L, _ = za.shape
    z_prev = jnp.concatenate([shift0.astype(F32)[:, None, :], za[:, :-1]], axis=1)
    zm = za + mu * (z_prev - za)
    r, k, v, wd, ad, gd = jnp.split(zm, [A_WIDTH, 2 * A_WIDTH, 3 * A_WIDTH, 3 * A_WIDTH + DECAY_RANK,
                                         3 * A_WIDTH + DECAY_RANK + AAA_RANK], axis=-1)
    w = -jax.nn.softplus(-(w0 + jnp.tanh(wd) @ w_up)) - 0.5
    decay = jnp.exp(-jnp.exp(w))
    a = jax.nn.sigmoid(a0 + ad @ a_up)
    g = jax.nn.sigmoid(gd) @ g_up
    heads = lambda t: t.reshape(Bsz, L, A_HEADS, A_HEAD_DIM)
    kk = heads(k * k_k)
    kk = kk / jnp.maximum(jnp.linalg.norm(kk, axis=-1, keepdims=True), 1e-12)
    k = k * (1.0 + (a - 1.0) * k_a)
    r_h, k_h, v_h, a_h = heads(r), heads(k), heads(v), heads(a)
    y, S = wkv7_scan(r_h, heads(decay), k_h, v_h, kk, kk * a_h, wkv0.astype(F32))
    mean = jnp.mean(y, axis=-1, keepdims=True)
    var = jnp.mean(jnp.square(y - mean), axis=-1, keepdims=True)
    y = ((y - mean) * lax.rsqrt(var + LNX_EPS)).reshape(Bsz, L, A_WIDTH) * lnx_w + lnx_b
    y = y + (jnp.sum(r_h * k_h * r_k, axis=-1, keepdims=True) * v_h).reshape(Bsz, L, A_WIDTH)
    return y * g, za[:, -1], S


def ssd_chunked(x, dt, A, Bm, Cm, S0, chunk):
    Bsz, L = x.shape[:2]
    nc = L // chunk

    def blocks(t):
        return jnp.moveaxis(t.reshape((Bsz, nc, chunk) + t.shape[2:]), 1, 0)

    causal = jnp.tril(jnp.ones((chunk, chunk), dtype=bool))[None, :, :, None, None]

    def step(S, inp):
        xc, dtc, Bc, Cc = inp
        a_cs = jnp.cumsum(dtc * A, axis=1)
        seg = a_cs[:, :, None] - a_cs[:, None, :]
        decay_qs = jnp.exp(jnp.where(causal, seg, -jnp.inf))
        xdt = xc * dtc[..., None]
        cb = jnp.einsum('bqgn,bsgn->bqsg', Cc, Bc)
        y = jnp.einsum('bqsg,bqsgh,bsghp->bqghp', cb, decay_qs, xdt)
        y = y + jnp.einsum('bqgn,bghpn->bqghp', Cc, S) * jnp.exp(a_cs)[..., None]
        to_end = jnp.exp(a_cs[:, -1:] - a_cs)
        S = S * jnp.exp(a_cs[:, -1])[..., None, None] + jnp.einsum('bsgn,bsgh,bsghp->bghpn', Bc, to_end, xdt)
        return S, y

    S, ys = lax.scan(step, S0, (blocks(x), blocks(dt), blocks(Bm), blocks(Cm)))
    return jnp.moveaxis(ys, 0, 1).reshape(x.shape), S


def mamba2_mix(zb, conv0, ssm0, segments, conv_w, conv_b, dt_bias, A_log, D_skip, norm_w):
    Bsz, L, _ = zb.shape
    Hg = B_HEADS // B_GROUPS
    z, xbc, dt = jnp.split(zb, [B_WIDTH, B_WIDTH + B_CONV_DIM], axis=-1)
    xpad = jnp.concatenate([conv0.astype(F32), xbc], axis=1)
    taps = conv_w.astype(F32).T[:, None, :]
    conv = lax.conv_general_dilated(xpad, taps, (1,), 'VALID',
                                    dimension_numbers=('NWC', 'WIO', 'NWC'),
                                    feature_group_count=B_CONV_DIM)
    xbc = jax.nn.silu(conv + conv_b)
    xs, Bm, Cm = jnp.split(xbc, [B_WIDTH, B_WIDTH + B_GROUPS * D_STATE], axis=-1)
    xs = xs.reshape(Bsz, L, B_GROUPS, Hg, B_HEAD_DIM)
    Bm = Bm.reshape(Bsz, L, B_GROUPS, D_STATE)
    Cm = Cm.reshape(Bsz, L, B_GROUPS, D_STATE)
    dt = jax.nn.softplus(dt + dt_bias).reshape(Bsz, L, B_GROUPS, Hg)
    A = -jnp.exp(A_log.astype(F32)).reshape(B_GROUPS, Hg)
    S = ssm0.astype(F32).reshape(Bsz, B_GROUPS, Hg, B_HEAD_DIM, D_STATE)
    ys = []
    start = 0
    for seg_len, chunk in segments:
        sl = slice(start, start + seg_len)
        y_seg, S = ssd_chunked(xs[:, sl], dt[:, sl], A, Bm[:, sl], Cm[:, sl], S, chunk)
        ys.append(y_seg)
        start += seg_len
    y = jnp.concatenate(ys, axis=1) + xs * D_skip.reshape(B_GROUPS, Hg, 1)
    y = y.reshape(Bsz, L, B_WIDTH) * jax.nn.silu(z)
    yg = y.reshape(Bsz, L, B_GROUPS, B_WIDTH // B_GROUPS)
    yg = yg * lax.rsqrt(jnp.mean(yg * yg, axis=-1, keepdims=True) + EPS)
    y = yg.reshape(Bsz, L, B_WIDTH) * norm_w
    return y, xpad[:, -(CONV_W - 1):], S.reshape(Bsz, B_HEADS, B_HEAD_DIM, D_STATE)


def hybrid_layer(x, shift0, wkv0, conv0, ssm0, segments,
                 norm_mix_w, w_in, rwkv_mu, rwkv_w0, rwkv_w_up, rwkv_a0, rwkv_a_up, rwkv_g_up,
                 rwkv_k_k, rwkv_k_a, rwkv_r_k, rwkv_lnx_w, rwkv_lnx_b,
                 ssm_conv_w, ssm_conv_b, ssm_dt_bias, ssm_A_log, ssm_D, ssm_norm_w,
                 w_out, norm_ffn_w, ffn_w_gate, ffn_w_up, ffn_w_down):
    dtype = x.dtype
    h = rmsnorm(x, norm_mix_w)
    proj = (h @ w_in).astype(F32)
    ya, shift_new, wkv_new = rwkv7_mix(proj[..., :A_COLS], shift0, wkv0, rwkv_mu, rwkv_w0, rwkv_w_up,
                                       rwkv_a0, rwkv_a_up, rwkv_g_up, rwkv_k_k, rwkv_k_a, rwkv_r_k,
                                       rwkv_lnx_w, rwkv_lnx_b)
    yb, conv_new, ssm_new = mamba2_mix(proj[..., A_COLS:], conv0, ssm0, segments, ssm_conv_w, ssm_conv_b,
                                       ssm_dt_bias, ssm_A_log, ssm_D, ssm_norm_w)
    x = x + jnp.concatenate([ya, yb], axis=-1).astype(dtype) @ w_out
    h2 = rmsnorm(x, norm_ffn_w)
    x = x + (jax.nn.silu(h2 @ ffn_w_gate) * (h2 @ ffn_w_up)) @ ffn_w_down
    return (x, shift_new.astype(dtype), wkv_new.astype(dtype), conv_new.astype(dtype), ssm_new.astype(dtype))


def run_trunk(x, shift, wkv, conv, ssm, segments, layer_weights, norm_final_w):
    new_shift, new_wkv, new_conv, new_ssm = [], [], [], []
    for l in range(DEPTH):
        x, s1, s2, s3, s4 = hybrid_layer(x, shift[l], wkv[l], conv[l], ssm[l], segments,
                                         *[w[l] for w in layer_weights])
        new_shift.append(s1)
        new_wkv.append(s2)
        new_conv.append(s3)
        new_ssm.append(s4)
    return (rmsnorm(x, norm_final_w), jnp.stack(new_shift), jnp.stack(new_wkv),
            jnp.stack(new_conv), jnp.stack(new_ssm))


def setup_inputs(seed: int = 0) -> dict:
    key = jax.random.key(seed)
    k = jax.random.split(key, 32)
    nrm = lambda kk, shape, scale: scale * jax.random.normal(kk, shape, F32)
    unif = lambda kk, shape, lo, hi: jax.random.uniform(kk, shape, F32, lo, hi)
    dt_init = jnp.exp(unif(k[20], (DEPTH, B_HEADS), math.log(1e-3), math.log(1e-1)))
    return {
        'x_prompt': nrm(k[0], (BATCH, SEQ, D_MODEL), 1.0),
        'x_sample': nrm(k[1], (DEC_BATCH, DEC_SEQ, D_MODEL), 1.0),
        'state_rwkv_shift': nrm(k[2], (DEPTH, DEC_BATCH, A_COLS), 1.0),
        'state_rwkv_wkv': nrm(k[3], (DEPTH, DEC_BATCH, A_HEADS, A_HEAD_DIM, A_HEAD_DIM), 0.1),
        'state_ssm_conv': nrm(k[4], (DEPTH, DEC_BATCH, CONV_W - 1, B_CONV_DIM), 1.0),
        'state_ssm': nrm(k[5], (DEPTH, DEC_BATCH, B_HEADS, B_HEAD_DIM, D_STATE), 0.1),
        'meta_tokens': nrm(k[6], (N_META, D_MODEL), 1.0),
        'norm_mix_w': 1.0 + nrm(k[7], (DEPTH, D_MODEL), 0.02),
        'w_in': nrm(k[8], (DEPTH, D_MODEL, IN_COLS), D_MODEL ** -0.5),
        'rwkv_mu': unif(k[9], (DEPTH, A_COLS), 0.0, 1.0),
        'rwkv_w0': unif(k[10], (DEPTH, A_WIDTH), -5.5, -0.5),
        'rwkv_w_up': nrm(k[11], (DEPTH, DECAY_RANK, A_WIDTH), 0.5 * DECAY_RANK ** -0.5),
        'rwkv_a0': nrm(k[12], (DEPTH, A_WIDTH), 0.1),
        'rwkv_a_up': nrm(k[13], (DEPTH, AAA_RANK, A_WIDTH), AAA_RANK ** -0.5),
        'rwkv_g_up': nrm(k[14], (DEPTH, GATE_RANK, A_WIDTH), GATE_RANK ** -0.5),
        'rwkv_k_k': 0.85 + nrm(k[15], (DEPTH, A_WIDTH), 0.02),
        'rwkv_k_a': 1.0 + nrm(k[16], (DEPTH, A_WIDTH), 0.02),
        'rwkv_r_k': nrm(k[17], (DEPTH, A_HEADS, A_HEAD_DIM), 0.1),
        'rwkv_lnx_w': 1.0 + nrm(k[18], (DEPTH, A_WIDTH), 0.02),
        'rwkv_lnx_b': nrm(k[19], (DEPTH, A_WIDTH), 0.01),
        'ssm_conv_w': nrm(k[21], (DEPTH, B_CONV_DIM, CONV_W), CONV_W ** -0.5),
        'ssm_conv_b': nrm(k[22], (DEPTH, B_CONV_DIM), 0.01),
        'ssm_dt_bias': dt_init + jnp.log(-jnp.expm1(-dt_init)),
        'ssm_A_log': jnp.log(unif(k[23], (DEPTH, B_HEADS), 1.0, 16.0)),
        'ssm_D': 1.0 + nrm(k[24], (DEPTH, B_HEADS), 0.02),
        'ssm_norm_w': 1.0 + nrm(k[25], (DEPTH, B_WIDTH), 0.02),
        'w_out': nrm(k[26], (DEPTH, D_MIX, D_MODEL), D_MIX ** -0.5),
        'norm_ffn_w': 1.0 + nrm(k[27], (DEPTH, D_MODEL), 0.02),
        'ffn_w_gate': nrm(k[28], (DEPTH, D_MODEL, D_FF), D_MODEL ** -0.5),
        'ffn_w_up': nrm(k[29], (DEPTH, D_MODEL, D_FF), D_MODEL ** -0.5),
        'ffn_w_down': nrm(k[30], (DEPTH, D_FF, D_MODEL), D_FF ** -0.5),
        'norm_final_w': 1.0 + nrm(k[31], (D_MODEL,), 0.02),
    }


def reference(x_prompt, x_sample, state_rwkv_shift, state_rwkv_wkv, state_ssm_conv, state_ssm,
              meta_tokens, norm_mix_w, w_in, rwkv_mu, rwkv_w0, rwkv_w_up, rwkv_a0, rwkv_a_up, rwkv_g_up,
              rwkv_k_k, rwkv_k_a, rwkv_r_k, rwkv_lnx_w, rwkv_lnx_b,
              ssm_conv_w, ssm_conv_b, ssm_dt_bias, ssm_A_log, ssm_D, ssm_norm_w,
              w_out, norm_ffn_w, ffn_w_gate, ffn_w_up, ffn_w_down, norm_final_w):
    layer_weights = (norm_mix_w, w_in, rwkv_mu, rwkv_w0, rwkv_w_up, rwkv_a0, rwkv_a_up, rwkv_g_up,
                     rwkv_k_k, rwkv_k_a, rwkv_r_k, rwkv_lnx_w, rwkv_lnx_b,
                     ssm_conv_w, ssm_conv_b, ssm_dt_bias, ssm_A_log, ssm_D, ssm_norm_w,
                     w_out, norm_ffn_w, ffn_w_gate, ffn_w_up, ffn_w_down)
    Bp, Lp = x_prompt.shape[:2]
    dtype = x_prompt.dtype
    meta = jnp.broadcast_to(meta_tokens.astype(dtype)[None], (Bp, N_META, D_MODEL))
    x_p = jnp.concatenate([meta, x_prompt], axis=1)
    zeros = lambda *shape: jnp.zeros(shape, dtype)
    prompt_segments = ((N_META, N_META), (Lp, min(CHUNK, Lp)))
    y_p, p_shift, p_wkv, p_conv, p_ssm = run_trunk(
        x_p, zeros(DEPTH, Bp, A_COLS), zeros(DEPTH, Bp, A_HEADS, A_HEAD_DIM, A_HEAD_DIM),
        zeros(DEPTH, Bp, CONV_W - 1, B_CONV_DIM), zeros(DEPTH, Bp, B_HEADS, B_HEAD_DIM, D_STATE),
        prompt_segments, layer_weights, norm_final_w)
    y_prompt = y_p[:, N_META:]
    Ls = x_sample.shape[1]
    y_sample, s_shift, s_wkv, s_conv, s_ssm = run_trunk(
        x_sample, state_rwkv_shift, state_rwkv_wkv, state_ssm_conv, state_ssm,
        ((Ls, min(CHUNK, Ls)),), layer_weights, norm_final_w)
    return (y_prompt, y_sample, p_shift, p_wkv, p_conv, p_ssm, s_shift, s_wkv, s_conv, s_ssm)
```

```python
import contextlib
import math
import os
import numpy as np
import concourse.bass as bass
import concourse.mybir as mybir
from concourse.bass_utils import run_bass_kernel_spmd

F32 = mybir.dt.float32
BF16 = mybir.dt.bfloat16
AF = mybir.ActivationFunctionType
ALU = mybir.AluOpType
AX = mybir.AxisListType

import os
KDBG = os.environ.get('KDBG', '')
EPS = 1e-6
LNX_EPS = 64e-5
N_META = 16


class Buf:
    __slots__ = ("w", "r", "name")

    def __init__(self, name=""):
        self.w = None
        self.r = {}
        self.name = name


class Prog:
    ENGS = ("pe", "act", "dve", "pool", "sp")

    def __init__(self, nc, n_dma_sems=24):
        self.nc = nc
        self.stack = contextlib.ExitStack()
        self.sems = {}
        self.EPOCH = 24000
        self.nep = {"pe": 8, "act": 3, "dve": 3, "pool": 2}
        for e in ("pe", "act", "dve", "pool"):
            for ep in range(self.nep[e]):
                self.sems[(e, ep)] = self.stack.enter_context(nc.semaphore("s_%s%d" % (e, ep)))
        self.n_dma = n_dma_sems
        for i in range(n_dma_sems):
            self.sems[("d", i)] = self.stack.enter_context(nc.semaphore("s_d%d" % i))
        self.dma_cnt = [0] * n_dma_sems
        self.dma_rr = 0
        self.sw_rr = 0
        self.count = {e: 0 for e in self.ENGS}
        self.waited = {e: {} for e in self.ENGS}
        self.prog = {e: [] for e in self.ENGS}
        self.ninstr = 0

    def _need(self, eng, deps):
        for (k, v) in deps:
            if k == eng:
                if eng == "pe":
                    continue
                if eng != "pool" and self.count[eng] - v >= 2:
                    continue
            if self.waited[eng].get(k, 0) >= v:
                continue
            self.waited[eng][k] = v
            if isinstance(k, str):
                ep = (v - 1) // self.EPOCH
                sem = self.sems[(k, ep)]
                v = v - ep * self.EPOCH
            else:
                sem = self.sems[k]
            self.prog[eng].append(lambda e, sem=sem, v=v: e.wait_ge(sem, v))

    @staticmethod
    def _deps(reads, writes):
        deps = set()
        for b in reads:
            if b.w is not None:
                deps.add(b.w)
        for b in writes:
            if b.w is not None:
                deps.add(b.w)
            for k, v in b.r.items():
                deps.add((k, v))
        return deps

    @staticmethod
    def _commit(token, reads, writes):
        k, v = token
        for b in reads:
            if b.r.get(k, 0) < v:
                b.r[k] = v
        for b in writes:
            b.w = token
            b.r = {}

    def op(self, eng, fn, reads=(), writes=()):
        self._need(eng, self._deps(reads, writes))
        self.count[eng] += 1
        self.ninstr += 1
        token = (eng, self.count[eng])
        sem = self.sems[(eng, (self.count[eng] - 1) // self.EPOCH)]
        self.prog[eng].append(lambda e, fn=fn, sem=sem: fn(e).then_inc(sem, 1))
        self._commit(token, reads, writes)
        return token

    def dma(self, eng, out, in_, reads=(), writes=()):
        deps = self._deps(reads, writes)
        nsw = 6
        if eng == "pool":
            i = self.sw_rr
            self.sw_rr = (self.sw_rr + 1) % nsw
        else:
            i = nsw + self.dma_rr
            self.dma_rr = (self.dma_rr + 1) % (self.n_dma - nsw)
        k = ("d", i)
        if self.dma_cnt[i] > 0:
            deps.add((k, 16 * self.dma_cnt[i]))
        self._need(eng, deps)
        self.dma_cnt[i] += 1
        self.ninstr += 1
        token = (k, 16 * self.dma_cnt[i])
        sem = self.sems[k]
        self.prog[eng].append(lambda e, out=out, in_=in_, sem=sem: e.dma_start(out=out, in_=in_).then_inc(sem, 16))
        self._commit(token, reads, writes)
        return token

    def barrier(self):
        engs = ("pe", "act", "dve", "pool")
        for e in engs:
            self._need(e, {(k, self.count[k]) for k in engs if k != e and self.count[k] > 0})

    def finish(self):
        deps = set()
        for i in range(self.n_dma):
            if self.dma_cnt[i] > 0:
                deps.add((("d", i), 16 * self.dma_cnt[i]))
        kw = os.environ.get("KWAIT", "pe,act,dve,pool").split(",")
        for k in ("pe", "act", "dve", "pool"):
            if k in kw and self.count[k] > 0:
                deps.add((k, self.count[k]))
        self._need("sp", deps)
        nc = self.nc
        prog = self.prog
        with nc.Block() as block:
            @block.tensor
            def _(e):
                for f in prog["pe"]:
                    f(e)

            @block.scalar
            def _(e):
                for f in prog["act"]:
                    f(e)

            @block.vector
            def _(e):
                for f in prog["dve"]:
                    f(e)

            @block.gpsimd
            def _(e):
                for f in prog["pool"]:
                    f(e)

            @block.sync
            def _(e):
                for f in prog["sp"]:
                    f(e)
        self.stack.close()


class Cfg:
    def __init__(self, D=2048, SEQ=2048, DFF=5632, G=3, NCORES=8, BATCH=16, DEC_BATCH=16):
        self.D, self.SEQ, self.DFF, self.G, self.NCORES = D, SEQ, DFF, G, NCORES
        self.BATCH, self.DEC_BATCH = BATCH, DEC_BATCH
        self.AW = D // 2
        self.NA = self.AW // 64
        self.NHP = self.NA // 2
        self.BW = D - self.AW
        self.NB = self.BW // 64
        self.HG = self.NB // 2
        self.NBC = self.BW // 128
        self.CONVD = self.BW + 512
        self.ACOLS = 3 * self.AW + 288
        self.BCOLS = self.BW + self.CONVD + self.NB
        self.INCOLS = self.ACOLS + self.BCOLS
        self.KC = D // 128
        self.NF = DFF // 128
        self.NW = min(512, D)
        self.NNG = D // self.NW
        self.KPG_OUT = min(8, self.KC)
        self.KPG_DN = 4 if self.NF % 4 == 0 else 2
        assert self.KC % self.KPG_OUT == 0 and self.NF % self.KPG_DN == 0
        assert (SEQ + 64) % 64 == 0
        self.NSTEP_P = (SEQ + 64) // 64
        assert self.NSTEP_P % G == 0
        nh = self.AW // 128
        A = []
        for part in range(3):
            for c in range(nh):
                A.append((part * self.AW + c * 128, 128))
        A.append((3 * self.AW, 128))
        A.append((3 * self.AW + 128, 128))
        A.append((3 * self.AW + 256, 32))
        self.ACH = A
        self.NCHA = len(A)
        self.cR, self.cK, self.cV = 0, nh, 2 * nh
        self.cWA, self.cG0, self.cG1 = 3 * nh, 3 * nh + 1, 3 * nh + 2
        B = []
        o = self.ACOLS
        for c in range(self.NBC):
            B.append((o + c * 128, 128))
        o += self.BW
        for c in range(self.NBC + 4):
            B.append((o + c * 128, 128))
        o += self.CONVD
        B.append((o, self.NB))
        self.BCH = B
        self.NCHB = len(B)
        self.cZ, self.cX = 0, self.NBC
        self.cBm, self.cCm, self.cDT = 2 * self.NBC, 2 * self.NBC + 2, 2 * self.NBC + 4
        self.NXBC = self.NBC + 4
        self.NCHIN = self.NCHA + self.NCHB


FULL = Cfg(G=1)

def pv_layout(c):
    off = {}
    n = 0
    def add(name, w):
        nonlocal n
        off[name] = (n, w)
        n += w
    add("mu", c.NCHA)
    for nm in ("w0", "a0", "kk", "ka", "rk", "lnw", "lnb"):
        add(nm, c.AW // 128)
    add("convw", c.NXBC * 4)
    add("convb", c.NXBC)
    add("snw", c.NBC)
    add("dskip", c.NBC)
    add("dtb", 1)
    add("nmix", c.KC)
    add("nffn", c.KC)
    return off, n


def build_program(c):
    nc = bass.Bass("TRN2", target_bir_lowering=False)
    D, G, KC, NHP, NA, NB, NBC, NF, NW = c.D, c.G, c.KC, c.NHP, c.NA, c.NB, c.NBC, c.NF, c.NW
    T = 128 * G
    NCHA, NCHB, NXBC = c.NCHA, c.NCHB, c.NXBC
    pvo, NPV = pv_layout(c)

    def din(name, shape, dt=F32):
        return nc.dram_tensor(name, list(shape), dt, kind="ExternalInput").ap()

    def dout(name, shape, dt=F32):
        return nc.dram_tensor(name, list(shape), dt, kind="ExternalOutput").ap()

    def dscr(name, shape, dt=BF16):
        return nc.dram_tensor(name, list(shape), dt, kind="Internal").ap()

    xp = din("xp", [2, c.SEQ, D])
    xs = din("xs", [2, 64, D])
    metac = din("metac", [64, D])
    st_shift = din("st_shift", [128, NCHA, 2])
    st_wkv = din("st_wkv", [128, 2, NHP, 64])
    st_conv = din("st_conv", [128, NXBC, 2, 3])
    st_ssm = din("st_ssm", [128, 2, NB * 64])
    pvec = din("pvec", [128, NPV])
    alog = din("alog", [1, NB])
    nfw = din("nfw", [1, D])
    lowr = din("lowr", [128, 4, c.AW])
    win_h = din("win_h", [c.NCHIN, 128, KC, 128])
    wout_h = din("wout_h", [c.NNG, KC // c.KPG_OUT, 128, c.KPG_OUT, NW])
    wgu_h = din("wgu_h", [NF, 128, 2, KC, 128])
    wdn_h = din("wdn_h", [c.NNG, NF // c.KPG_DN, 128, c.KPG_DN, NW])

    yp = dout("yp", [2, c.SEQ, D])
    ys = dout("ys", [2, 64, D])
    o_shift = [dout("p_shift", [128, NCHA, 2]), dout("s_shift", [128, NCHA, 2])]
    o_wkv = [dout("p_wkv", [128, 2, NHP, 64]), dout("s_wkv", [128, 2, NHP, 64])]
    o_conv = [dout("p_conv", [128, NXBC, 2, 3]), dout("s_conv", [128, NXBC, 2, 3])]
    o_ssm = [dout("p_ssm", [128, 2, NB * 64]), dout("s_ssm", [128, 2, NB * 64])]

    NTILES = c.NSTEP_P // G + 1
    x1s = dscr("x1s", [NTILES, G, 128, D], F32)
    x1b = [Buf("x1s%d" % i) for i in range(NTILES)]
    win_s = dscr("win_s", [c.NCHIN, 128, KC, 128])
    wout_s = dscr("wout_s", [c.NNG, KC // c.KPG_OUT, 128, c.KPG_OUT, NW])
    wgu_s = dscr("wgu_s", [NF, 128, 2, KC, 128])
    wdn_s = dscr("wdn_s", [c.NNG, NF // c.KPG_DN, 128, c.KPG_DN, NW])

    P = Prog(nc)
    S = P.stack

    def sb(name, shape, dt=F32):
        return S.enter_context(nc.sbuf_tensor(name, list(shape), dt))

    NH8 = c.AW // 128
    NBH = 2 if NH8 >= 2 else 1
    HB = NA // NBH
    NHH = NH8 // NBH
    NPAR = 2 if getattr(c, "pipe", True) else 1
    u_off = [0]

    class _USpec:
        pass

    def cu(shape, dt=F32):
        n = 1
        for x in shape[1:]:
            n *= x
        sp = _USpec()
        sp.off, sp.nf, sp.shape, sp.dt, sp.n = u_off[0], ((n + 1) // 2 if dt == BF16 else n), list(shape), dt, n
        u_off[0] += sp.nf
        return sp
    xt_all = [cu([128, G, D]) for i in range(NPAR)]
    hT_sp = cu([128, KC, T], BF16)
    ZCH = max(NCHA, NCHB)
    zS_sp = cu([128, NCHB * T])
    zR_sp = cu([128, NCHA * T])
    yT_all = [cu([128, KC, T], BF16) for i in range(NPAR)]
    WSLOT = max(2 * KC * 128, c.KPG_OUT * NW, c.KPG_DN * NW)
    NWS = getattr(c, "nws", 4)
    wslot = [sb("wslot%d" % i, [128, WSLOT], BF16) for i in range(NWS)]
    wslot_b = [Buf() for _ in range(NWS)]
    wrr = [0]
    nfw_bc = sb("nfw_bc", [128, D])
    pv = sb("pv", [128, NPV])
    lowr_bf = sb("lowr_bf", [128, 4, c.AW], BF16)
    ident_f = sb("ident_f", [128, 128])
    ident_b = sb("ident_b", [128, 128], BF16)
    m_iu = sb("m_iu", [128, 128])
    m_sl = sb("m_sl", [128, 128])
    m_xr = sb("m_xr", [128, 256])
    imx = sb("imx", [128, 128], BF16)
    blk = sb("blk", [128, 128], BF16)
    ones_b = sb("ones_b", [128, 128], BF16)
    ones_f = sb("ones_f", [128, 128])
    rmask = sb("rmask", [128, 128])
    a_bc = sb("a_bc", [128, NB])
    onemka = sb("onemka", [128, NH8])
    epsc = sb("epsc", [128, 4])
    pmask = sb("pmask", [128, 2])
    seqsel = sb("seqsel", [128, 2, 128])
    carry = sb("carry", [128, NCHA, 2])
    Hst = sb("Hst", [128, 2, NHP, 64])
    Hbf = sb("Hbf", [128, 2, NHP, 64], BF16)
    Sst = sb("Sst", [128, 2, NB * 64])
    Sbf = sb("Sbf", [128, 2, NB * 64], BF16)
    convst = sb("convst", [128, NXBC, 2, 3])
    wa_bf = sb("wa_bf", [128, T], BF16)
    sg_bf = sb("sg_bf", [128, 2, T], BF16)
    dt_f = sb("dt_f", [128, T])
    big1_sp = cu([128, max(NH8 * T, D // 2, T)])
    big2_sp = cu([128, D // 2])
    arena_elems = [0, 0]

    class _Spec:
        pass
    specs = []

    def cv(which, name, shape, dt=F32):
        n = 1
        for x in shape[1:]:
            n *= x
        nf = (n + 1) // 2 if dt == BF16 else n
        sp = _Spec()
        sp.which, sp.off, sp.nf, sp.shape, sp.dt, sp.n = which, arena_elems[which], nf, list(shape), dt, n
        arena_elems[which] += nf
        specs.append(sp)
        return sp

    f_lw = cv(0, 'f_lw', [128, NHH, 128])
    f_cs = cv(0, 'f_cs', [128, NHH, 128])
    f_a = cv(0, 'f_a', [128, NHH, 128])
    f_ep = cv(0, 'f_ep', [128, NHH, 128])
    f_en = cv(0, 'f_en', [128, NHH, 128])
    f_t1 = cv(0, 'f_t1', [128, NHH, 128])
    f_g = cv(0, 'f_g', [128, NHH, 128])
    f_gw = cv(0, 'f_gw', [128, NHH, 128])
    f_gb = cv(0, 'f_gb', [128, NHH, 128])
    f_bq = cv(0, 'f_bq', [128, NHH, 128], BF16)
    ar_b = cv(0, 'ar_b', [128, NHH, 2, 128], BF16)
    bt_b = cv(0, 'bt_b', [128, NHH, 128], BF16)
    kt_b = cv(0, 'kt_b', [128, NHH, 128], BF16)
    v_b = cv(0, 'v_b', [128, NHH, 128], BF16)
    arm = cv(0, 'arm', [128, NHH, 4 * 128], BF16)
    btm = cv(0, 'btm', [128, NHH, 2 * 128], BF16)
    ktm = cv(0, 'ktm', [128, NHH, 2 * 128], BF16)
    Vts = cv(0, 'Vts', [128, 2, NHH * 128], BF16)
    Uss = cv(0, 'Uss', [128, 2, HB * 64], BF16)
    Vtok = cv(0, 'Vtok', [128, NHH * 128], BF16)
    Ktok = cv(0, 'Ktok', [128, NHH * 128], BF16)
    Btok = cv(0, 'Btok', [128, NHH * 128], BF16)
    XRB = cv(0, 'XRB', [128, HB, 2, 128], BF16)
    AKR = cv(0, 'AKR', [128, HB, 2, 128], BF16)
    Xp = [cv(0, 'Xp' + str(i), [128, HB, 128], BF16) for i in range(2)]
    Lp = [cv(0, 'Lp' + str(i), [128, HB, 128], BF16) for i in range(2)]
    Pp = [cv(0, 'Pp' + str(i), [128, HB, 128], BF16) for i in range(2)]
    Wsb = cv(0, 'Wsb', [128, HB, 64], BF16)
    Usb = cv(0, 'Usb', [128, HB, 64], BF16)
    Ysb = cv(0, 'Ysb', [128, HB, 64])
    Ysq = cv(0, 'Ysq', [128, HB, 64])
    gst = cv(0, 'gst', [128, 4, HB])
    Htmp = cv(0, 'Htmp', [128, NHH, 64])
    szl = cv(1, 'szl', [128, NBC, T], BF16)
    xpad = cv(1, 'xpad', [128, NXBC, 2 * (3 + 64 * G)])
    dt_tok = cv(1, 'dt_tok', [128, NB])
    dta = cv(1, 'dta', [128, NB])
    dtx = cv(1, 'dtx', [128, NB, 64])
    rhs1 = cv(1, 'rhs1', [128, max(NB * 128, NXBC * 128 * G)])
    decT = cv(1, 'decT', [128, max(NB * 128, NXBC * 128 * G)])
    MT = cv(1, 'MT', [128, NB, 128], BF16)
    cbm = cv(1, 'cbm', [128, 2, 128])
    xs_tok = cv(1, 'xs_tok', [128, NB * 64])
    xdt = cv(1, 'xdt', [128, NB * 64], BF16)
    xdt2 = cv(1, 'xdt2', [128, 2, NB * 64], BF16)
    Btk = cv(1, 'Btk', [128, 2, 128], BF16)
    bc_b = cv(1, 'bc_b', [128, 4, 128], BF16)
    eaB = cv(1, 'eaB', [128, NBC, 128])
    yb = cv(1, 'yb', [128, NBC, 128])
    ytmp = cv(1, 'ytmp', [128, NBC, 128])
    ysq = cv(1, 'ysq', [128, NBC, 128], BF16)
    rs2 = cv(1, 'rs2', [128, 2, 128])
    eaL = cv(1, 'eaL', [128, NB])
    small = cv(1, 'small', [128, 8])
    arena_sp = cu([128, max(arena_elems)])
    GF = getattr(c, "gf", 4)
    TF = 128 * GF
    p1_elems = u_off[0]
    u_off[0] = 0
    xtF = [cu([128, GF, D]) for i in range(2)]
    hTF = cu([128, KC, TF], BF16)
    actTF = cu([128, NF, TF], BF16)
    svF = cu([128, TF])
    xnF = cu([128, D], BF16)
    p2_elems = u_off[0]
    U = sb("U", [128, max(p1_elems, p2_elems)])

    def mkU(sp):
        v = U[:, sp.off:sp.off + sp.nf]
        if sp.dt == BF16:
            v = v.bitcast(BF16)[:, 0:sp.n]
        if len(sp.shape) == 3:
            v = v.rearrange("p (a b) -> p a b", b=sp.shape[2])
        return v
    xt_all = [mkU(x) for x in xt_all]
    hT = mkU(hT_sp)
    zS_raw, zR_raw = mkU(zS_sp), mkU(zR_sp)
    zS = zS_raw.rearrange("p (c t) -> p c t", t=T)
    zR = zR_raw.rearrange("p (c t) -> p c t", t=T)
    yT_all = [mkU(x) for x in yT_all]
    big1, big2 = mkU(big1_sp), mkU(big2_sp)
    arena = mkU(arena_sp)
    xtF = [mkU(x) for x in xtF]
    hTF, actTF, svF, xnF = mkU(hTF), mkU(actTF), mkU(svF), mkU(xnF)
    sqF = xnF

    def _mk(sp):
        v = arena[:, sp.off:sp.off + sp.nf]
        if sp.dt == BF16:
            v = v.bitcast(BF16)[:, 0:sp.n]
        if len(sp.shape) == 3:
            v = v.rearrange("p (a b) -> p a b", b=sp.shape[2])
        elif len(sp.shape) == 4:
            v = v.rearrange("p (a b c) -> p a b c", b=sp.shape[2], c=sp.shape[3])
        return v

    f_lw = _mk(f_lw)
    f_cs = _mk(f_cs)
    f_a = _mk(f_a)
    f_ep = _mk(f_ep)
    f_en = _mk(f_en)
    f_t1 = _mk(f_t1)
    f_g = _mk(f_g)
    f_gw = _mk(f_gw)
    f_gb = _mk(f_gb)
    f_bq = _mk(f_bq)
    ar_b = _mk(ar_b)
    bt_b = _mk(bt_b)
    kt_b = _mk(kt_b)
    v_b = _mk(v_b)
    arm = _mk(arm).rearrange('p c (h a t) -> p c h a t', h=2, a=2)
    btm = _mk(btm).rearrange('p c (h t) -> p c h t', h=2)
    ktm = _mk(ktm).rearrange('p c (h t) -> p c h t', h=2)
    Vts = _mk(Vts)
    Uss = _mk(Uss)
    Vtok = _mk(Vtok)
    Ktok = _mk(Ktok)
    Btok = _mk(Btok)
    XRB = _mk(XRB)
    AKR = _mk(AKR)
    Xp = [_mk(x) for x in Xp]
    Lp = [_mk(x) for x in Lp]
    Pp = [_mk(x) for x in Pp]
    Wsb = _mk(Wsb)
    Usb = _mk(Usb)
    Ysb = _mk(Ysb)
    Ysq = _mk(Ysq)
    gst = _mk(gst)
    Htmp = _mk(Htmp)
    szl = _mk(szl)
    xpad = _mk(xpad).rearrange('p c (s l) -> p c s l', s=2)
    dt_tok = _mk(dt_tok)
    dta = _mk(dta)
    dtx = _mk(dtx)
    caccA = _mk(rhs1)
    ctmpA = _mk(decT)
    rhs1 = caccA[:, 0:NB * 128].rearrange('p (h q) -> p h q', q=128)
    decT = ctmpA[:, 0:NB * 128].rearrange('p (h q) -> p h q', q=128)
    MT = _mk(MT)
    cbm = _mk(cbm)
    xs_tok = _mk(xs_tok)
    xdt = _mk(xdt)
    xdt2 = _mk(xdt2)
    Btk = _mk(Btk)
    bc_b = _mk(bc_b)
    eaB = _mk(eaB)
    yb = _mk(yb)
    ytmp = _mk(ytmp)
    ysq = _mk(ysq)
    rs2 = _mk(rs2)
    eaL = _mk(eaL)
    small = _mk(small)
    f_t3 = f_lw
    f_t2 = f_cs
    small_all = [sb('small%d' % i, [128, 8]) for i in range(NPAR)]

    NPS = 8
    ps = [S.enter_context(nc.psum_tensor("ps%d" % i, [128, 512], F32)) for i in range(NPS)]
    ps_b = [Buf() for _ in range(NPS)]
    prr = [0]

    def bank():
        i = prr[0]
        prr[0] = (i + 1) % NPS
        return ps[i], ps_b[i]

    B = {}

    def tk(name):
        if name not in B:
            B[name] = Buf(name)
        return B[name]

    extra_reads = []
    AR = Buf("arena_phase")

    def dve(fn, r=(), w=()):
        return P.op("dve", fn, list(r) + extra_reads, w)

    def act(fn, r=(), w=()):
        return P.op("act", fn, list(r) + extra_reads, w)

    def pool(fn, r=(), w=()):
        return P.op("pool", fn, list(r) + extra_reads, w)

    def pe(fn, r=(), w=()):
        return P.op("pe", fn, list(r) + extra_reads, w)

    def fence():
        P.op("dve", lambda e: e.memset(epsc[:, 3:4], 0.0), [], [AR])

    def pvc(name, i=0, n=1):
        o, w = pvo[name]
        return pv[:, o + i:o + i + n]

    def bc3(ap2, n):
        return ap2.unsqueeze(2).to_broadcast([128, ap2.shape[1], n])

    def TT(eng, out, in0, in1, op, r, w):
        return eng(lambda e: e.tensor_tensor(out=out, in0=in0, in1=in1, op=op), r, w)

    cst = tk("const")
    pvb = tk("pv")
    EPS_I, LNX_I, ONE_I = 0, 1, 2

    pending = []

    def conv_w(dst, src, nsplit, name, defer=False):
        n0 = dst.shape[0]
        per = max(1, (n0 + nsplit - 1) // nsplit)
        toks = []
        for i in range(0, n0, per):
            b = Buf(name)
            job = (dst[i:min(n0, i + per)], src[i:min(n0, i + per)], b)
            if defer:
                pending.append(job)
            else:
                P.dma("pool", job[0], job[1], writes=[b])
            toks.append((i, min(n0, i + per), b))
        return toks

    def issue_pending(n):
        for _ in range(n):
            if pending:
                d_, s_, b_ = pending.pop(0)
                P.dma("pool", d_, s_, writes=[b_])

    def find(toks, i):
        for a, b, t in toks:
            if a <= i < b:
                return t
        raise KeyError

    P.dma("sp", pv[:], pvec, writes=[pvb])
    lowr_s = dscr("lowr_s", [128, 4, c.AW])
    P.dma("pool", lowr_s, lowr, writes=[tk("lowr_s")])
    P.dma("sp", lowr_bf[:], lowr_s, reads=[tk("lowr_s")], writes=[tk("lowr")])
    P.dma("sp", nfw_bc[:], nfw.partition_broadcast(128), writes=[tk("nfw")])
    P.dma("sp", a_bc[:], alog.partition_broadcast(128), writes=[tk("abc")])
    pool(lambda e: e.memset(ident_f[:], 1.0), w=[cst])
    pool(lambda e: e.affine_select(out=ident_f[:], in_=ident_f[:], pattern=[[-1, 128]], compare_op=ALU.is_equal,
                                   fill=0.0, base=0, channel_multiplier=1), r=[cst], w=[cst])
    pool(lambda e: e.tensor_copy(out=ident_b[:], in_=ident_f[:]), r=[cst], w=[cst])
    pool(lambda e: e.tensor_copy(out=imx[:], in_=ident_f[:]), r=[cst], w=[cst])
    pool(lambda e: e.memset(ones_f[:], 1.0), w=[cst])
    pool(lambda e: e.memset(ones_b[:], 1.0), w=[cst])
    pool(lambda e: e.memset(blk[:], 0.0), w=[cst])
    pool(lambda e: e.memset(blk[0:64, 0:64], 1.0), w=[cst])
    pool(lambda e: e.memset(blk[64:128, 64:128], 1.0), w=[cst])

    def blockmask(m, base_mult, pat, op):
        pool(lambda e: e.memset(m, 0.0), w=[cst])
        pool(lambda e: e.memset(m[0:64, 0:64], 1.0), w=[cst])
        pool(lambda e: e.memset(m[64:128, 64:128], 1.0), w=[cst])
        pool(lambda e: e.affine_select(out=m, in_=m, pattern=[[pat, 128]], compare_op=op, fill=0.0, base=0,
                                       channel_multiplier=base_mult), r=[cst], w=[cst])
    blockmask(m_xr[:, 0:128], -1, 1, ALU.is_gt)
    blockmask(m_xr[:, 128:256], -1, 1, ALU.is_ge)
    blockmask(m_iu[:], -1, 1, ALU.is_ge)
    blockmask(m_sl[:], 1, -1, ALU.is_gt)
    pool(lambda e: e.memset(pmask[:], 0.0), w=[cst])
    pool(lambda e: e.memset(pmask[0:64, 0:1], 1.0), w=[cst])
    pool(lambda e: e.memset(pmask[64:128, 1:2], 1.0), w=[cst])
    pool(lambda e: e.memset(seqsel[:], 0.0), w=[cst])
    pool(lambda e: e.memset(seqsel[0:64, 0, :], 1.0), w=[cst])
    pool(lambda e: e.memset(seqsel[64:128, 1, :], 1.0), w=[cst])
    pool(lambda e: e.memset(rmask[:], 1.0), w=[cst])
    pool(lambda e: e.memset(rmask[:, 0:1], 0.0), w=[cst])
    pool(lambda e: e.memset(rmask[:, 64:65], 0.0), w=[cst])
    pool(lambda e: e.memset(epsc[:, 0:1], EPS), w=[cst])
    pool(lambda e: e.memset(epsc[:, 1:2], LNX_EPS), w=[cst])
    pool(lambda e: e.memset(epsc[:, 2:3], 1.0), w=[cst])
    pool(lambda e: e.memset(epsc[:, 3:4], 0.0), w=[cst])
    act(lambda e: e.activation(out=a_bc[:], in_=a_bc[:], func=AF.Exp), r=[tk("abc")], w=[tk("abc")])
    dve(lambda e: e.tensor_scalar(out=a_bc[:], in0=a_bc[:], scalar1=-1.0, scalar2=None, op0=ALU.mult),
        r=[tk("abc")], w=[tk("abc")])
    dve(lambda e: e.tensor_scalar(out=onemka[:], in0=pvc("ka", 0, NH8), scalar1=-1.0, scalar2=1.0, op0=ALU.mult,
                                  op1=ALU.add), r=[pvb], w=[cst])

    t_win = conv_w(win_s, win_h, 12, "win")
    fl = "a b p k n -> (a b) p k n"
    t_wout = conv_w(wout_s.rearrange(fl), wout_h.rearrange(fl), 4, "wout")
    t_wgu = conv_w(wgu_s, wgu_h, 11, "wgu", defer=True)
    t_wdn = conv_w(wdn_s.rearrange(fl), wdn_h.rearrange(fl), 8, "wdn", defer=True)


    def load_w(src_ap, n, src_tok):
        i = wrr[0]
        wrr[0] = (i + 1) % NWS
        view = wslot[i][:, 0:n]
        P.dma("sp", view, src_ap, reads=[src_tok], writes=[wslot_b[i]])
        return view, wslot_b[i]

    def rstd(ss_ap, out_ap, scale, eps_i, rb, wb):
        act(lambda e: e.activation(out=out_ap, in_=ss_ap, func=AF.Ln, bias=epsc[0:ss_ap.shape[0], eps_i:eps_i + 1], scale=scale),
            r=list(rb) + [cst], w=wb)
        act(lambda e: e.activation(out=out_ap, in_=out_ap, func=AF.Exp, scale=-0.5), r=wb, w=wb)

    class _Stop(Exception):
        pass
    stop = getattr(c, "stop", 99)

    def chk(k):
        if stop == k:
            raise _Stop()

    def make_phases(par):
        xt, yT = xt_all[par], yT_all[par]
        small = small_all[0]
        xb, hb, ytb = tk("xt%d" % par), tk("hT"), tk("yT%d" % par)
        zsb, zrb = tk("zS"), tk("zR")
        b1, b2 = tk("big1"), tk("big2")
        sfx = ""

        def norm_to_hT(g, normname):
            sqv = big1[:].bitcast(BF16)[:, 0:D]
            act(lambda e: e.activation(out=sqv, in_=xt[:, g, :], func=AF.Square, accum_out=small[:, 0:1]),
                r=[xb], w=[b1, tk("ss" + sfx)])
            rstd(small[:, 0:1], small[:, 1:2], 1.0 / D, EPS_I, [tk("ss" + sfx)], [tk("rstd" + sfx)])
            xnv = big2[:].bitcast(BF16)[:, 0:D]
            act(lambda e: e.activation(out=xnv, in_=xt[:, g, :], func=AF.Copy, scale=small[:, 1:2]),
                r=[xb, tk("rstd" + sfx)], w=[b2])
            for k0 in range(0, KC, 4):
                kn = min(4, KC - k0)
                pt, pb = bank()
                ptb = pt[:].bitcast(BF16)
                for k in range(kn):
                    pe(lambda e, k=k, k0=k0, ptb=ptb: e.transpose(ptb[:, k * 128:(k + 1) * 128],
                                                                  xnv[:, (k0 + k) * 128:(k0 + k + 1) * 128], ident_b[:]),
                       r=[b2, cst], w=[pb])
                TT(dve, hT[:, k0:k0 + kn, g * 128:(g + 1) * 128], ptb[:, 0:kn * 128].rearrange("p (k t) -> p k t", t=128),
                   bc3(pvc(normname, k0, kn), 128), ALU.mult, [pb, pvb], [hb])

        def in_proj(chunks, cc0, Tt, zT, zb):
            for ci, (col0, ncol) in enumerate(chunks):
                wv, wb = load_w(win_s[cc0 + ci].rearrange("p k n -> p (k n)"), KC * 128, find(t_win, cc0 + ci))
                wv3 = wv.rearrange("p (k n) -> p k n", n=128)
                pt, pb = bank()
                for k in range(KC):
                    pe(lambda e, k=k, ncol=ncol, pt=pt, wv3=wv3: e.matmul(pt[0:ncol, 0:Tt], lhsT=wv3[:, k, 0:ncol], rhs=hT[:, k, 0:Tt],
                                                                          start=(k == 0), stop=(k == KC - 1)), r=[wb, hb], w=[pb])
                act(lambda e, ci=ci, ncol=ncol, pt=pt: e.copy(out=zT[0:ncol, ci, 0:Tt], in_=pt[0:ncol, 0:Tt]), r=[pb], w=[zb])
                if ci % 2 == 1:
                    yield

        def dense_tok(nG, w_s, w_tok, nkg, kpg, lhs_of, evac):
            for ng in range(c.NNG):
                banks = [bank() for _ in range(nG)]
                for kg in range(nkg):
                    wv, wb = load_w(w_s[ng, kg].rearrange("p k n -> p (k n)"), kpg * NW, find(w_tok, ng * nkg + kg))
                    wv3 = wv.rearrange("p (k n) -> p k n", n=NW)
                    for g in range(nG):
                        pt, pb = banks[g]
                        for kk in range(kpg):
                            kabs = kg * kpg + kk
                            lh, lhb = lhs_of(kabs, g)
                            pe(lambda e, pt=pt, lh=lh, wv3=wv3, kk=kk, kabs=kabs: e.matmul(
                                pt[:, 0:NW], lhsT=lh, rhs=wv3[:, kk, :], start=(kabs == 0), stop=(kabs == nkg * kpg - 1)),
                               r=[wb, lhb], w=[pb])
                for g in range(nG):
                    evac(g, ng, banks[g][0], banks[g][1])
                yield

        cX, cZ, cBm, cCm, cDT = c.cX, c.cZ, c.cBm, c.cCm, c.cDT
        HG = c.HG

        def ssd_prep(nG, first_prompt):
            Tt = 128 * nG
            L = 64 * nG
            cvb, xpb = tk("convst"), tk("xpad")
            cacc = caccA[:, 0:NXBC * 2 * L].rearrange("p (c s l) -> p c s l", s=2, l=L)
            ctmp = ctmpA[:, 0:NXBC * 2 * L].rearrange("p (c s l) -> p c s l", s=2, l=L)
            dve(lambda e: e.tensor_copy(out=xpad[:, :, :, 0:3], in_=convst[:]), r=[cvb], w=[xpb])
            for s in range(2):
                eng = dve if s == 0 else pool
                eng(lambda e, s=s: e.tensor_copy(
                    out=xpad[:, :, s, 3:3 + L].rearrange("p c (g t) -> p c g t", t=64),
                    in_=zS[:, cX:cX + NXBC, 0:Tt].rearrange("p c (g s t) -> p c g s t", s=2, t=64)[:, :, :, s, :]),
                    r=[zsb], w=[xpb])
            dve(lambda e: e.tensor_copy(out=convst[:], in_=xpad[:, :, :, L:L + 3]), r=[xpb], w=[cvb])
            cwo = pvo["convw"][0]
            cwv = pv[:, cwo:cwo + NXBC * 4].rearrange("p (c k) -> p c k", k=4)

            def cw_b(k):
                return cwv[:, :, k:k + 1].unsqueeze(3).to_broadcast([128, NXBC, 2, L])
            TT(dve, cacc, xpad[:, :, :, 0:L], cw_b(0), ALU.mult, [xpb, pvb], [tk("rhs1")])
            for k in range(1, 4):
                TT(pool, ctmp, xpad[:, :, :, k:k + L], cw_b(k), ALU.mult, [xpb, pvb], [tk("decT")])
                TT(dve, cacc, cacc, ctmp, ALU.add, [tk("rhs1"), tk("decT")], [tk("rhs1")])
            for ci in range(NXBC):
                act(lambda e, ci=ci: e.activation(
                    out=zS[:, cX + ci, 0:Tt].rearrange("p (g s t) -> p s g t", s=2, t=64),
                    in_=cacc[:, ci, :, :].rearrange("p s (g t) -> p s g t", t=64),
                    func=AF.Silu, bias=pvc("convb", ci, 1)), r=[tk("rhs1"), pvb], w=[zsb])
            for ci in range(NBC):
                act(lambda e, ci=ci: e.activation(out=szl[:, ci, 0:Tt], in_=zS[:, cZ + ci, 0:Tt], func=AF.Silu), r=[zsb], w=[tk("szl")])
            dtb = tk("dt_f")
            act(lambda e: e.activation(out=dt_f[0:NB, 0:Tt], in_=zS[0:NB, cDT, 0:Tt], func=AF.Exp, bias=pvc("dtb")[0:NB, :]),
                r=[zsb, pvb], w=[dtb])
            act(lambda e: e.activation(out=dt_f[0:NB, 0:Tt], in_=dt_f[0:NB, 0:Tt], func=AF.Ln, bias=epsc[0:NB, ONE_I:ONE_I + 1]),
                r=[dtb, cst], w=[dtb])
            if first_prompt:
                dve(lambda e: e.memset(dt_f[0:NB, 0:128].rearrange("p (s t) -> p s t", t=64)[:, :, 0:48], 0.0), r=[dtb], w=[dtb])

        def ssd_step(g):
            tsl = slice(g * 128, (g + 1) * 128)
            dtb, szb = tk("dt_f"), tk("szl")
            sstb, sbfb = tk("Sst"), tk("Sbf")
            pt, pb = bank()
            pe(lambda e, pt=pt: e.transpose(pt[:, 0:NB], dt_f[0:NB, tsl], ident_f[0:NB, 0:NB]), r=[dtb, cst], w=[pb])
            dtk, dab = tk("dt_tok"), tk("dta")
            dve(lambda e, pt=pt: e.tensor_copy(out=dt_tok[:], in_=pt[:, 0:NB]), r=[pb], w=[dtk])
            TT(dve, dta[:], dt_tok[:], a_bc[:], ALU.mult, [dtk, tk("abc")], [dab])
            xtk = tk("xs_tok")
            for c0 in range(0, NBC, 4):
                cn = min(4, NBC - c0)
                pt, pb = bank()
                for k in range(cn):
                    pe(lambda e, k=k, c0=c0, pt=pt: e.transpose(pt[:, k * 128:(k + 1) * 128], zS[:, cX + c0 + k, tsl], ident_f[:]),
                       r=[zsb, cst], w=[pb])
                act(lambda e, c0=c0, cn=cn, pt=pt: e.copy(out=xs_tok[:, c0 * 128:(c0 + cn) * 128], in_=pt[:, 0:cn * 128]), r=[pb], w=[xtk])
            bcb = tk("bc_b")
            act(lambda e: e.copy(out=bc_b[:], in_=zS[:, cBm:cBm + 4, tsl]), r=[zsb], w=[bcb])
            pt, pb = bank()
            ptb = pt[:].bitcast(BF16)
            for gq in range(2):
                pe(lambda e, gq=gq, ptb=ptb: e.transpose(ptb[:, gq * 128:(gq + 1) * 128], bc_b[:, gq, :], ident_b[:]), r=[bcb, cst], w=[pb])
            btb = tk("Btk")
            dve(lambda e, ptb=ptb: e.tensor_copy(out=Btk[:], in_=ptb[:, 0:256].rearrange("p (a n) -> p a n", n=128)), r=[pb], w=[btb])
            xdb = tk("xdt")
            TT(dve, xdt[:].rearrange("p (h q) -> p h q", q=64), xs_tok[:].rearrange("p (h q) -> p h q", q=64),
               bc3(dt_tok[:], 64), ALU.mult, [xtk, dtk], [xdb])
            yield
            r1b, dcb = tk("rhs1"), tk("decT")
            TT(pool, rhs1[:], bc3(dta[:], 128), m_iu[:].unsqueeze(1).to_broadcast([128, NB, 128]), ALU.mult, [dab, cst], [r1b])
            for h0 in range(0, NB, 4):
                hn = min(4, NB - h0)
                pt, pb = bank()
                pe(lambda e, h0=h0, hn=hn, pt=pt: e.matmul(pt[:, 0:hn * 128], lhsT=m_sl[:],
                                                           rhs=rhs1[:, h0:h0 + hn, :].rearrange("p h q -> p (h q)"),
                                                           start=True, stop=True), r=[cst, r1b], w=[pb])
                act(lambda e, h0=h0, hn=hn, pt=pt: e.activation(out=decT[:, h0:h0 + hn, :].rearrange("p h q -> p (h q)"),
                                                                in_=pt[:, 0:hn * 128], func=AF.Exp), r=[pb], w=[dcb])
            yield
            pt, pb = bank()
            for gq in range(2):
                pe(lambda e, gq=gq, pt=pt: e.matmul(pt[:, gq * 128:(gq + 1) * 128], lhsT=bc_b[:, gq, :], rhs=bc_b[:, 2 + gq, :],
                                                    start=True, stop=True), r=[bcb], w=[pb])
            cbb, mtb = tk("cbm"), tk("MT")
            TT(dve, cbm[:], pt[:, 0:256].rearrange("p (a q) -> p a q", q=128), m_iu[:].unsqueeze(1).to_broadcast([128, 2, 128]),
               ALU.mult, [pb, cst], [cbb])
            for gq in range(2):
                TT(dve if gq == 0 else pool, MT[:, gq * HG:(gq + 1) * HG, :], decT[:, gq * HG:(gq + 1) * HG, :],
                   cbm[:, gq:gq + 1, :].to_broadcast([128, HG, 128]), ALU.mult, [dcb, cbb], [mtb])
            x2b = tk("xdt2")
            pool(lambda e: e.memset(xdt2[:], 0.0), w=[x2b])
            for s in range(2):
                TT(dve, xdt2[s * 64:(s + 1) * 64, s, :].rearrange("p (h q) -> p h q", q=64),
                   xdt[s * 64:(s + 1) * 64, :].rearrange("p (h q) -> p h q", q=64),
                   decT[s * 64:(s + 1) * 64, :, s * 64 + 63:s * 64 + 64].to_broadcast([64, NB, 64]), ALU.mult, [xdb, dcb], [x2b])
            dxb = tk("dtx")
            act(lambda e: e.copy(out=dtx[:], in_=bc3(dta[:], 64)), r=[dab], w=[dxb])
            eBb, ybb = tk("eaB"), tk("yb")
            for c0 in range(0, NBC, 4):
                cn = min(4, NBC - c0)
                pt, pb = bank()
                pa, pab = bank()
                pq, pqb = bank()
                for k in range(cn):
                    cc = c0 + k
                    grp = (2 * cc) // HG
                    for hh in range(2):
                        h = 2 * cc + hh
                        pe(lambda e, k=k, h=h, hh=hh, pt=pt: e.matmul(pt[hh * 64:(hh + 1) * 64, k * 128:(k + 1) * 128],
                                                                      lhsT=xdt[:, h * 64:(h + 1) * 64], rhs=MT[:, h, :],
                                                                      start=True, stop=True), r=[xdb, mtb], w=[pb])
                    for s in range(2):
                        if HG >= 2:
                            pe(lambda e, k=k, cc=cc, grp=grp, s=s, pa=pa: e.matmul(
                                pa[:, k * 128 + s * 64:k * 128 + s * 64 + 64], lhsT=Sbf[:, s, cc * 128:(cc + 1) * 128],
                                rhs=bc_b[:, 2 + grp, s * 64:(s + 1) * 64], start=True, stop=True), r=[sbfb, bcb], w=[pab])
                        else:
                            for hh in range(2):
                                h = 2 * cc + hh
                                pe(lambda e, k=k, h=h, hh=hh, s=s, pa=pa: e.matmul(
                                    pa[hh * 64:(hh + 1) * 64, k * 128 + s * 64:k * 128 + s * 64 + 64], lhsT=Sbf[:, s, h * 64:(h + 1) * 64],
                                    rhs=bc_b[:, 2 + h // HG, s * 64:(s + 1) * 64], start=True, stop=True), r=[sbfb, bcb], w=[pab])
                    pe(lambda e, k=k, cc=cc, pq=pq: e.matmul(pq[:, k * 128:(k + 1) * 128],
                                                             lhsT=dtx[:, 2 * cc:2 * cc + 2, :].rearrange("p h q -> p (h q)"),
                                                             rhs=m_iu[:], start=True, stop=True), r=[dxb, cst], w=[pqb])
                fl2 = lambda t: t[:, c0:c0 + cn, :].rearrange("p c q -> p (c q)")
                act(lambda e, pq=pq, cn=cn, o=fl2(eaB): e.activation(out=o, in_=pq[:, 0:cn * 128], func=AF.Exp), r=[pqb], w=[eBb])
                TT(dve, fl2(yb), pa[:, 0:cn * 128], fl2(eaB), ALU.mult, [pab, eBb], [ybb])
                TT(dve, fl2(yb), fl2(yb), pt[:, 0:cn * 128], ALU.add, [ybb, pb], [ybb])
            ytb_ = tk("ytmp")
            TT(pool, ytmp[:], zS[:, cX:cX + NBC, tsl], bc3(pvc("dskip", 0, NBC), 128), ALU.mult, [zsb, pvb], [ytb_])
            TT(dve, yb[:], yb[:], ytmp[:], ALU.add, [ybb, ytb_], [ybb])
            TT(dve, yb[:], yb[:], szl[:, :, tsl], ALU.mult, [ybb, szb], [ybb])
            ysb_ = tk("ysq")
            TT(pool, ysq[:], yb[:], yb[:], ALU.mult, [ybb], [ysb_])
            pt, pb = bank()
            rsb = tk("rs2")
            if NBC >= 2:
                hc = NBC // 2
                for gq in range(2):
                    for k in range(hc):
                        pe(lambda e, gq=gq, k=k, pt=pt: e.matmul(pt[:, gq * 128:(gq + 1) * 128], lhsT=ones_b[:], rhs=ysq[:, gq * hc + k, :],
                                                                 start=(k == 0), stop=(k == hc - 1)), r=[ysb_, cst], w=[pb])
                rstd(pt[:, 0:256], rs2[:].rearrange("p a q -> p (a q)"), 2.0 / c.BW, EPS_I, [pb], [rsb])
                for gq in range(2):
                    TT(dve, yb[:, gq * hc:(gq + 1) * hc, :], yb[:, gq * hc:(gq + 1) * hc, :],
                       rs2[:, gq:gq + 1, :].to_broadcast([128, hc, 128]), ALU.mult, [ybb, rsb], [ybb])
            else:
                pe(lambda e, pt=pt: e.matmul(pt[:, 0:128], lhsT=blk[:], rhs=ysq[:, 0, :], start=True, stop=True), r=[ysb_, cst], w=[pb])
                rstd(pt[:, 0:128], rs2[:, 0, :], 2.0 / c.BW, EPS_I, [pb], [rsb])
                TT(dve, yb[:, 0, :], yb[:, 0, :], rs2[:, 0, :], ALU.mult, [ybb, rsb], [ybb])
            TT(dve, yT[:, NH8:NH8 + NBC, tsl], yb[:], bc3(pvc("snw", 0, NBC), 128), ALU.mult, [ybb, pvb], [ytb])
            yield
            for s in range(2):
                pt, pb = bank()
                pe(lambda e, s=s, pt=pt: e.matmul(pt[:, 0:NB], lhsT=seqsel[:, s, :], rhs=dta[:],
                                                  start=True, stop=True), r=[cst, dab], w=[pb])
                elb = tk("eaL")
                act(lambda e, pt=pt: e.activation(out=eaL[:], in_=pt[:, 0:NB], func=AF.Exp), r=[pb], w=[elb])
                TT(dve, Sst[:, s, :].rearrange("p (h q) -> p h q", q=64), Sst[:, s, :].rearrange("p (h q) -> p h q", q=64),
                   bc3(eaL[:], 64), ALU.mult, [sstb, elb], [sstb])
                gw_ = HG * 64
                for n0 in range(0, NB * 64, min(512, gw_)):
                    nn = min(512, gw_)
                    grp = n0 // gw_
                    pt, pb = bank()
                    pe(lambda e, s=s, n0=n0, nn=nn, grp=grp, pt=pt: e.matmul(pt[:, 0:nn], lhsT=Btk[:, grp, :],
                                                                             rhs=xdt2[:, s, n0:n0 + nn], start=True, stop=True),
                       r=[btb, x2b], w=[pb])
                    TT(dve, Sst[:, s, n0:n0 + nn], Sst[:, s, n0:n0 + nn], pt[:, 0:nn], ALU.add, [sstb, pb], [sstb])
                act(lambda e, s=s: e.copy(out=Sbf[:, s, :], in_=Sst[:, s, :]), r=[sstb], w=[sbfb])

        cR, cK, cV, cWA, cG0, cG1 = c.cR, c.cK, c.cV, c.cWA, c.cG0, c.cG1

        def rwkv_prep(nG):
            Tt = 128 * nG
            nb = 2 * nG
            crb = tk("carry")
            dsc = big1[:, 0:NH8 * Tt].rearrange("p (c t) -> p c t", t=Tt)
            for c0 in range(0, NCHA, NH8):
                n = min(NH8, NCHA - c0)
                z4 = zR[:, c0:c0 + n, 0:Tt].rearrange("p c (b t) -> p c b t", t=64)
                d4 = dsc[:, 0:n, :].rearrange("p c (b t) -> p c b t", t=64)
                TT(dve, d4[:, :, :, 1:64], z4[:, :, :, 0:63], z4[:, :, :, 1:64], ALU.subtract, [zrb], [b1])
                if nb > 2:
                    TT(dve, d4[:, :, 2:nb, 0:1], z4[:, :, 0:nb - 2, 63:64], z4[:, :, 2:nb, 0:1], ALU.subtract, [zrb], [b1])
                TT(dve, d4[:, :, 0:2, 0:1], carry[:, c0:c0 + n, :].unsqueeze(3), z4[:, :, 0:2, 0:1], ALU.subtract, [zrb, crb], [b1])
                dve(lambda e, c0=c0, n=n, z4=z4: e.tensor_copy(out=carry[:, c0:c0 + n, :].unsqueeze(3), in_=z4[:, :, nb - 2:nb, 63:64]),
                    r=[zrb], w=[crb])
                TT(pool, dsc[:, 0:n, :], dsc[:, 0:n, :], bc3(pvc("mu", c0, n), Tt), ALU.mult, [b1, pvb], [b1])
                TT(dve, zR[:, c0:c0 + n, 0:Tt], zR[:, c0:c0 + n, 0:Tt], dsc[:, 0:n, :], ALU.add, [zrb, b1], [zrb])
            chk(46)
            wab, sgb = tk("wa_bf"), tk("sg_bf")
            act(lambda e: e.activation(out=wa_bf[0:64, 0:Tt], in_=zR[0:64, cWA, 0:Tt], func=AF.Tanh), r=[zrb], w=[wab])
            chk(47)
            act(lambda e: e.copy(out=wa_bf[64:128, 0:Tt], in_=zR[64:128, cWA, 0:Tt]), r=[zrb], w=[wab])
            chk(48)
            act(lambda e: e.activation(out=sg_bf[:, 0, 0:Tt], in_=zR[:, cG0, 0:Tt], func=AF.Sigmoid), r=[zrb], w=[sgb])
            chk(49)
            act(lambda e: e.activation(out=sg_bf[0:32, 1, 0:Tt], in_=zR[0:32, cG1, 0:Tt], func=AF.Sigmoid), r=[zrb], w=[sgb])
            chk(50)

        def rwkv_step(g, hb_):
            tsl = slice(g * 128, (g + 1) * 128)
            c_lo = hb_ * NHH
            h0 = hb_ * HB
            wab, sgb, lrb = tk("wa_bf"), tk("sg_bf"), tk("lowr")
            flw, fcs, fa, fep, fen, ft1, ft2, ft3, fg, fgw, fgb, fbq = [tk(n) for n in (
                "f_lw", "f_cs", "f_a", "f_ep", "f_en", "f_t1", "f_cs", "f_lw", "f_g", "f_gw", "f_gb", "f_bq")]
            arb, btb_, ktb, vbb = tk("ar_b"), tk("bt_b"), tk("kt_b"), tk("v_b")
            rz, kz, vz = zR[:, cR + c_lo:cR + c_lo + NHH, tsl], zR[:, cK + c_lo:cK + c_lo + NHH, tsl], zR[:, cV + c_lo:cV + c_lo + NHH, tsl]
            TT(dve, f_t1[:], kz, bc3(pvc("kk", c_lo, NHH), 128), ALU.mult, [zrb, pvb], [ft1])
            TT(pool, f_bq[:], f_t1[:], f_t1[:], ALU.mult, [ft1], [fbq])
            for c0 in range(0, NHH, 4):
                cn = min(4, NHH - c0)
                pt, pb = bank()
                for k in range(cn):
                    pe(lambda e, k=k, c0=c0, pt=pt: e.matmul(pt[:, k * 128:(k + 1) * 128], lhsT=blk[:], rhs=f_bq[:, c0 + k, :], start=True, stop=True),
                       r=[cst, fbq], w=[pb])
                o2 = f_t2[:, c0:c0 + cn, :].rearrange("p c t -> p (c t)")
                dve(lambda e, pt=pt, cn=cn, o2=o2: e.tensor_scalar(out=o2, in0=pt[:, 0:cn * 128], scalar1=1e-24, scalar2=None, op0=ALU.max),
                    r=[pb], w=[ft2])
                act(lambda e, o2=o2: e.activation(out=o2, in_=o2, func=AF.Ln), r=[ft2], w=[ft2])
                act(lambda e, o2=o2: e.activation(out=o2, in_=o2, func=AF.Exp, scale=-0.5), r=[ft2], w=[ft2])
            TT(dve, f_t1[:], f_t1[:], f_t2[:], ALU.mult, [ft1, ft2], [ft1])
            for cc in range(NHH):
                csl = slice((c_lo + cc) * 128, (c_lo + cc + 1) * 128)
                pt, pb = bank()
                pe(lambda e, pt=pt, csl=csl: e.matmul(pt[:, 0:128], lhsT=lowr_bf[:, 0, csl], rhs=wa_bf[:, tsl], start=True, stop=True),
                   r=[lrb, wab], w=[pb])
                pe(lambda e, pt=pt, csl=csl: e.matmul(pt[:, 128:256], lhsT=lowr_bf[:, 3, csl], rhs=wa_bf[:, tsl], start=True, stop=True),
                   r=[lrb, wab], w=[pb])
                ptg, pbg = bank()
                pe(lambda e, ptg=ptg, csl=csl: e.matmul(ptg[:, 0:128], lhsT=lowr_bf[:, 1, csl], rhs=sg_bf[:, 0, tsl], start=True, stop=False),
                   r=[lrb, sgb], w=[pbg])
                pe(lambda e, ptg=ptg, csl=csl: e.matmul(ptg[:, 0:128], lhsT=lowr_bf[0:32, 2, csl], rhs=sg_bf[0:32, 1, tsl], start=False, stop=True),
                   r=[lrb, sgb], w=[pbg])
                act(lambda e, pt=pt, cc=cc: e.activation(out=f_lw[:, cc, :], in_=pt[:, 0:128], func=AF.Sigmoid, bias=pvc("w0", c_lo + cc)),
                    r=[pb, pvb], w=[flw])
                act(lambda e, pt=pt, cc=cc: e.activation(out=f_a[:, cc, :], in_=pt[:, 128:256], func=AF.Sigmoid, bias=pvc("a0", c_lo + cc)),
                    r=[pb, pvb], w=[fa])
                dve(lambda e, ptg=ptg, cc=cc: e.tensor_copy(out=f_g[:, cc, :], in_=ptg[:, 0:128]), r=[pbg], w=[fg])
            chk(601)
            dve(lambda e: e.tensor_scalar(out=f_lw[:], in0=f_lw[:], scalar1=-0.6065306597126334, scalar2=None, op0=ALU.mult), r=[flw], w=[flw])
            for cc in range(NHH):
                dve(lambda e, cc=cc: e.tensor_tensor_scan(out=f_cs[:, cc, :], data0=rmask[:], data1=f_lw[:, cc, :], initial=0.0,
                                                          op0=ALU.mult, op1=ALU.add), r=[flw, cst], w=[fcs])
            chk(602)
            fl3 = lambda t: t[:].rearrange("p c t -> p (c t)")
            act(lambda e: e.activation(out=fl3(f_ep), in_=fl3(f_cs), func=AF.Exp), r=[fcs], w=[fep])
            act(lambda e: e.activation(out=fl3(f_en), in_=fl3(f_cs), func=AF.Exp, scale=-1.0), r=[fcs], w=[fen])
            ep4 = f_ep[:].rearrange("p c (b t) -> p c b t", t=64)
            t34 = f_t3[:].rearrange("p c (b t) -> p c b t", t=64)
            pool(lambda e: e.memset(t34[:, :, :, 0:1], 1.0), w=[ft3])
            pool(lambda e: e.tensor_copy(out=t34[:, :, :, 1:64], in_=ep4[:, :, :, 0:63]), r=[fep], w=[ft3])
            chk(603)
            TT(dve, ar_b[:, :, 0, :], f_t1[:], f_t3[:], ALU.mult, [ft1, ft3], [arb])
            TT(pool, f_t2[:], f_t1[:], f_a[:], ALU.mult, [ft1, fa], [ft2])
            TT(dve, bt_b[:], f_t2[:], f_en[:], ALU.mult, [ft2, fen], [btb_])
            TT(dve, f_t3[:], f_a[:], bc3(pvc("ka", c_lo, NHH), 128), ALU.mult, [fa, pvb, arb], [ft3])
            TT(dve, f_t3[:], f_t3[:], bc3(onemka[:, c_lo:c_lo + NHH], 128), ALU.add, [ft3, cst], [ft3])
            TT(dve, f_t3[:], f_t3[:], kz, ALU.mult, [ft3, zrb], [ft3])
            TT(pool, kt_b[:], f_t3[:], f_en[:], ALU.mult, [ft3, fen], [ktb])
            TT(dve, ar_b[:, :, 1, :], rz, f_ep[:], ALU.mult, [zrb, fep], [arb])
            chk(605)
            act(lambda e: e.copy(out=v_b[:], in_=vz), r=[zrb], w=[vbb])
            TT(dve, f_t2[:], f_t3[:], rz, ALU.mult, [ft3, zrb, btb_], [ft2])
            TT(dve, f_bq[:], f_t2[:], bc3(pvc("rk", c_lo, NHH), 128), ALU.mult, [ft2, pvb], [fbq])
            for c0 in range(0, NHH, 4):
                cn = min(4, NHH - c0)
                pt, pb = bank()
                for k in range(cn):
                    pe(lambda e, k=k, c0=c0, pt=pt: e.matmul(pt[:, k * 128:(k + 1) * 128], lhsT=blk[:], rhs=f_bq[:, c0 + k, :], start=True, stop=True),
                       r=[cst, fbq], w=[pb])
                TT(dve, f_t2[:, c0:c0 + cn, :], pt[:, 0:cn * 128].rearrange("p (c t) -> p c t", t=128), zR[:, cV + c_lo + c0:cV + c_lo + c0 + cn, tsl],
                   ALU.mult, [pb, zrb, ft2], [ft2])
            TT(dve, f_t2[:], f_t2[:], bc3(pvc("lnb", c_lo, NHH), 128), ALU.add, [ft2, pvb], [ft2])
            TT(dve, f_gb[:], f_t2[:], f_g[:], ALU.mult, [ft2, fg], [fgb])
            TT(pool, f_gw[:], f_g[:], bc3(pvc("lnw", c_lo, NHH), 128), ALU.mult, [fg, pvb], [fgw])
            armb, btmb, ktmb = tk("arm"), tk("btm"), tk("ktm")
            for hp in range(2):
                dve(lambda e, hp=hp: e.tensor_scalar(out=arm[:, :, hp, :, :].rearrange("p c a t -> p c (a t)"), in0=ar_b[:].rearrange("p c a t -> p c (a t)"),
                                                     scalar1=pmask[:, hp:hp + 1], scalar2=None, op0=ALU.mult), r=[arb, cst], w=[armb])
                act(lambda e, hp=hp: e.activation(out=btm[:, :, hp, :], in_=bt_b[:], func=AF.Copy, scale=pmask[:, hp:hp + 1]),
                    r=[btb_, cst], w=[btmb])
                act(lambda e, hp=hp: e.activation(out=ktm[:, :, hp, :], in_=kt_b[:], func=AF.Copy, scale=pmask[:, hp:hp + 1]),
                    r=[ktb, cst], w=[ktmb])
            chk(61)
            yield
            vtb, kkb, bbb = tk("Vtok"), tk("Ktok"), tk("Btok")
            for src, srcb, dst, dstb in ((v_b, vbb, Vtok, vtb), (kt_b, ktb, Ktok, kkb), (bt_b, btb_, Btok, bbb)):
                for c0 in range(0, NHH, 8):
                    cn = min(8, NHH - c0)
                    pt, pb = bank()
                    ptb = pt[:].bitcast(BF16)
                    for k in range(cn):
                        pe(lambda e, k=k, c0=c0, ptb=ptb, src=src: e.transpose(ptb[:, k * 128:(k + 1) * 128], src[:, c0 + k, :], ident_b[:]),
                           r=[srcb, cst], w=[pb])
                    act(lambda e, c0=c0, cn=cn, ptb=ptb, dst=dst: e.copy(out=dst[:, c0 * 128:(c0 + cn) * 128], in_=ptb[:, 0:cn * 128]),
                        r=[pb], w=[dstb])
            vtsb, ussb = tk("Vts"), tk("Uss")
            for s in range(2):
                dve(lambda e, s=s: e.tensor_scalar(out=Vts[:, s, :], in0=Vtok[:], scalar1=pmask[:, s:s + 1], scalar2=None, op0=ALU.mult),
                    r=[vtb, cst], w=[vtsb])
            chk(62)
            yield
            hstb, hbfb = tk("Hst"), tk("Hbf")
            xrb, akb = tk("XRB"), tk("AKR")
            ysbb = tk("Ysb")
            xb_ = [tk("Xp0"), tk("Xp1")]
            lb_ = [tk("Lp0"), tk("Lp1")]
            pb_ = [tk("Pp0"), tk("Pp1")]
            for hl in range(0, HB, 2):
                ptA, pbA = bank()
                ptB, pbB = bank()
                for d in range(2):
                    h = h0 + hl + d
                    cg, hp = h // 2, h % 2
                    cc = cg - c_lo
                    arv = ar_b[:, cc, :, :].rearrange("p a t -> p (a t)")
                    pe(lambda e, ptA=ptA, d=d, cc=cc, hp=hp, arv=arv: e.matmul(ptA[:, d * 256:(d + 1) * 256], lhsT=btm[:, cc, hp, :], rhs=arv,
                                                                             start=True, stop=True), r=[btmb, arb], w=[pbA])
                    pe(lambda e, ptB=ptB, d=d, cc=cc, hp=hp, arv=arv: e.matmul(ptB[:, d * 256:(d + 1) * 256], lhsT=ktm[:, cc, hp, :], rhs=arv,
                                                                             start=True, stop=True), r=[ktmb, arb], w=[pbB])
                TT(dve, XRB[:, hl:hl + 2, :, :].rearrange("p h a t -> p h (a t)"), ptA[:, 0:512].rearrange("p (h x) -> p h x", x=256),
                   m_xr[:].unsqueeze(1).to_broadcast([128, 2, 256]), ALU.mult, [pbA, cst], [xrb])
                TT(dve, AKR[:, hl:hl + 2, :, :].rearrange("p h a t -> p h (a t)"), ptB[:, 0:512].rearrange("p (h x) -> p h x", x=256),
                   m_xr[:].unsqueeze(1).to_broadcast([128, 2, 256]), ALU.mult, [pbB, cst], [akb])
            for hl0 in range(0, HB, 4):
                hn = min(4, HB - hl0)
                pt, pb = bank()
                for d in range(hn):
                    h = h0 + hl0 + d
                    cg, hp = h // 2, h % 2
                    cc = cg - c_lo
                    pe(lambda e, pt=pt, d=d, cc=cc, hp=hp: e.matmul(pt[:, d * 128:(d + 1) * 128], lhsT=arm[:, cc, hp, 0, :], rhs=bt_b[:, cc, :],
                                                                    start=True, stop=True), r=[armb, btb_], w=[pb])
                TT(dve, Lp[0][:, hl0:hl0 + hn, :], pt[:, 0:hn * 128].rearrange("p (h t) -> p h t", t=128),
                   m_sl[:].unsqueeze(1).to_broadcast([128, hn, 128]), ALU.mult, [pb, cst], [lb_[0]])
            chk(63)
            yield
            TT(pool, Pp[0][:], imx[:].unsqueeze(1).to_broadcast([128, HB, 128]), XRB[:, :, 0, :], ALU.subtract, [cst, xrb], [pb_[0]])

            def Xk(k, hl):
                return XRB[:, hl, 0, :] if k == 0 else Xp[k % 2][:, hl, :]

            def Xkb(k):
                return xrb if k == 0 else xb_[k % 2]
            for k in range(1, 6):
                pr, cu = (k - 1) % 2, k % 2
                for hl0 in range(0, HB, 4):
                    hn = min(4, HB - hl0)
                    pt, pb = bank()
                    for d in range(hn):
                        hl = hl0 + d
                        pe(lambda e, pt=pt, d=d, hl=hl, k=k, pr=pr: e.matmul(pt[:, d * 128:(d + 1) * 128], lhsT=Xk(k - 1, hl), rhs=Lp[pr][:, hl, :],
                                                                             start=True, stop=True), r=[Xkb(k - 1), lb_[pr]], w=[pb])
                    act(lambda e, pt=pt, hl0=hl0, hn=hn, cu=cu: e.copy(out=Lp[cu][:, hl0:hl0 + hn, :].rearrange("p h t -> p (h t)"), in_=pt[:, 0:hn * 128]),
                        r=[pb], w=[lb_[cu]])
                    if k <= 4:
                        pt2, pb2 = bank()
                        for d in range(hn):
                            hl = hl0 + d
                            pe(lambda e, pt2=pt2, d=d, hl=hl, k=k, pr=pr: e.matmul(pt2[:, d * 128:(d + 1) * 128], lhsT=Lp[pr][:, hl, :], rhs=Xk(k - 1, hl),
                                                                                   start=True, stop=True), r=[Xkb(k - 1), lb_[pr]], w=[pb2])
                        dve(lambda e, pt2=pt2, hl0=hl0, hn=hn, cu=cu: e.tensor_copy(out=Xp[cu][:, hl0:hl0 + hn, :].rearrange("p h t -> p (h t)"),
                                                                                    in_=pt2[:, 0:hn * 128]), r=[pb2], w=[xb_[cu]])
                for hl0 in range(0, HB, 4):
                    hn = min(4, HB - hl0)
                    pt, pb = bank()
                    for d in range(hn):
                        hl = hl0 + d
                        pe(lambda e, pt=pt, d=d, hl=hl, cu=cu, pr=pr: e.matmul(pt[:, d * 128:(d + 1) * 128], lhsT=Lp[cu][:, hl, :], rhs=Pp[pr][:, hl, :],
                                                                               start=True, stop=True), r=[lb_[cu], pb_[pr]], w=[pb])
                    TT(dve, Pp[cu][:, hl0:hl0 + hn, :].rearrange("p h t -> p (h t)"), pt[:, 0:hn * 128],
                       Pp[pr][:, hl0:hl0 + hn, :].rearrange("p h t -> p (h t)"), ALU.add, [pb, pb_[pr]], [pb_[cu]])
            chk(64)
            yield
            TTt = Pp[1]
            ttb = pb_[1]
            wsb_, usb_ = tk("Wsb"), tk("Usb")
            pt, pb = bank()
            for hl in range(HB):
                h = h0 + hl
                cg, hp = h // 2, h % 2
                cc = cg - c_lo
                psl = slice(hp * 64, hp * 64 + 64)
                for s in range(2):
                    ssl = slice(s * 64, s * 64 + 64)
                    pe(lambda e, pt=pt, hl=hl, cc=cc, cg=cg, hp=hp, s=s, ssl=ssl: e.matmul(pt[ssl, hl * 64:(hl + 1) * 64], lhsT=arm[:, cc, hp, 0, ssl],
                                                                                  rhs=Hbf[:, s, cg, :], start=True, stop=False),
                       r=[armb, hbfb], w=[pb])
                    pe(lambda e, pt=pt, hl=hl, h=h, s=s, ssl=ssl: e.matmul(pt[ssl, hl * 64:(hl + 1) * 64], lhsT=AKR[:, hl, 0, ssl],
                                                                         rhs=Vtok[:, hl * 64:(hl + 1) * 64], start=False, stop=True),
                       r=[akb, vtb], w=[pb])
            act(lambda e, pt=pt: e.copy(out=Wsb[:].rearrange("p h i -> p (h i)"), in_=pt[:, 0:HB * 64]), r=[pb], w=[wsb_])
            pt, pb = bank()
            for hl in range(HB):
                pe(lambda e, pt=pt, hl=hl: e.matmul(pt[:, hl * 64:(hl + 1) * 64], lhsT=TTt[:, hl, :], rhs=Wsb[:, hl, :], start=True, stop=True),
                   r=[ttb, wsb_], w=[pb])
            act(lambda e, pt=pt: e.activation(out=Usb[:].rearrange("p h i -> p (h i)"), in_=pt[:, 0:HB * 64], func=AF.Copy, scale=-1.0),
                r=[pb], w=[usb_])
            for s in range(2):
                dve(lambda e, s=s: e.tensor_scalar(out=Uss[:, s, :], in0=Usb[:].rearrange("p h i -> p (h i)"), scalar1=pmask[:, s:s + 1],
                                                   scalar2=None, op0=ALU.mult), r=[usb_, cst], w=[ussb])
            chk(65)
            yield
            pt, pb = bank()
            for hl in range(HB):
                h = h0 + hl
                cg, hp = h // 2, h % 2
                cc = cg - c_lo
                psl = slice(hp * 64, hp * 64 + 64)
                for s in range(2):
                    ssl = slice(s * 64, s * 64 + 64)
                    o = pt[ssl, hl * 64:(hl + 1) * 64]
                    pe(lambda e, o=o, cc=cc, cg=cg, hp=hp, s=s, ssl=ssl: e.matmul(o, lhsT=arm[:, cc, hp, 1, ssl], rhs=Hbf[:, s, cg, :], start=True, stop=False),
                       r=[armb, hbfb], w=[pb])
                    pe(lambda e, o=o, hl=hl, ssl=ssl: e.matmul(o, lhsT=XRB[:, hl, 1, ssl], rhs=Usb[:, hl, :], start=False, stop=False),
                       r=[xrb, usb_], w=[pb])
                    pe(lambda e, o=o, hl=hl, h=h, ssl=ssl: e.matmul(o, lhsT=AKR[:, hl, 1, ssl], rhs=Vtok[:, hl * 64:(hl + 1) * 64], start=False, stop=True),
                       r=[akb, vtb], w=[pb])
            act(lambda e, pt=pt, h0=h0: e.copy(out=Ysb[:, 0:HB, :].rearrange("p h i -> p (h i)"), in_=pt[:, 0:HB * 64]), r=[pb], w=[ysbb])
            chk(66)
            yield
            ncl = HB // 2
            cl0 = h0 // 2
            cll = 0
            for s in range(2):
                ssl = slice(s * 64, s * 64 + 64)
                pt, pb = bank()
                for hl in range(HB):
                    h = h0 + hl
                    cg, hp = h // 2, h % 2
                    cc = cg - c_lo
                    o = pt[hp * 64:hp * 64 + 64, (hl // 2) * 64:(hl // 2) * 64 + 64]
                    pe(lambda e, o=o, h=h, hl=hl, s=s: e.matmul(o, lhsT=Btok[:, hl * 64:(hl + 1) * 64], rhs=Uss[:, s, hl * 64:(hl + 1) * 64], start=True, stop=False),
                       r=[bbb, ussb], w=[pb])
                    pe(lambda e, hl=hl, o=o, h=h, s=s: e.matmul(o, lhsT=Ktok[:, hl * 64:(hl + 1) * 64], rhs=Vts[:, s, hl * 64:(hl + 1) * 64], start=False, stop=True),
                       r=[kkb, vtsb], w=[pb])
                htb = tk("Htmp")
                TT(dve, Htmp[:, 0:ncl, :], pt[:, 0:ncl * 64].rearrange("p (c i) -> p c i", i=64), Hst[:, s, cl0:cl0 + ncl, :], ALU.add,
                   [pb, hstb], [htb])
                TT(dve, Hst[:, s, cl0:cl0 + ncl, :], Htmp[:, 0:ncl, :], f_ep[:, 0:ncl, s * 64 + 63:s * 64 + 64].to_broadcast([128, ncl, 64]),
                   ALU.mult, [htb, fep], [hstb])
                act(lambda e, s=s, cl0=cl0, ncl=ncl: e.copy(out=Hbf[:, s, cl0:cl0 + ncl, :], in_=Hst[:, s, cl0:cl0 + ncl, :]), r=[hstb], w=[hbfb])
            chk(67)
            yield
            gsb, ysq_b = tk("gst"), tk("Ysq")
            dve(lambda e: e.tensor_reduce(out=gst[:, 0, :], in_=Ysb[:], axis=AX.X, op=ALU.add), r=[ysbb], w=[gsb])
            TT(pool, Ysq[:], Ysb[:], Ysb[:], ALU.mult, [ysbb], [ysq_b])
            dve(lambda e: e.tensor_reduce(out=gst[:, 1, :], in_=Ysq[:], axis=AX.X, op=ALU.add), r=[ysq_b], w=[gsb])
            dve(lambda e: e.tensor_scalar(out=gst[:, 0, :], in0=gst[:, 0, :], scalar1=1.0 / 64, scalar2=None, op0=ALU.mult), r=[gsb], w=[gsb])
            TT(dve, gst[:, 2, :], gst[:, 0, :], gst[:, 0, :], ALU.mult, [gsb], [gsb])
            dve(lambda e: e.scalar_tensor_tensor(out=gst[:, 3, :], in0=gst[:, 1, :], scalar=1.0 / 64, in1=gst[:, 2, :], op0=ALU.mult, op1=ALU.subtract),
                r=[gsb], w=[gsb])
            rstd(gst[:, 3, :], gst[:, 3, :], 1.0, LNX_I, [gsb], [gsb])
            TT(dve, Ysq[:], Ysb[:], bc3(gst[:, 0, :], 64), ALU.subtract, [ysbb, gsb, ysq_b], [ysq_b])
            TT(dve, Ysq[:], Ysq[:], bc3(gst[:, 3, :], 64), ALU.mult, [ysq_b, gsb], [ysq_b])
            yfl = Ysq[:].rearrange("p h i -> p (h i)")
            for c0 in range(0, NHH, 4):
                cn = min(4, NHH - c0)
                pt, pb = bank()
                for k in range(cn):
                    pe(lambda e, k=k, c0=c0, pt=pt: e.transpose(pt[:, k * 128:(k + 1) * 128], yfl[:, (c0 + k) * 128:(c0 + k + 1) * 128], ident_f[:]),
                       r=[ysq_b, cst], w=[pb])
                TT(dve, f_t1[:, c0:c0 + cn, :], pt[:, 0:cn * 128].rearrange("p (c t) -> p c t", t=128), f_gw[:, c0:c0 + cn, :], ALU.mult,
                   [pb, fgw, ft1], [ft1])
                TT(dve, yT[:, c_lo + c0:c_lo + c0 + cn, tsl], f_t1[:, c0:c0 + cn, :], f_gb[:, c0:c0 + cn, :], ALU.add, [ft1, fgb], [ytb])


        def m1_gen(nG, x_rows):
            Tt = 128 * nG
            chk(0)
            for g in range(nG):
                for s in range(2):
                    P.dma("sp", xt[s * 64:(s + 1) * 64, g, :], x_rows[g][s], writes=[xb])
            for g in range(nG):
                norm_to_hT(g, "nmix")
            yield
            chk(1)
            yield from in_proj(c.BCH, NCHA, Tt, zS, zsb)
            chk(2)

        def m2_gen(nG, first_prompt):
            Tt = 128 * nG
            ssd_prep(nG, first_prompt)
            yield
            chk(3)

            def ssd_all():
                for g in range(nG):
                    yield from ssd_step(g)
            ga, gb = ssd_all(), in_proj(c.ACH, 0, Tt, zR, zrb)
            while ga is not None or gb is not None:
                if ga is not None:
                    try:
                        next(ga)
                    except StopIteration:
                        ga = None
                if gb is not None:
                    try:
                        next(gb)
                        next(gb)
                    except StopIteration:
                        gb = None
                yield
            chk(4)
            chk(45)
            rwkv_prep(nG)
            yield
            chk(5)
            fence()
            yield

        def m3_gen(nG):
            for g in range(nG):
                for hb_ in range(NBH):
                    yield from rwkv_step(g, hb_)
            chk(6)
            fence()
            yield

        def dense_gen(nG, x_rows, y_rows, tidx):
            Tt = 128 * nG
            for g in range(nG):
                for s in range(2):
                    P.dma("sp", xt[s * 64:(s + 1) * 64, g, :], x_rows[g][s], writes=[xb])

            def ev_res(g, ng, pt, pb):
                TT(dve, xt[:, g, ng * NW:(ng + 1) * NW], xt[:, g, ng * NW:(ng + 1) * NW], pt[:, 0:NW], ALU.add, [xb, pb], [xb])
            yield from dense_tok(nG, wout_s, t_wout, KC // c.KPG_OUT, c.KPG_OUT, lambda k, g: (yT[:, k, g * 128:(g + 1) * 128], ytb), ev_res)
            for g in range(nG):
                P.dma("sp", x1s[tidx, g], xt[:, g, :], reads=[xb], writes=[x1b[tidx]])
            yield

        return m1_gen, m2_gen, m3_gen, dense_gen

    phases = [make_phases(i) for i in range(NPAR)]

    def interleave(ga, gb, fa=False, fb=False):
        while ga is not None or gb is not None:
            if ga is not None:
                extra_reads[:] = [AR] if fa else []
                try:
                    next(ga)
                except StopIteration:
                    ga = None
            if gb is not None:
                extra_reads[:] = [AR] if fb else []
                try:
                    next(gb)
                except StopIteration:
                    gb = None
        extra_reads[:] = []

    stb = [tk("carry"), tk("Hst"), tk("Hbf"), tk("Sst"), tk("Sbf"), tk("convst")]
    pool(lambda e: e.memset(zS_raw[:], 0.0), w=[tk("zS")])
    pool(lambda e: e.memset(zR_raw[:], 0.0), w=[tk("zR")])
    pool(lambda e: e.memset(carry[:], 0.0), w=[tk("carry")])
    pool(lambda e: e.memset(Hst[:], 0.0), w=[tk("Hst")])
    pool(lambda e: e.memset(Hbf[:], 0.0), w=[tk("Hbf")])
    pool(lambda e: e.memset(Sst[:], 0.0), w=[tk("Sst")])
    pool(lambda e: e.memset(Sbf[:], 0.0), w=[tk("Sbf")])
    pool(lambda e: e.memset(convst[:], 0.0), w=[tk("convst")])

    def x_of(step, s):
        if step == 0:
            return metac
        return xp[s, (step - 1) * 64:step * 64, :]

    def y_of(step, s):
        if step == 0:
            return None
        return yp[s, (step - 1) * 64:step * 64, :]

    def dump_states(i):
        P.dma("sp", o_shift[i], carry[:], reads=[tk("carry")])
        P.dma("sp", o_wkv[i], Hst[:], reads=[tk("Hst")])
        P.dma("sp", o_conv[i], convst[:], reads=[tk("convst")])
        P.dma("sp", o_ssm[i], Sst[:], reads=[tk("Sst")])

    def swap_states():
        dump_states(0)
        P.dma("sp", carry[:], st_shift, writes=[tk("carry")])
        P.dma("sp", Hst[:], st_wkv, writes=[tk("Hst")])
        P.dma("sp", convst[:], st_conv, writes=[tk("convst")])
        P.dma("sp", Sst[:], st_ssm, writes=[tk("Sst")])
        act(lambda e: e.copy(out=Hbf[:], in_=Hst[:]), r=[tk("Hst")], w=[tk("Hbf")])
        act(lambda e: e.copy(out=Sbf[:], in_=Sst[:]), r=[tk("Sst")], w=[tk("Sbf")])

    tiles = []
    for mt in range(c.NSTEP_P // G):
        steps = [mt * G + g for g in range(G)]
        tiles.append((G, [[x_of(st, s) for s in range(2)] for st in steps], [[y_of(st, s) for s in range(2)] for st in steps], mt == 0, False))
    tiles.append((1, [[xs[0], xs[1]]], [[ys[0], ys[1]]], False, True))
    NT = len(tiles)

    def mk_m1(t):
        nG, xr, yr, fp, smp = tiles[t]
        return phases[t % NPAR][0](nG, xr)

    def mk_m2(t):
        nG, xr, yr, fp, smp = tiles[t]
        if smp:
            swap_states()
        return phases[t % NPAR][1](nG, fp)

    def mk_m3(t):
        return phases[t % NPAR][2](tiles[t][0])

    def mk_dense(t):
        nG, xr, yr, fp, smp = tiles[t]
        return phases[t % NPAR][3](nG, xr, yr, t)

    try:
        if NPAR == 1:
            for t in range(NT):
                for mk, fl_ in ((mk_m1, False), (mk_m2, True), (mk_m3, True), (mk_dense, False)):
                    interleave(mk(t), None, fl_, False)
        else:
            interleave(mk_m1(0), None)
            interleave(mk_m2(0), None, True, False)
            for t in range(NT):
                interleave(mk_m3(t), mk_m1(t + 1) if t + 1 < NT else None, True, False)
                issue_pending(1)
                interleave(mk_dense(t), mk_m2(t + 1) if t + 1 < NT else None, False, True)
    except _Stop:
        P.finish()
        return nc, P
    dump_states(1)
    issue_pending(len(pending))
    P.barrier()
    rts = []
    for t in range(NT):
        nG, xr, yr, fp, smp = tiles[t]
        for g in range(nG):
            if yr[g][0] is not None or yr[g][1] is not None:
                rts.append((t, g, yr[g]))
    xFb = [Buf("xtF0"), Buf("xtF1")]
    hFb, aFb, svb, xnb = Buf("hTF"), Buf("actTF"), Buf("svF"), Buf("xnF")
    sqb = xnb
    smallF = small_all[0]
    ssb, rsb2 = Buf("ssF"), Buf("rstdF")
    ngrp = (len(rts) + GF - 1) // GF
    for gi in range(ngrp):
        grp = rts[gi * GF:(gi + 1) * GF]
        nj = len(grp)
        TFg = 128 * nj
        xf, xfb = xtF[gi % 2], xFb[gi % 2]
        for j, (t, g, yr) in enumerate(grp):
            rd = list(x1b) if gi < 2 else [x1b[t]]
            P.dma("sp", xf[:, j, :], x1s[t, g], reads=rd, writes=[xfb])
        for j in range(nj):
            act(lambda e, j=j, xf=xf: e.activation(out=sqF[:], in_=xf[:, j, :], func=AF.Square, accum_out=smallF[:, 4:5]), r=[xfb], w=[sqb, ssb])
            rstd(smallF[:, 4:5], smallF[:, 5:6], 1.0 / D, EPS_I, [ssb], [rsb2])
            act(lambda e, j=j, xf=xf: e.activation(out=xnF[:], in_=xf[:, j, :], func=AF.Copy, scale=smallF[:, 5:6]), r=[xfb, rsb2], w=[xnb])
            for k0 in range(0, KC, 4):
                kn = min(4, KC - k0)
                pt, pb = bank()
                ptb = pt[:].bitcast(BF16)
                for k in range(kn):
                    pe(lambda e, k=k, k0=k0, ptb=ptb: e.transpose(ptb[:, k * 128:(k + 1) * 128], xnF[:, (k0 + k) * 128:(k0 + k + 1) * 128], ident_b[:]),
                       r=[xnb, cst], w=[pb])
                TT(dve, hTF[:, k0:k0 + kn, j * 128:(j + 1) * 128], ptb[:, 0:kn * 128].rearrange("p (k t) -> p k t", t=128),
                   bc3(pvc("nffn", k0, kn), 128), ALU.mult, [pb, pvb], [hFb])
        for f in range(NF):
            wv, wb = load_w(wgu_s[f].rearrange("p a k n -> p (a k n)"), 2 * KC * 128, find(t_wgu, f))
            wv4 = wv.rearrange("p (a k n) -> p a k n", a=2, n=128)
            pg, pgb = bank()
            pu, pub = bank()
            for k in range(KC):
                pe(lambda e, k=k, pg=pg, wv4=wv4, TFg=TFg: e.matmul(pg[:, 0:TFg], lhsT=wv4[:, 0, k, :], rhs=hTF[:, k, 0:TFg], start=(k == 0), stop=(k == KC - 1)),
                   r=[wb, hFb], w=[pgb])
            for k in range(KC):
                pe(lambda e, k=k, pu=pu, wv4=wv4, TFg=TFg: e.matmul(pu[:, 0:TFg], lhsT=wv4[:, 1, k, :], rhs=hTF[:, k, 0:TFg], start=(k == 0), stop=(k == KC - 1)),
                   r=[wb, hFb], w=[pub])
            act(lambda e, pg=pg, TFg=TFg: e.activation(out=svF[:, 0:TFg], in_=pg[:, 0:TFg], func=AF.Silu), r=[pgb], w=[svb])
            TT(dve, actTF[:, f, 0:TFg], svF[:, 0:TFg], pu[:, 0:TFg], ALU.mult, [svb, pub], [aFb])
        nkg, kpg = NF // c.KPG_DN, c.KPG_DN
        for ng in range(c.NNG):
            banks = [bank() for _ in range(nj)]
            for kg in range(nkg):
                wv, wb = load_w(wdn_s[ng, kg].rearrange("p k n -> p (k n)"), kpg * NW, find(t_wdn, ng * nkg + kg))
                wv3 = wv.rearrange("p (k n) -> p k n", n=NW)
                for j in range(nj):
                    pt, pb = banks[j]
                    for kk in range(kpg):
                        kabs = kg * kpg + kk
                        pe(lambda e, pt=pt, j=j, wv3=wv3, kk=kk, kabs=kabs: e.matmul(
                            pt[:, 0:NW], lhsT=actTF[:, kabs, j * 128:(j + 1) * 128], rhs=wv3[:, kk, :], start=(kabs == 0), stop=(kabs == nkg * kpg - 1)),
                           r=[wb, aFb], w=[pb])
            for j in range(nj):
                pt, pb = banks[j]
                TT(dve, xf[:, j, ng * NW:(ng + 1) * NW], xf[:, j, ng * NW:(ng + 1) * NW], pt[:, 0:NW], ALU.add, [xfb, pb], [xfb])
        for j, (t, g, yr) in enumerate(grp):
            act(lambda e, j=j, xf=xf: e.activation(out=sqF[:], in_=xf[:, j, :], func=AF.Square, accum_out=smallF[:, 6:7]), r=[xfb], w=[sqb, ssb])
            rstd(smallF[:, 6:7], smallF[:, 7:8], 1.0 / D, EPS_I, [ssb], [rsb2])
            dve(lambda e, j=j, xf=xf: e.scalar_tensor_tensor(out=xf[:, j, :], in0=xf[:, j, :], scalar=smallF[:, 7:8], in1=nfw_bc[:], op0=ALU.mult, op1=ALU.mult),
                r=[xfb, rsb2, tk("nfw")], w=[xfb])
            for s in range(2):
                if yr[s] is not None:
                    P.dma("sp", yr[s], xf[s * 64:(s + 1) * 64, j, :], reads=[xfb])
    P.finish()
    return nc, P


def _fm(vec, chunks):
    out = np.zeros((128, len(chunks)), np.float32)
    for i, (c0, n) in enumerate(chunks):
        out[:n, i] = vec[c0:c0 + n]
    return out


def _fm_even(vec):
    return np.ascontiguousarray(np.asarray(vec, np.float32).reshape(-1, 128).T)


def host_weights(c, I):
    f = lambda k: np.asarray(I[k], np.float32)
    pvo, NPV = pv_layout(c)
    pv = np.zeros((128, NPV), np.float32)

    def put(name, arr):
        o, w = pvo[name]
        assert arr.shape == (128, w), (name, arr.shape, w)
        pv[:, o:o + w] = arr
    put("mu", _fm(f("rwkv_mu")[0], c.ACH))
    put("w0", _fm_even(f("rwkv_w0")[0]))
    put("a0", _fm_even(f("rwkv_a0")[0]))
    put("kk", _fm_even(f("rwkv_k_k")[0]))
    put("ka", _fm_even(f("rwkv_k_a")[0]))
    put("rk", _fm_even(f("rwkv_r_k")[0].reshape(-1)))
    put("lnw", _fm_even(f("rwkv_lnx_w")[0]))
    put("lnb", _fm_even(f("rwkv_lnx_b")[0]))
    cw = f("ssm_conv_w")[0]
    put("convw", np.ascontiguousarray(cw.reshape(c.NXBC, 128, 4).transpose(1, 0, 2)).reshape(128, c.NXBC * 4))
    put("convb", _fm_even(f("ssm_conv_b")[0]))
    put("snw", _fm_even(f("ssm_norm_w")[0]))
    put("dskip", _fm_even(np.repeat(f("ssm_D")[0], 64)))
    dtb = np.zeros((128, 1), np.float32)
    dtb[:c.NB, 0] = f("ssm_dt_bias")[0]
    put("dtb", dtb)
    put("nmix", _fm_even(f("norm_mix_w")[0]))
    put("nffn", _fm_even(f("norm_ffn_w")[0]))
    lowr = np.zeros((128, 4, c.AW), np.float32)
    lowr[0:64, 0] = f("rwkv_w_up")[0]
    lowr[64:128, 3] = f("rwkv_a_up")[0]
    gu = f("rwkv_g_up")[0]
    lowr[:, 1] = gu[0:128]
    lowr[0:32, 2] = gu[128:160]
    w_in = f("w_in")[0]
    KC = c.KC
    win_h = np.zeros((c.NCHIN, 128, KC, 128), np.float32)
    for ci, (c0, n) in enumerate(c.ACH + c.BCH):
        win_h[ci, :, :, :n] = w_in[:, c0:c0 + n].reshape(KC, 128, n).transpose(1, 0, 2)
    NW = c.NW

    def tokw(w, kpg):
        K, N = w.shape
        return np.ascontiguousarray(w.reshape(K // (128 * kpg), kpg, 128, N // NW, NW).transpose(3, 0, 2, 1, 4))
    wout_h = tokw(f("w_out")[0], c.KPG_OUT)
    wdn_h = tokw(f("ffn_w_down")[0], c.KPG_DN)
    wg = f("ffn_w_gate")[0].reshape(KC, 128, c.NF, 128)
    wu = f("ffn_w_up")[0].reshape(KC, 128, c.NF, 128)
    wgu_h = np.ascontiguousarray(np.stack([wg, wu], 0).transpose(3, 2, 0, 1, 4))
    metac = np.zeros((64, c.D), np.float32)
    metac[48:] = f("meta_tokens")
    return dict(pvec=pv, alog=f("ssm_A_log")[0][None, :].copy(), nfw=f("norm_final_w")[None, :].copy(), lowr=lowr,
                win_h=win_h, wout_h=wout_h, wgu_h=wgu_h, wdn_h=wdn_h, metac=metac)


def host_core_inputs(c, I, core):
    f = lambda k: np.asarray(I[k], np.float32)
    sl = slice(2 * core, 2 * core + 2)
    sh = f("state_rwkv_shift")[0][sl]
    st_shift = np.stack([_fm(sh[s], c.ACH) for s in range(2)], -1)
    wkv = f("state_rwkv_wkv")[0][sl]
    st_wkv = np.ascontiguousarray(wkv.reshape(2, c.NHP, 2, 64, 64).transpose(2, 4, 0, 1, 3).reshape(128, 2, c.NHP, 64))
    cv = f("state_ssm_conv")[0][sl]
    st_conv = np.ascontiguousarray(cv.reshape(2, 3, c.NXBC, 128).transpose(3, 2, 0, 1))
    sm = f("state_ssm")[0][sl]
    st_ssm = np.ascontiguousarray(sm.reshape(2, c.NB * 64, 128).transpose(2, 0, 1))
    return dict(xp=np.ascontiguousarray(f("x_prompt")[sl]), xs=np.ascontiguousarray(f("x_sample")[sl]),
                st_shift=np.ascontiguousarray(st_shift), st_wkv=st_wkv, st_conv=st_conv, st_ssm=st_ssm)


def host_unpack(c, r, pfx):
    sh = r[pfx + "_shift"]
    shift = np.zeros((2, c.ACOLS), np.float32)
    for ci, (c0, n) in enumerate(c.ACH):
        shift[:, c0:c0 + n] = sh[:n, ci, :].T
    wk = r[pfx + "_wkv"].reshape(2, 64, 2, c.NHP, 64)
    wkv = np.ascontiguousarray(wk.transpose(2, 3, 0, 4, 1)).reshape(2, c.NA, 64, 64)
    cv = r[pfx + "_conv"]
    conv = np.ascontiguousarray(cv.transpose(2, 3, 1, 0)).reshape(2, 3, c.CONVD)
    sm = r[pfx + "_ssm"]
    ssm = np.ascontiguousarray(sm.transpose(1, 2, 0)).reshape(2, c.NB, 64, 128)
    return shift, wkv, conv, ssm


_CACHE = {}


def run(c, I, runner=None):
    key = (c.D, c.SEQ, c.DFF, c.G)
    if key not in _CACHE:
        _CACHE[key] = build_program(c)[0]
    nc = _CACHE[key]
    W = host_weights(c, I)
    in_maps = []
    for core in range(c.NCORES):
        m = dict(W)
        m.update(host_core_inputs(c, I, core))
        in_maps.append(m)
    if runner is None:
        res = run_bass_kernel_spmd(nc, in_maps, core_ids=list(range(c.NCORES))).results
    else:
        res = runner(nc, in_maps)
    yp = np.concatenate([r["yp"] for r in res], 0)
    ys = np.concatenate([r["ys"] for r in res], 0)
    outs = [yp, ys]
    for pfx in ("p", "s"):
        parts = [host_unpack(c, r, pfx) for r in res]
        for j in range(4):
            outs.append(np.concatenate([p[j] for p in parts], 0)[None])
    return tuple(np.ascontiguousarray(o, dtype=np.float32) for o in outs)


def kernel(**inputs):
    return run(FULL, inputs)
```

```python
import contextlib
import math
import os
import numpy as np
import concourse.bass as bass
import concourse.mybir as mybir
from concourse.bass_utils import run_bass_kernel_spmd

F32 = mybir.dt.float32
BF16 = mybir.dt.bfloat16
AF = mybir.ActivationFunctionType
ALU = mybir.AluOpType
AX = mybir.AxisListType

import os
KDBG = os.environ.get('KDBG', '')
EPS = 1e-6
LNX_EPS = 64e-5
N_META = 16


class Buf:
    __slots__ = ("w", "r", "name")

    def __init__(self, name=""):
        self.w = None
        self.r = {}
        self.name = name


class Prog:
    ENGS = ("pe", "act", "dve", "pool", "sp")

    def __init__(self, nc, n_dma_sems=24):
        self.nc = nc
        self.stack = contextlib.ExitStack()
        self.sems = {}
        self.EPOCH = 24000
        self.nep = {"pe": 8, "act": 3, "dve": 3, "pool": 2}
        for e in ("pe", "act", "dve", "pool"):
            for ep in range(self.nep[e]):
                self.sems[(e, ep)] = self.stack.enter_context(nc.semaphore("s_%s%d" % (e, ep)))
        self.n_dma = n_dma_sems
        for i in range(n_dma_sems):
            self.sems[("d", i)] = self.stack.enter_context(nc.semaphore("s_d%d" % i))
        self.dma_cnt = [0] * n_dma_sems
        self.dma_rr = 0
        self.sw_rr = 0
        self.count = {e: 0 for e in self.ENGS}
        self.waited = {e: {} for e in self.ENGS}
        self.prog = {e: [] for e in self.ENGS}
        self.ninstr = 0

    def _need(self, eng, deps):
        for (k, v) in deps:
            if k == eng:
                if eng == "pe":
                    continue
                if eng != "pool" and self.count[eng] - v >= 2:
                    continue
            if self.waited[eng].get(k, 0) >= v:
                continue
            self.waited[eng][k] = v
            if isinstance(k, str):
                ep = (v - 1) // self.EPOCH
                sem = self.sems[(k, ep)]
                v = v - ep * self.EPOCH
            else:
                sem = self.sems[k]
            self.prog[eng].append(lambda e, sem=sem, v=v: e.wait_ge(sem, v))

    @staticmethod
    def _deps(reads, writes):
        deps = set()
        for b in reads:
            if b.w is not None:
                deps.add(b.w)
        for b in writes:
            if b.w is not None:
                deps.add(b.w)
            for k, v in b.r.items():
                deps.add((k, v))
        return deps

    @staticmethod
    def _commit(token, reads, writes):
        k, v = token
        for b in reads:
            if b.r.get(k, 0) < v:
                b.r[k] = v
        for b in writes:
            b.w = token
            b.r = {}

    def op(self, eng, fn, reads=(), writes=()):
        self._need(eng, self._deps(reads, writes))
        self.count[eng] += 1
        self.ninstr += 1
        token = (eng, self.count[eng])
        sem = self.sems[(eng, (self.count[eng] - 1) // self.EPOCH)]
        self.prog[eng].append(lambda e, fn=fn, sem=sem: fn(e).then_inc(sem, 1))
        self._commit(token, reads, writes)
        return token

    def dma(self, eng, out, in_, reads=(), writes=()):
        deps = self._deps(reads, writes)
        nsw = 6
        if eng == "pool":
            i = self.sw_rr
            self.sw_rr = (self.sw_rr + 1) % nsw
        else:
            i = nsw + self.dma_rr
            self.dma_rr = (self.dma_rr + 1) % (self.n_dma - nsw)
        k = ("d", i)
        if self.dma_cnt[i] > 0:
            deps.add((k, 16 * self.dma_cnt[i]))
        self._need(eng, deps)
        self.dma_cnt[i] += 1
        self.ninstr += 1
        token = (k, 16 * self.dma_cnt[i])
        sem = self.sems[k]
        self.prog[eng].append(lambda e, out=out, in_=in_, sem=sem: e.dma_start(out=out, in_=in_).then_inc(sem, 16))
        self._commit(token, reads, writes)
        return token

    def barrier(self):
        engs = ("pe", "act", "dve", "pool")
        for e in engs:
            self._need(e, {(k, self.count[k]) for k in engs if k != e and self.count[k] > 0})

    def finish(self):
        deps = set()
        for i in range(self.n_dma):
            if self.dma_cnt[i] > 0:
                deps.add((("d", i), 16 * self.dma_cnt[i]))
        kw = os.environ.get("KWAIT", "pe,act,dve,pool").split(",")
        for k in ("pe", "act", "dve", "pool"):
            if k in kw and self.count[k] > 0:
                deps.add((k, self.count[k]))
        self._need("sp", deps)
        nc = self.nc
        prog = self.prog
        with nc.Block() as block:
            @block.tensor
            def _(e):
                for f in prog["pe"]:
                    f(e)

            @block.scalar
            def _(e):
                for f in prog["act"]:
                    f(e)

            @block.vector
            def _(e):
                for f in prog["dve"]:
                    f(e)

            @block.gpsimd
            def _(e):
                for f in prog["pool"]:
                    f(e)

            @block.sync
            def _(e):
                for f in prog["sp"]:
                    f(e)
        self.stack.close()


class Cfg:
    def __init__(self, D=2048, SEQ=2048, DFF=5632, G=3, NCORES=8, BATCH=16, DEC_BATCH=16):
        self.D, self.SEQ, self.DFF, self.G, self.NCORES = D, SEQ, DFF, G, NCORES
        self.BATCH, self.DEC_BATCH = BATCH, DEC_BATCH
        self.AW = D // 2
        self.NA = self.AW // 64
        self.NHP = self.NA // 2
        self.BW = D - self.AW
        self.NB = self.BW // 64
        self.HG = self.NB // 2
        self.NBC = self.BW // 128
        self.CONVD = self.BW + 512
        self.ACOLS = 3 * self.AW + 288
        self.BCOLS = self.BW + self.CONVD + self.NB
        self.INCOLS = self.ACOLS + self.BCOLS
        self.KC = D // 128
        self.NF = DFF // 128
        self.NW = min(512, D)
        self.NNG = D // self.NW
        self.KPG_OUT = min(4, self.KC)
        self.KPG_DN = 4 if self.NF % 4 == 0 else 2
        assert self.KC % self.KPG_OUT == 0 and self.NF % self.KPG_DN == 0
        assert (SEQ + 64) % 64 == 0
        self.NSTEP_P = (SEQ + 64) // 64
        assert self.NSTEP_P % G == 0
        nh = self.AW // 128
        A = []
        for part in range(3):
            for c in range(nh):
                A.append((part * self.AW + c * 128, 128))
        A.append((3 * self.AW, 128))
        A.append((3 * self.AW + 128, 128))
        A.append((3 * self.AW + 256, 32))
        self.ACH = A
        self.NCHA = len(A)
        self.cR, self.cK, self.cV = 0, nh, 2 * nh
        self.cWA, self.cG0, self.cG1 = 3 * nh, 3 * nh + 1, 3 * nh + 2
        B = []
        o = self.ACOLS
        for c in range(self.NBC):
            B.append((o + c * 128, 128))
        o += self.BW
        for c in range(self.NBC + 4):
            B.append((o + c * 128, 128))
        o += self.CONVD
        B.append((o, self.NB))
        self.BCH = B
        self.NCHB = len(B)
        self.cZ, self.cX = 0, self.NBC
        self.cBm, self.cCm, self.cDT = 2 * self.NBC, 2 * self.NBC + 2, 2 * self.NBC + 4
        self.NXBC = self.NBC + 4
        self.NCHIN = self.NCHA + self.NCHB


FULL = Cfg(G=1)

def pv_layout(c):
    off = {}
    n = 0
    def add(name, w):
        nonlocal n
        off[name] = (n, w)
        n += w
    add("mu", c.NCHA)
    for nm in ("w0", "a0", "kk", "ka", "rk", "lnw", "lnb"):
        add(nm, c.AW // 128)
    add("convw", c.NXBC * 4)
    add("convb", c.NXBC)
    add("snw", c.NBC)
    add("dskip", c.NBC)
    add("dtb", 1)
    add("nmix", c.KC)
    add("nffn", c.KC)
    return off, n


def build_program(c):
    nc = bass.Bass("TRN2", target_bir_lowering=False)
    D, G, KC, NHP, NA, NB, NBC, NF, NW = c.D, c.G, c.KC, c.NHP, c.NA, c.NB, c.NBC, c.NF, c.NW
    T = 128 * G
    NCHA, NCHB, NXBC = c.NCHA, c.NCHB, c.NXBC
    pvo, NPV = pv_layout(c)

    def din(name, shape, dt=F32):
        return nc.dram_tensor(name, list(shape), dt, kind="ExternalInput").ap()

    def dout(name, shape, dt=F32):
        return nc.dram_tensor(name, list(shape), dt, kind="ExternalOutput").ap()

    def dscr(name, shape, dt=BF16):
        return nc.dram_tensor(name, list(shape), dt, kind="Internal").ap()

    xp = din("xp", [2, c.SEQ, D])
    xs = din("xs", [2, 64, D])
    metac = din("metac", [64, D])
    st_shift = din("st_shift", [128, NCHA, 2])
    st_wkv = din("st_wkv", [128, 2, NHP, 64])
    st_conv = din("st_conv", [128, NXBC, 2, 3])
    st_ssm = din("st_ssm", [128, 2, NB * 64])
    pvec = din("pvec", [128, NPV])
    alog = din("alog", [1, NB])
    nfw = din("nfw", [1, D])
    lowr = din("lowr", [128, 4, c.AW])
    win_h = din("win_h", [c.NCHIN, 128, KC, 128])
    wout_h = din("wout_h", [c.NNG, KC // c.KPG_OUT, 128, c.KPG_OUT, NW])
    wgu_h = din("wgu_h", [NF, 128, 2, KC, 128])
    wdn_h = din("wdn_h", [c.NNG, NF // c.KPG_DN, 128, c.KPG_DN, NW])

    yp = dout("yp", [2, c.SEQ, D])
    ys = dout("ys", [2, 64, D])
    o_shift = [dout("p_shift", [128, NCHA, 2]), dout("s_shift", [128, NCHA, 2])]
    o_wkv = [dout("p_wkv", [128, 2, NHP, 64]), dout("s_wkv", [128, 2, NHP, 64])]
    o_conv = [dout("p_conv", [128, NXBC, 2, 3]), dout("s_conv", [128, NXBC, 2, 3])]
    o_ssm = [dout("p_ssm", [128, 2, NB * 64]), dout("s_ssm", [128, 2, NB * 64])]

    NTILES = c.NSTEP_P // G + 1
    x1s = dscr("x1s", [NTILES, G, 128, D], F32)
    x1b = [Buf("x1s%d" % i) for i in range(NTILES)]
    win_s = dscr("win_s", [c.NCHIN, 128, KC, 128])
    wout_s = dscr("wout_s", [c.NNG, KC // c.KPG_OUT, 128, c.KPG_OUT, NW])
    wgu_s = dscr("wgu_s", [NF, 128, 2, KC, 128])
    wdn_s = dscr("wdn_s", [c.NNG, NF // c.KPG_DN, 128, c.KPG_DN, NW])

    P = Prog(nc)
    S = P.stack

    def sb(name, shape, dt=F32):
        return S.enter_context(nc.sbuf_tensor(name, list(shape), dt))

    NH8 = c.AW // 128
    NBH = 2 if NH8 >= 2 else 1
    HB = NA // NBH
    NHH = NH8 // NBH
    NPAR = 2 if getattr(c, "pipe", True) else 1
    u_off = [0]

    class _USpec:
        pass

    def cu(shape, dt=F32):
        n = 1
        for x in shape[1:]:
            n *= x
        sp = _USpec()
        sp.off, sp.nf, sp.shape, sp.dt, sp.n = u_off[0], ((n + 1) // 2 if dt == BF16 else n), list(shape), dt, n
        u_off[0] += sp.nf
        return sp
    xt_all = [cu([128, G, D]) for i in range(NPAR)]
    hT_sp = cu([128, KC, T], BF16)
    ZCH = max(NCHA, NCHB)
    zS_sp = cu([128, NCHB * T])
    zR_sp = cu([128, NCHA * T])
    yT_all = [cu([128, KC, T], BF16) for i in range(NPAR)]
    WSLOT = max(2 * KC * 128, c.KPG_OUT * NW, c.KPG_DN * NW)
    NWS = getattr(c, "nws", 4)
    wslot = [sb("wslot%d" % i, [128, WSLOT], BF16) for i in range(NWS)]
    wslot_b = [Buf() for _ in range(NWS)]
    wrr = [0]
    nfw_bc = sb("nfw_bc", [128, D])
    pv = sb("pv", [128, NPV])
    lowr_bf = sb("lowr_bf", [128, 4, c.AW], BF16)
    ident_f = sb("ident_f", [128, 128])
    ident_b = sb("ident_b", [128, 128], BF16)
    m_iu = sb("m_iu", [128, 128])
    m_sl = sb("m_sl", [128, 128])
    m_xr = sb("m_xr", [128, 256])
    imx = sb("imx", [128, 128], BF16)
    blk = sb("blk", [128, 128], BF16)
    ones_b = sb("ones_b", [128, 128], BF16)
    ones_f = sb("ones_f", [128, 128])
    rmask = sb("rmask", [128, 128])
    a_bc = sb("a_bc", [128, NB])
    onemka = sb("onemka", [128, NH8])
    epsc = sb("epsc", [128, 4])
    pmask = sb("pmask", [128, 2])
    seqsel = sb("seqsel", [128, 2, 128])
    carry = sb("carry", [128, NCHA, 2])
    Hst = sb("Hst", [128, 2, NHP, 64])
    Hbf = sb("Hbf", [128, 2, NHP, 64], BF16)
    Sst = sb("Sst", [128, 2, NB * 64])
    Sbf = sb("Sbf", [128, 2, NB * 64], BF16)
    convst = sb("convst", [128, NXBC, 2, 3])
    wa_bf = sb("wa_bf", [128, T], BF16)
    sg_bf = sb("sg_bf", [128, 2, T], BF16)
    dt_f = sb("dt_f", [128, T])
    big1_sp = cu([128, max(NH8 * T, D // 2, T)])
    big2_sp = cu([128, D // 2])
    arena_elems = [0, 0]

    class _Spec:
        pass
    specs = []

    def cv(which, name, shape, dt=F32):
        n = 1
        for x in shape[1:]:
            n *= x
        nf = (n + 1) // 2 if dt == BF16 else n
        sp = _Spec()
        sp.which, sp.off, sp.nf, sp.shape, sp.dt, sp.n = which, arena_elems[which], nf, list(shape), dt, n
        arena_elems[which] += nf
        specs.append(sp)
        return sp

    f_lw = cv(0, 'f_lw', [128, NHH, 128])
    f_cs = cv(0, 'f_cs', [128, NHH, 128])
    f_a = cv(0, 'f_a', [128, NHH, 128])
    f_ep = cv(0, 'f_ep', [128, NHH, 128])
    f_en = cv(0, 'f_en', [128, NHH, 128])
    f_t1 = cv(0, 'f_t1', [128, NHH, 128])
    f_g = cv(0, 'f_g', [128, NHH, 128])
    f_gw = cv(0, 'f_gw', [128, NHH, 128])
    f_gb = cv(0, 'f_gb', [128, NHH, 128])
    f_bq = cv(0, 'f_bq', [128, NHH, 128], BF16)
    ar_b = cv(0, 'ar_b', [128, NHH, 2, 128], BF16)
    bt_b = cv(0, 'bt_b', [128, NHH, 128], BF16)
    kt_b = cv(0, 'kt_b', [128, NHH, 128], BF16)
    v_b = cv(0, 'v_b', [128, NHH, 128], BF16)
    arm = cv(0, 'arm', [128, NHH, 4 * 128], BF16)
    btm = cv(0, 'btm', [128, NHH, 2 * 128], BF16)
    ktm = cv(0, 'ktm', [128, NHH, 2 * 128], BF16)
    Vts = cv(0, 'Vts', [128, 2, NHH * 128], BF16)
    Uss = cv(0, 'Uss', [128, 2, HB * 64], BF16)
    Vtok = cv(0, 'Vtok', [128, NHH * 128], BF16)
    Ktok = cv(0, 'Ktok', [128, NHH * 128], BF16)
    Btok = cv(0, 'Btok', [128, NHH * 128], BF16)
    XRB = cv(0, 'XRB', [128, HB, 2, 128], BF16)
    AKR = cv(0, 'AKR', [128, HB, 2, 128], BF16)
    Xp = [cv(0, 'Xp' + str(i), [128, HB, 128], BF16) for i in range(2)]
    Lp = [cv(0, 'Lp' + str(i), [128, HB, 128], BF16) for i in range(2)]
    Pp = [cv(0, 'Pp' + str(i), [128, HB, 128], BF16) for i in range(2)]
    Wsb = cv(0, 'Wsb', [128, HB, 64], BF16)
    Usb = cv(0, 'Usb', [128, HB, 64], BF16)
    Ysb = cv(0, 'Ysb', [128, HB, 64])
    Ysq = cv(0, 'Ysq', [128, HB, 64])
    gst = cv(0, 'gst', [128, 4, HB])
    Htmp = cv(0, 'Htmp', [128, NHH, 64])
    szl = cv(1, 'szl', [128, NBC, T], BF16)
    xpad = cv(1, 'xpad', [128, NXBC, 2 * (3 + 64 * G)])
    dt_tok = cv(1, 'dt_tok', [128, NB])
    dta = cv(1, 'dta', [128, NB])
    dtx = cv(1, 'dtx', [128, NB, 64])
    rhs1 = cv(1, 'rhs1', [128, max(NB * 128, NXBC * 128 * G)])
    decT = cv(1, 'decT', [128, max(NB * 128, NXBC * 128 * G)])
    MT = cv(1, 'MT', [128, NB, 128], BF16)
    cbm = cv(1, 'cbm', [128, 2, 128])
    xs_tok = cv(1, 'xs_tok', [128, NB * 64])
    xdt = cv(1, 'xdt', [128, NB * 64], BF16)
    xdt2 = cv(1, 'xdt2', [128, 2, NB * 64], BF16)
    Btk = cv(1, 'Btk', [128, 2, 128], BF16)
    bc_b = cv(1, 'bc_b', [128, 4, 128], BF16)
    eaB = cv(1, 'eaB', [128, NBC, 128])
    yb = cv(1, 'yb', [128, NBC, 128])
    ytmp = cv(1, 'ytmp', [128, NBC, 128])
    ysq = cv(1, 'ysq', [128, NBC, 128], BF16)
    rs2 = cv(1, 'rs2', [128, 2, 128])
    eaL = cv(1, 'eaL', [128, NB])
    small = cv(1, 'small', [128, 8])
    arena_sp = cu([128, max(arena_elems)])
    GF = getattr(c, "gf", 4)
    TF = 128 * GF
    p1_elems = u_off[0]
    u_off[0] = 0
    xtF = [cu([128, GF, D]) for i in range(2)]
    hTF = cu([128, KC, TF], BF16)
    actTF = cu([128, NF, TF], BF16)
    svF = cu([128, TF])
    xnF = cu([128, D], BF16)
    p2_elems = u_off[0]
    U = sb("U", [128, max(p1_elems, p2_elems)])

    def mkU(sp):
        v = U[:, sp.off:sp.off + sp.nf]
        if sp.dt == BF16:
            v = v.bitcast(BF16)[:, 0:sp.n]
        if len(sp.shape) == 3:
            v = v.rearrange("p (a b) -> p a b", b=sp.shape[2])
        return v
    xt_all = [mkU(x) for x in xt_all]
    hT = mkU(hT_sp)
    zS_raw, zR_raw = mkU(zS_sp), mkU(zR_sp)
    zS = zS_raw.rearrange("p (c t) -> p c t", t=T)
    zR = zR_raw.rearrange("p (c t) -> p c t", t=T)
    yT_all = [mkU(x) for x in yT_all]
    big1, big2 = mkU(big1_sp), mkU(big2_sp)
    arena = mkU(arena_sp)
    xtF = [mkU(x) for x in xtF]
    hTF, actTF, svF, xnF = mkU(hTF), mkU(actTF), mkU(svF), mkU(xnF)
    sqF = xnF

    def _mk(sp):
        v = arena[:, sp.off:sp.off + sp.nf]
        if sp.dt == BF16:
            v = v.bitcast(BF16)[:, 0:sp.n]
        if len(sp.shape) == 3:
            v = v.rearrange("p (a b) -> p a b", b=sp.shape[2])
        elif len(sp.shape) == 4:
            v = v.rearrange("p (a b c) -> p a b c", b=sp.shape[2], c=sp.shape[3])
        return v

    f_lw = _mk(f_lw)
    f_cs = _mk(f_cs)
    f_a = _mk(f_a)
    f_ep = _mk(f_ep)
    f_en = _mk(f_en)
    f_t1 = _mk(f_t1)
    f_g = _mk(f_g)
    f_gw = _mk(f_gw)
    f_gb = _mk(f_gb)
    f_bq = _mk(f_bq)
    ar_b = _mk(ar_b)
    bt_b = _mk(bt_b)
    kt_b = _mk(kt_b)
    v_b = _mk(v_b)
    arm = _mk(arm).rearrange('p c (h a t) -> p c h a t', h=2, a=2)
    btm = _mk(btm).rearrange('p c (h t) -> p c h t', h=2)
    ktm = _mk(ktm).rearrange('p c (h t) -> p c h t', h=2)
    Vts = _mk(Vts)
    Uss = _mk(Uss)
    Vtok = _mk(Vtok)
    Ktok = _mk(Ktok)
    Btok = _mk(Btok)
    XRB = _mk(XRB)
    AKR = _mk(AKR)
    Xp = [_mk(x) for x in Xp]
    Lp = [_mk(x) for x in Lp]
    Pp = [_mk(x) for x in Pp]
    Wsb = _mk(Wsb)
    Usb = _mk(Usb)
    Ysb = _mk(Ysb)
    Ysq = _mk(Ysq)
    gst = _mk(gst)
    Htmp = _mk(Htmp)
    szl = _mk(szl)
    xpad = _mk(xpad).rearrange('p c (s l) -> p c s l', s=2)
    dt_tok = _mk(dt_tok)
    dta = _mk(dta)
    dtx = _mk(dtx)
    caccA = _mk(rhs1)
    ctmpA = _mk(decT)
    rhs1 = caccA[:, 0:NB * 128].rearrange('p (h q) -> p h q', q=128)
    decT = ctmpA[:, 0:NB * 128].rearrange('p (h q) -> p h q', q=128)
    MT = _mk(MT)
    cbm = _mk(cbm)
    xs_tok = _mk(xs_tok)
    xdt = _mk(xdt)
    xdt2 = _mk(xdt2)
    Btk = _mk(Btk)
    bc_b = _mk(bc_b)
    eaB = _mk(eaB)
    yb = _mk(yb)
    ytmp = _mk(ytmp)
    ysq = _mk(ysq)
    rs2 = _mk(rs2)
    eaL = _mk(eaL)
    small = _mk(small)
    f_t3 = f_lw
    f_t2 = f_cs
    small_all = [sb('small%d' % i, [128, 8]) for i in range(NPAR)]

    NPS = 8
    ps = [S.enter_context(nc.psum_tensor("ps%d" % i, [128, 512], F32)) for i in range(NPS)]
    ps_b = [Buf() for _ in range(NPS)]
    prr = [0]

    def bank():
        i = prr[0]
        prr[0] = (i + 1) % NPS
        return ps[i], ps_b[i]

    B = {}

    def tk(name):
        if name not in B:
            B[name] = Buf(name)
        return B[name]

    extra_reads = []
    AR = Buf("arena_phase")

    def dve(fn, r=(), w=()):
        return P.op("dve", fn, list(r) + extra_reads, w)

    def act(fn, r=(), w=()):
        return P.op("act", fn, list(r) + extra_reads, w)

    def pool(fn, r=(), w=()):
        return P.op("pool", fn, list(r) + extra_reads, w)

    def pe(fn, r=(), w=()):
        return P.op("pe", fn, list(r) + extra_reads, w)

    def fence():
        P.op("dve", lambda e: e.memset(epsc[:, 3:4], 0.0), [], [AR])

    def pvc(name, i=0, n=1):
        o, w = pvo[name]
        return pv[:, o + i:o + i + n]

    def bc3(ap2, n):
        return ap2.unsqueeze(2).to_broadcast([128, ap2.shape[1], n])

    def TT(eng, out, in0, in1, op, r, w):
        return eng(lambda e: e.tensor_tensor(out=out, in0=in0, in1=in1, op=op), r, w)

    cst = tk("const")
    pvb = tk("pv")
    EPS_I, LNX_I, ONE_I = 0, 1, 2

    pending = []

    def conv_w(dst, src, nsplit, name, defer=False):
        n0 = dst.shape[0]
        per = max(1, (n0 + nsplit - 1) // nsplit)
        toks = []
        for i in range(0, n0, per):
            b = Buf(name)
            job = (dst[i:min(n0, i + per)], src[i:min(n0, i + per)], b)
            if defer:
                pending.append(job)
            else:
                P.dma("pool", job[0], job[1], writes=[b])
            toks.append((i, min(n0, i + per), b))
        return toks

    def issue_pending(n):
        for _ in range(n):
            if pending:
                d_, s_, b_ = pending.pop(0)
                P.dma("pool", d_, s_, writes=[b_])

    def find(toks, i):
        for a, b, t in toks:
            if a <= i < b:
                return t
        raise KeyError

    P.dma("sp", pv[:], pvec, writes=[pvb])
    lowr_s = dscr("lowr_s", [128, 4, c.AW])
    P.dma("pool", lowr_s, lowr, writes=[tk("lowr_s")])
    P.dma("sp", lowr_bf[:], lowr_s, reads=[tk("lowr_s")], writes=[tk("lowr")])
    P.dma("sp", nfw_bc[:], nfw.partition_broadcast(128), writes=[tk("nfw")])
    P.dma("sp", a_bc[:], alog.partition_broadcast(128), writes=[tk("abc")])
    pool(lambda e: e.memset(ident_f[:], 1.0), w=[cst])
    pool(lambda e: e.affine_select(out=ident_f[:], in_=ident_f[:], pattern=[[-1, 128]], compare_op=ALU.is_equal,
                                   fill=0.0, base=0, channel_multiplier=1), r=[cst], w=[cst])
    pool(lambda e: e.tensor_copy(out=ident_b[:], in_=ident_f[:]), r=[cst], w=[cst])
    pool(lambda e: e.tensor_copy(out=imx[:], in_=ident_f[:]), r=[cst], w=[cst])
    pool(lambda e: e.memset(ones_f[:], 1.0), w=[cst])
    pool(lambda e: e.memset(ones_b[:], 1.0), w=[cst])
    pool(lambda e: e.memset(blk[:], 0.0), w=[cst])
    pool(lambda e: e.memset(blk[0:64, 0:64], 1.0), w=[cst])
    pool(lambda e: e.memset(blk[64:128, 64:128], 1.0), w=[cst])

    def blockmask(m, base_mult, pat, op):
        pool(lambda e: e.memset(m, 0.0), w=[cst])
        pool(lambda e: e.memset(m[0:64, 0:64], 1.0), w=[cst])
        pool(lambda e: e.memset(m[64:128, 64:128], 1.0), w=[cst])
        pool(lambda e: e.affine_select(out=m, in_=m, pattern=[[pat, 128]], compare_op=op, fill=0.0, base=0,
                                       channel_multiplier=base_mult), r=[cst], w=[cst])
    blockmask(m_xr[:, 0:128], -1, 1, ALU.is_gt)
    blockmask(m_xr[:, 128:256], -1, 1, ALU.is_ge)
    blockmask(m_iu[:], -1, 1, ALU.is_ge)
    blockmask(m_sl[:], 1, -1, ALU.is_gt)
    pool(lambda e: e.memset(pmask[:], 0.0), w=[cst])
    pool(lambda e: e.memset(pmask[0:64, 0:1], 1.0), w=[cst])
    pool(lambda e: e.memset(pmask[64:128, 1:2], 1.0), w=[cst])
    pool(lambda e: e.memset(seqsel[:], 0.0), w=[cst])
    pool(lambda e: e.memset(seqsel[0:64, 0, :], 1.0), w=[cst])
    pool(lambda e: e.memset(seqsel[64:128, 1, :], 1.0), w=[cst])
    pool(lambda e: e.memset(rmask[:], 1.0), w=[cst])
    pool(lambda e: e.memset(rmask[:, 0:1], 0.0), w=[cst])
    pool(lambda e: e.memset(rmask[:, 64:65], 0.0), w=[cst])
    pool(lambda e: e.memset(epsc[:, 0:1], EPS), w=[cst])
    pool(lambda e: e.memset(epsc[:, 1:2], LNX_EPS), w=[cst])
    pool(lambda e: e.memset(epsc[:, 2:3], 1.0), w=[cst])
    pool(lambda e: e.memset(epsc[:, 3:4], 0.0), w=[cst])
    act(lambda e: e.activation(out=a_bc[:], in_=a_bc[:], func=AF.Exp), r=[tk("abc")], w=[tk("abc")])
    dve(lambda e: e.tensor_scalar(out=a_bc[:], in0=a_bc[:], scalar1=-1.0, scalar2=None, op0=ALU.mult),
        r=[tk("abc")], w=[tk("abc")])
    dve(lambda e: e.tensor_scalar(out=onemka[:], in0=pvc("ka", 0, NH8), scalar1=-1.0, scalar2=1.0, op0=ALU.mult,
                                  op1=ALU.add), r=[pvb], w=[cst])

    t_win = conv_w(win_s, win_h, 12, "win")
    fl = "a b p k n -> (a b) p k n"
    t_wout = conv_w(wout_s.rearrange(fl), wout_h.rearrange(fl), 4, "wout")
    t_wgu = conv_w(wgu_s, wgu_h, 11, "wgu", defer=True)
    t_wdn = conv_w(wdn_s.rearrange(fl), wdn_h.rearrange(fl), 8, "wdn", defer=True)


    HSL = WSLOT // 2
    half_b = [Buf("wh%d" % i) for i in range(2 * NWS)]
    hrr = [0]

    def load_w(src_ap, n, src_tok, half=False):
        if half and n <= HSL:
            j = hrr[0]
            hrr[0] = (j + 1) % (2 * NWS)
            view = wslot[j // 2][:, (j % 2) * HSL:(j % 2) * HSL + n]
            P.dma("sp", view, src_ap, reads=[src_tok], writes=[half_b[j]])
            return view, half_b[j]
        i = wrr[0]
        wrr[0] = (i + 1) % NWS
        view = wslot[i][:, 0:n]
        P.dma("sp", view, src_ap, reads=[src_tok], writes=[wslot_b[i], half_b[2 * i], half_b[2 * i + 1]])
        return view, wslot_b[i]

    def rstd(ss_ap, out_ap, scale, eps_i, rb, wb):
        act(lambda e: e.activation(out=out_ap, in_=ss_ap, func=AF.Ln, bias=epsc[0:ss_ap.shape[0], eps_i:eps_i + 1], scale=scale),
            r=list(rb) + [cst], w=wb)
        act(lambda e: e.activation(out=out_ap, in_=out_ap, func=AF.Exp, scale=-0.5), r=wb, w=wb)

    class _Stop(Exception):
        pass
    stop = getattr(c, "stop", 99)

    def chk(k):
        if stop == k:
            raise _Stop()

    def make_phases(par):
        xt, yT = xt_all[par], yT_all[par]
        small = small_all[0]
        xb, hb, ytb = tk("xt%d" % par), tk("hT"), tk("yT%d" % par)
        zsb, zrb = tk("zS"), tk("zR")
        b1, b2 = tk("big1"), tk("big2")
        sfx = ""

        def norm_to_hT(g, normname):
            sqv = big1[:].bitcast(BF16)[:, 0:D]
            act(lambda e: e.activation(out=sqv, in_=xt[:, g, :], func=AF.Square, accum_out=small[:, 0:1]),
                r=[xb], w=[b1, tk("ss" + sfx)])
            rstd(small[:, 0:1], small[:, 1:2], 1.0 / D, EPS_I, [tk("ss" + sfx)], [tk("rstd" + sfx)])
            xnv = big2[:].bitcast(BF16)[:, 0:D]
            act(lambda e: e.activation(out=xnv, in_=xt[:, g, :], func=AF.Copy, scale=small[:, 1:2]),
                r=[xb, tk("rstd" + sfx)], w=[b2])
            for k0 in range(0, KC, 4):
                kn = min(4, KC - k0)
                pt, pb = bank()
                ptb = pt[:].bitcast(BF16)
                for k in range(kn):
                    pe(lambda e, k=k, k0=k0, ptb=ptb: e.transpose(ptb[:, k * 128:(k + 1) * 128],
                                                                  xnv[:, (k0 + k) * 128:(k0 + k + 1) * 128], ident_b[:]),
                       r=[b2, cst], w=[pb])
                TT(dve, hT[:, k0:k0 + kn, g * 128:(g + 1) * 128], ptb[:, 0:kn * 128].rearrange("p (k t) -> p k t", t=128),
                   bc3(pvc(normname, k0, kn), 128), ALU.mult, [pb, pvb], [hb])

        def in_proj(chunks, cc0, Tt, zT, zb):
            for ci, (col0, ncol) in enumerate(chunks):
                wv, wb = load_w(win_s[cc0 + ci].rearrange("p k n -> p (k n)"), KC * 128, find(t_win, cc0 + ci), half=True)
                wv3 = wv.rearrange("p (k n) -> p k n", n=128)
                pt, pb = bank()
                for k in range(KC):
                    pe(lambda e, k=k, ncol=ncol, pt=pt, wv3=wv3: e.matmul(pt[0:ncol, 0:Tt], lhsT=wv3[:, k, 0:ncol], rhs=hT[:, k, 0:Tt],
                                                                          start=(k == 0), stop=(k == KC - 1)), r=[wb, hb], w=[pb])
                act(lambda e, ci=ci, ncol=ncol, pt=pt: e.copy(out=zT[0:ncol, ci, 0:Tt], in_=pt[0:ncol, 0:Tt]), r=[pb], w=[zb])
                if ci % 2 == 1:
                    yield

        def dense_tok(nG, w_s, w_tok, nkg, kpg, lhs_of, evac):
            for ng in range(c.NNG):
                banks = [bank() for _ in range(nG)]
                for kg in range(nkg):
                    wv, wb = load_w(w_s[ng, kg].rearrange("p k n -> p (k n)"), kpg * NW, find(w_tok, ng * nkg + kg), half=True)
                    wv3 = wv.rearrange("p (k n) -> p k n", n=NW)
                    for g in range(nG):
                        pt, pb = banks[g]
                        for kk in range(kpg):
                            kabs = kg * kpg + kk
                            lh, lhb = lhs_of(kabs, g)
                            pe(lambda e, pt=pt, lh=lh, wv3=wv3, kk=kk, kabs=kabs: e.matmul(
                                pt[:, 0:NW], lhsT=lh, rhs=wv3[:, kk, :], start=(kabs == 0), stop=(kabs == nkg * kpg - 1)),
                               r=[wb, lhb], w=[pb])
                for g in range(nG):
                    evac(g, ng, banks[g][0], banks[g][1])
                yield

        cX, cZ, cBm, cCm, cDT = c.cX, c.cZ, c.cBm, c.cCm, c.cDT
        HG = c.HG

        def ssd_prep(nG, first_prompt):
            Tt = 128 * nG
            L = 64 * nG
            cvb, xpb = tk("convst"), tk("xpad")
            cacc = caccA[:, 0:NXBC * 2 * L].rearrange("p (c s l) -> p c s l", s=2, l=L)
            ctmp = ctmpA[:, 0:NXBC * 2 * L].rearrange("p (c s l) -> p c s l", s=2, l=L)
            dve(lambda e: e.tensor_copy(out=xpad[:, :, :, 0:3], in_=convst[:]), r=[cvb], w=[xpb])
            for s in range(2):
                eng = dve if s == 0 else pool
                eng(lambda e, s=s: e.tensor_copy(
                    out=xpad[:, :, s, 3:3 + L].rearrange("p c (g t) -> p c g t", t=64),
                    in_=zS[:, cX:cX + NXBC, 0:Tt].rearrange("p c (g s t) -> p c g s t", s=2, t=64)[:, :, :, s, :]),
                    r=[zsb], w=[xpb])
            dve(lambda e: e.tensor_copy(out=convst[:], in_=xpad[:, :, :, L:L + 3]), r=[xpb], w=[cvb])
            cwo = pvo["convw"][0]
            cwv = pv[:, cwo:cwo + NXBC * 4].rearrange("p (c k) -> p c k", k=4)

            def cw_b(k):
                return cwv[:, :, k:k + 1].unsqueeze(3).to_broadcast([128, NXBC, 2, L])
            TT(dve, cacc, xpad[:, :, :, 0:L], cw_b(0), ALU.mult, [xpb, pvb], [tk("rhs1")])
            for k in range(1, 4):
                TT(pool, ctmp, xpad[:, :, :, k:k + L], cw_b(k), ALU.mult, [xpb, pvb], [tk("decT")])
                TT(dve, cacc, cacc, ctmp, ALU.add, [tk("rhs1"), tk("decT")], [tk("rhs1")])
            for ci in range(NXBC):
                act(lambda e, ci=ci: e.activation(
                    out=zS[:, cX + ci, 0:Tt].rearrange("p (g s t) -> p s g t", s=2, t=64),
                    in_=cacc[:, ci, :, :].rearrange("p s (g t) -> p s g t", t=64),
                    func=AF.Silu, bias=pvc("convb", ci, 1)), r=[tk("rhs1"), pvb], w=[zsb])
            for ci in range(NBC):
                act(lambda e, ci=ci: e.activation(out=szl[:, ci, 0:Tt], in_=zS[:, cZ + ci, 0:Tt], func=AF.Silu), r=[zsb], w=[tk("szl")])
            dtb = tk("dt_f")
            act(lambda e: e.activation(out=dt_f[0:NB, 0:Tt], in_=zS[0:NB, cDT, 0:Tt], func=AF.Exp, bias=pvc("dtb")[0:NB, :]),
                r=[zsb, pvb], w=[dtb])
            act(lambda e: e.activation(out=dt_f[0:NB, 0:Tt], in_=dt_f[0:NB, 0:Tt], func=AF.Ln, bias=epsc[0:NB, ONE_I:ONE_I + 1]),
                r=[dtb, cst], w=[dtb])
            if first_prompt:
                dve(lambda e: e.memset(dt_f[0:NB, 0:128].rearrange("p (s t) -> p s t", t=64)[:, :, 0:48], 0.0), r=[dtb], w=[dtb])

        def ssd_step(g):
            tsl = slice(g * 128, (g + 1) * 128)
            dtb, szb = tk("dt_f"), tk("szl")
            sstb, sbfb = tk("Sst"), tk("Sbf")
            pt, pb = bank()
            pe(lambda e, pt=pt: e.transpose(pt[:, 0:NB], dt_f[0:NB, tsl], ident_f[0:NB, 0:NB]), r=[dtb, cst], w=[pb])
            dtk, dab = tk("dt_tok"), tk("dta")
            dve(lambda e, pt=pt: e.tensor_copy(out=dt_tok[:], in_=pt[:, 0:NB]), r=[pb], w=[dtk])
            TT(dve, dta[:], dt_tok[:], a_bc[:], ALU.mult, [dtk, tk("abc")], [dab])
            xtk = tk("xs_tok")
            for c0 in range(0, NBC, 4):
                cn = min(4, NBC - c0)
                pt, pb = bank()
                for k in range(cn):
                    pe(lambda e, k=k, c0=c0, pt=pt: e.transpose(pt[:, k * 128:(k + 1) * 128], zS[:, cX + c0 + k, tsl], ident_f[:]),
                       r=[zsb, cst], w=[pb])
                act(lambda e, c0=c0, cn=cn, pt=pt: e.copy(out=xs_tok[:, c0 * 128:(c0 + cn) * 128], in_=pt[:, 0:cn * 128]), r=[pb], w=[xtk])
            bcb = tk("bc_b")
            act(lambda e: e.copy(out=bc_b[:], in_=zS[:, cBm:cBm + 4, tsl]), r=[zsb], w=[bcb])
            pt, pb = bank()
            ptb = pt[:].bitcast(BF16)
            for gq in range(2):
                pe(lambda e, gq=gq, ptb=ptb: e.transpose(ptb[:, gq * 128:(gq + 1) * 128], bc_b[:, gq, :], ident_b[:]), r=[bcb, cst], w=[pb])
            btb = tk("Btk")
            dve(lambda e, ptb=ptb: e.tensor_copy(out=Btk[:], in_=ptb[:, 0:256].rearrange("p (a n) -> p a n", n=128)), r=[pb], w=[btb])
            xdb = tk("xdt")
            TT(dve, xdt[:].rearrange("p (h q) -> p h q", q=64), xs_tok[:].rearrange("p (h q) -> p h q", q=64),
               bc3(dt_tok[:], 64), ALU.mult, [xtk, dtk], [xdb])
            yield
            r1b, dcb = tk("rhs1"), tk("decT")
            TT(pool, rhs1[:], bc3(dta[:], 128), m_iu[:].unsqueeze(1).to_broadcast([128, NB, 128]), ALU.mult, [dab, cst], [r1b])
            for h0 in range(0, NB, 4):
                hn = min(4, NB - h0)
                pt, pb = bank()
                pe(lambda e, h0=h0, hn=hn, pt=pt: e.matmul(pt[:, 0:hn * 128], lhsT=m_sl[:],
                                                           rhs=rhs1[:, h0:h0 + hn, :].rearrange("p h q -> p (h q)"),
                                                           start=True, stop=True), r=[cst, r1b], w=[pb])
                act(lambda e, h0=h0, hn=hn, pt=pt: e.activation(out=decT[:, h0:h0 + hn, :].rearrange("p h q -> p (h q)"),
                                                                in_=pt[:, 0:hn * 128], func=AF.Exp), r=[pb], w=[dcb])
            yield
            pt, pb = bank()
            for gq in range(2):
                pe(lambda e, gq=gq, pt=pt: e.matmul(pt[:, gq * 128:(gq + 1) * 128], lhsT=bc_b[:, gq, :], rhs=bc_b[:, 2 + gq, :],
                                                    start=True, stop=True), r=[bcb], w=[pb])
            cbb, mtb = tk("cbm"), tk("MT")
            TT(dve, cbm[:], pt[:, 0:256].rearrange("p (a q) -> p a q", q=128), m_iu[:].unsqueeze(1).to_broadcast([128, 2, 128]),
               ALU.mult, [pb, cst], [cbb])
            for gq in range(2):
                TT(dve if gq == 0 else pool, MT[:, gq * HG:(gq + 1) * HG, :], decT[:, gq * HG:(gq + 1) * HG, :],
                   cbm[:, gq:gq + 1, :].to_broadcast([128, HG, 128]), ALU.mult, [dcb, cbb], [mtb])
            x2b = tk("xdt2")
            pool(lambda e: e.memset(xdt2[:], 0.0), w=[x2b])
            for s in range(2):
                TT(dve, xdt2[s * 64:(s + 1) * 64, s, :].rearrange("p (h q) -> p h q", q=64),
                   xdt[s * 64:(s + 1) * 64, :].rearrange("p (h q) -> p h q", q=64),
                   decT[s * 64:(s + 1) * 64, :, s * 64 + 63:s * 64 + 64].to_broadcast([64, NB, 64]), ALU.mult, [xdb, dcb], [x2b])
            dxb = tk("dtx")
            act(lambda e: e.copy(out=dtx[:], in_=bc3(dta[:], 64)), r=[dab], w=[dxb])
            eBb, ybb = tk("eaB"), tk("yb")
            for c0 in range(0, NBC, 4):
                cn = min(4, NBC - c0)
                pt, pb = bank()
                pa, pab = bank()
                pq, pqb = bank()
                for k in range(cn):
                    cc = c0 + k
                    grp = (2 * cc) // HG
                    for hh in range(2):
                        h = 2 * cc + hh
                        pe(lambda e, k=k, h=h, hh=hh, pt=pt: e.matmul(pt[hh * 64:(hh + 1) * 64, k * 128:(k + 1) * 128],
                                                                      lhsT=xdt[:, h * 64:(h + 1) * 64], rhs=MT[:, h, :],
                                                                      start=True, stop=True), r=[xdb, mtb], w=[pb])
                    for s in range(2):
                        if HG >= 2:
                            pe(lambda e, k=k, cc=cc, grp=grp, s=s, pa=pa: e.matmul(
                                pa[:, k * 128 + s * 64:k * 128 + s * 64 + 64], lhsT=Sbf[:, s, cc * 128:(cc + 1) * 128],
                                rhs=bc_b[:, 2 + grp, s * 64:(s + 1) * 64], start=True, stop=True), r=[sbfb, bcb], w=[pab])
                        else:
                            for hh in range(2):
                                h = 2 * cc + hh
                                pe(lambda e, k=k, h=h, hh=hh, s=s, pa=pa: e.matmul(
                                    pa[hh * 64:(hh + 1) * 64, k * 128 + s * 64:k * 128 + s * 64 + 64], lhsT=Sbf[:, s, h * 64:(h + 1) * 64],
                                    rhs=bc_b[:, 2 + h // HG, s * 64:(s + 1) * 64], start=True, stop=True), r=[sbfb, bcb], w=[pab])
                    pe(lambda e, k=k, cc=cc, pq=pq: e.matmul(pq[:, k * 128:(k + 1) * 128],
                                                             lhsT=dtx[:, 2 * cc:2 * cc + 2, :].rearrange("p h q -> p (h q)"),
                                                             rhs=m_iu[:], start=True, stop=True), r=[dxb, cst], w=[pqb])
                fl2 = lambda t: t[:, c0:c0 + cn, :].rearrange("p c q -> p (c q)")
                act(lambda e, pq=pq, cn=cn, o=fl2(eaB): e.activation(out=o, in_=pq[:, 0:cn * 128], func=AF.Exp), r=[pqb], w=[eBb])
                TT(dve, fl2(yb), pa[:, 0:cn * 128], fl2(eaB), ALU.mult, [pab, eBb], [ybb])
                TT(dve, fl2(yb), fl2(yb), pt[:, 0:cn * 128], ALU.add, [ybb, pb], [ybb])
            ytb_ = tk("ytmp")
            TT(pool, ytmp[:], zS[:, cX:cX + NBC, tsl], bc3(pvc("dskip", 0, NBC), 128), ALU.mult, [zsb, pvb], [ytb_])
            TT(dve, yb[:], yb[:], ytmp[:], ALU.add, [ybb, ytb_], [ybb])
            TT(dve, yb[:], yb[:], szl[:, :, tsl], ALU.mult, [ybb, szb], [ybb])
            ysb_ = tk("ysq")
            TT(pool, ysq[:], yb[:], yb[:], ALU.mult, [ybb], [ysb_])
            pt, pb = bank()
            rsb = tk("rs2")
            if NBC >= 2:
                hc = NBC // 2
                for gq in range(2):
                    for k in range(hc):
                        pe(lambda e, gq=gq, k=k, pt=pt: e.matmul(pt[:, gq * 128:(gq + 1) * 128], lhsT=ones_b[:], rhs=ysq[:, gq * hc + k, :],
                                                                 start=(k == 0), stop=(k == hc - 1)), r=[ysb_, cst], w=[pb])
                rstd(pt[:, 0:256], rs2[:].rearrange("p a q -> p (a q)"), 2.0 / c.BW, EPS_I, [pb], [rsb])
                for gq in range(2):
                    TT(dve, yb[:, gq * hc:(gq + 1) * hc, :], yb[:, gq * hc:(gq + 1) * hc, :],
                       rs2[:, gq:gq + 1, :].to_broadcast([128, hc, 128]), ALU.mult, [ybb, rsb], [ybb])
            else:
                pe(lambda e, pt=pt: e.matmul(pt[:, 0:128], lhsT=blk[:], rhs=ysq[:, 0, :], start=True, stop=True), r=[ysb_, cst], w=[pb])
                rstd(pt[:, 0:128], rs2[:, 0, :], 2.0 / c.BW, EPS_I, [pb], [rsb])
                TT(dve, yb[:, 0, :], yb[:, 0, :], rs2[:, 0, :], ALU.mult, [ybb, rsb], [ybb])
            TT(dve, yT[:, NH8:NH8 + NBC, tsl], yb[:], bc3(pvc("snw", 0, NBC), 128), ALU.mult, [ybb, pvb], [ytb])
            yield
            for s in range(2):
                pt, pb = bank()
                pe(lambda e, s=s, pt=pt: e.matmul(pt[:, 0:NB], lhsT=seqsel[:, s, :], rhs=dta[:],
                                                  start=True, stop=True), r=[cst, dab], w=[pb])
                elb = tk("eaL")
                act(lambda e, pt=pt: e.activation(out=eaL[:], in_=pt[:, 0:NB], func=AF.Exp), r=[pb], w=[elb])
                TT(dve, Sst[:, s, :].rearrange("p (h q) -> p h q", q=64), Sst[:, s, :].rearrange("p (h q) -> p h q", q=64),
                   bc3(eaL[:], 64), ALU.mult, [sstb, elb], [sstb])
                gw_ = HG * 64
                for n0 in range(0, NB * 64, min(512, gw_)):
                    nn = min(512, gw_)
                    grp = n0 // gw_
                    pt, pb = bank()
                    pe(lambda e, s=s, n0=n0, nn=nn, grp=grp, pt=pt: e.matmul(pt[:, 0:nn], lhsT=Btk[:, grp, :],
                                                                             rhs=xdt2[:, s, n0:n0 + nn], start=True, stop=True),
                       r=[btb, x2b], w=[pb])
                    TT(dve, Sst[:, s, n0:n0 + nn], Sst[:, s, n0:n0 + nn], pt[:, 0:nn], ALU.add, [sstb, pb], [sstb])
                act(lambda e, s=s: e.copy(out=Sbf[:, s, :], in_=Sst[:, s, :]), r=[sstb], w=[sbfb])

        cR, cK, cV, cWA, cG0, cG1 = c.cR, c.cK, c.cV, c.cWA, c.cG0, c.cG1

        def rwkv_prep(nG):
            Tt = 128 * nG
            nb = 2 * nG
            crb = tk("carry")
            dsc = big1[:, 0:NH8 * Tt].rearrange("p (c t) -> p c t", t=Tt)
            for c0 in range(0, NCHA, NH8):
                n = min(NH8, NCHA - c0)
                z4 = zR[:, c0:c0 + n, 0:Tt].rearrange("p c (b t) -> p c b t", t=64)
                d4 = dsc[:, 0:n, :].rearrange("p c (b t) -> p c b t", t=64)
                TT(dve, d4[:, :, :, 1:64], z4[:, :, :, 0:63], z4[:, :, :, 1:64], ALU.subtract, [zrb], [b1])
                if nb > 2:
                    TT(dve, d4[:, :, 2:nb, 0:1], z4[:, :, 0:nb - 2, 63:64], z4[:, :, 2:nb, 0:1], ALU.subtract, [zrb], [b1])
                TT(dve, d4[:, :, 0:2, 0:1], carry[:, c0:c0 + n, :].unsqueeze(3), z4[:, :, 0:2, 0:1], ALU.subtract, [zrb, crb], [b1])
                dve(lambda e, c0=c0, n=n, z4=z4: e.tensor_copy(out=carry[:, c0:c0 + n, :].unsqueeze(3), in_=z4[:, :, nb - 2:nb, 63:64]),
                    r=[zrb], w=[crb])
                TT(pool, dsc[:, 0:n, :], dsc[:, 0:n, :], bc3(pvc("mu", c0, n), Tt), ALU.mult, [b1, pvb], [b1])
                TT(dve, zR[:, c0:c0 + n, 0:Tt], zR[:, c0:c0 + n, 0:Tt], dsc[:, 0:n, :], ALU.add, [zrb, b1], [zrb])
            chk(46)
            wab, sgb = tk("wa_bf"), tk("sg_bf")
            act(lambda e: e.activation(out=wa_bf[0:64, 0:Tt], in_=zR[0:64, cWA, 0:Tt], func=AF.Tanh), r=[zrb], w=[wab])
            chk(47)
            act(lambda e: e.copy(out=wa_bf[64:128, 0:Tt], in_=zR[64:128, cWA, 0:Tt]), r=[zrb], w=[wab])
            chk(48)
            act(lambda e: e.activation(out=sg_bf[:, 0, 0:Tt], in_=zR[:, cG0, 0:Tt], func=AF.Sigmoid), r=[zrb], w=[sgb])
            chk(49)
            act(lambda e: e.activation(out=sg_bf[0:32, 1, 0:Tt], in_=zR[0:32, cG1, 0:Tt], func=AF.Sigmoid), r=[zrb], w=[sgb])
            chk(50)

        def rwkv_step(g, hb_):
            tsl = slice(g * 128, (g + 1) * 128)
            c_lo = hb_ * NHH
            h0 = hb_ * HB
            wab, sgb, lrb = tk("wa_bf"), tk("sg_bf"), tk("lowr")
            flw, fcs, fa, fep, fen, ft1, ft2, ft3, fg, fgw, fgb, fbq = [tk(n) for n in (
                "f_lw", "f_cs", "f_a", "f_ep", "f_en", "f_t1", "f_cs", "f_lw", "f_g", "f_gw", "f_gb", "f_bq")]
            arb, btb_, ktb, vbb = tk("ar_b"), tk("bt_b"), tk("kt_b"), tk("v_b")
            rz, kz, vz = zR[:, cR + c_lo:cR + c_lo + NHH, tsl], zR[:, cK + c_lo:cK + c_lo + NHH, tsl], zR[:, cV + c_lo:cV + c_lo + NHH, tsl]
            for cc in range(NHH):
                csl = slice((c_lo + cc) * 128, (c_lo + cc + 1) * 128)
                pt, pb = bank()
                pe(lambda e, pt=pt, csl=csl: e.matmul(pt[:, 0:128], lhsT=lowr_bf[:, 0, csl], rhs=wa_bf[:, tsl], start=True, stop=True),
                   r=[lrb, wab], w=[pb])
                pe(lambda e, pt=pt, csl=csl: e.matmul(pt[:, 128:256], lhsT=lowr_bf[:, 3, csl], rhs=wa_bf[:, tsl], start=True, stop=True),
                   r=[lrb, wab], w=[pb])
                ptg, pbg = bank()
                pe(lambda e, ptg=ptg, csl=csl: e.matmul(ptg[:, 0:128], lhsT=lowr_bf[:, 1, csl], rhs=sg_bf[:, 0, tsl], start=True, stop=False),
                   r=[lrb, sgb], w=[pbg])
                pe(lambda e, ptg=ptg, csl=csl: e.matmul(ptg[:, 0:128], lhsT=lowr_bf[0:32, 2, csl], rhs=sg_bf[0:32, 1, tsl], start=False, stop=True),
                   r=[lrb, sgb], w=[pbg])
                act(lambda e, pt=pt, cc=cc: e.activation(out=f_lw[:, cc, :], in_=pt[:, 0:128], func=AF.Sigmoid, bias=pvc("w0", c_lo + cc)),
                    r=[pb, pvb], w=[flw])
                act(lambda e, pt=pt, cc=cc: e.activation(out=f_a[:, cc, :], in_=pt[:, 128:256], func=AF.Sigmoid, bias=pvc("a0", c_lo + cc)),
                    r=[pb, pvb], w=[fa])
                dve(lambda e, ptg=ptg, cc=cc: e.tensor_copy(out=f_g[:, cc, :], in_=ptg[:, 0:128]), r=[pbg], w=[fg])
            chk(601)
            dve(lambda e: e.tensor_scalar(out=f_lw[:], in0=f_lw[:], scalar1=-0.6065306597126334, scalar2=None, op0=ALU.mult), r=[flw], w=[flw])
            for cc in range(NHH):
                dve(lambda e, cc=cc: e.tensor_tensor_scan(out=f_cs[:, cc, :], data0=rmask[:], data1=f_lw[:, cc, :], initial=0.0,
                                                          op0=ALU.mult, op1=ALU.add), r=[flw, cst], w=[fcs])
            chk(602)
            fl3 = lambda t: t[:].rearrange("p c t -> p (c t)")
            act(lambda e: e.activation(out=fl3(f_ep), in_=fl3(f_cs), func=AF.Exp), r=[fcs], w=[fep])
            act(lambda e: e.activation(out=fl3(f_en), in_=fl3(f_cs), func=AF.Exp, scale=-1.0), r=[fcs], w=[fen])
            ep4 = f_ep[:].rearrange("p c (b t) -> p c b t", t=64)
            t34 = f_t3[:].rearrange("p c (b t) -> p c b t", t=64)
            pool(lambda e: e.memset(t34[:, :, :, 0:1], 1.0), w=[ft3])
            pool(lambda e: e.tensor_copy(out=t34[:, :, :, 1:64], in_=ep4[:, :, :, 0:63]), r=[fep], w=[ft3])
            chk(603)
            TT(dve, f_t1[:], kz, bc3(pvc("kk", c_lo, NHH), 128), ALU.mult, [zrb, pvb], [ft1])
            TT(pool, f_bq[:], f_t1[:], f_t1[:], ALU.mult, [ft1], [fbq])
            for c0 in range(0, NHH, 4):
                cn = min(4, NHH - c0)
                pt, pb = bank()
                for k in range(cn):
                    pe(lambda e, k=k, c0=c0, pt=pt: e.matmul(pt[:, k * 128:(k + 1) * 128], lhsT=blk[:], rhs=f_bq[:, c0 + k, :], start=True, stop=True),
                       r=[cst, fbq], w=[pb])
                o2 = f_t2[:, c0:c0 + cn, :].rearrange("p c t -> p (c t)")
                dve(lambda e, pt=pt, cn=cn, o2=o2: e.tensor_scalar(out=o2, in0=pt[:, 0:cn * 128], scalar1=1e-24, scalar2=None, op0=ALU.max),
                    r=[pb], w=[ft2])
                act(lambda e, o2=o2: e.activation(out=o2, in_=o2, func=AF.Ln), r=[ft2], w=[ft2])
                act(lambda e, o2=o2: e.activation(out=o2, in_=o2, func=AF.Exp, scale=-0.5), r=[ft2], w=[ft2])
            chk(604)
            TT(dve, f_t1[:], f_t1[:], f_t2[:], ALU.mult, [ft1, ft2], [ft1])
            TT(dve, ar_b[:, :, 0, :], f_t1[:], f_t3[:], ALU.mult, [ft1, ft3], [arb])
            TT(pool, f_t2[:], f_t1[:], f_a[:], ALU.mult, [ft1, fa], [ft2])
            TT(dve, bt_b[:], f_t2[:], f_en[:], ALU.mult, [ft2, fen], [btb_])
            TT(dve, f_t3[:], f_a[:], bc3(pvc("ka", c_lo, NHH), 128), ALU.mult, [fa, pvb, arb], [ft3])
            TT(dve, f_t3[:], f_t3[:], bc3(onemka[:, c_lo:c_lo + NHH], 128), ALU.add, [ft3, cst], [ft3])
            TT(dve, f_t3[:], f_t3[:], kz, ALU.mult, [ft3, zrb], [ft3])
            TT(pool, kt_b[:], f_t3[:], f_en[:], ALU.mult, [ft3, fen], [ktb])
            TT(dve, ar_b[:, :, 1, :], rz, f_ep[:], ALU.mult, [zrb, fep], [arb])
            chk(605)
            act(lambda e: e.copy(out=v_b[:], in_=vz), r=[zrb], w=[vbb])
            TT(dve, f_t2[:], f_t3[:], rz, ALU.mult, [ft3, zrb, btb_], [ft2])
            TT(dve, f_bq[:], f_t2[:], bc3(pvc("rk", c_lo, NHH), 128), ALU.mult, [ft2, pvb], [fbq])
            for c0 in range(0, NHH, 4):
                cn = min(4, NHH - c0)
                pt, pb = bank()
                for k in range(cn):
                    pe(lambda e, k=k, c0=c0, pt=pt: e.matmul(pt[:, k * 128:(k + 1) * 128], lhsT=blk[:], rhs=f_bq[:, c0 + k, :], start=True, stop=True),
                       r=[cst, fbq], w=[pb])
                TT(dve, f_t2[:, c0:c0 + cn, :], pt[:, 0:cn * 128].rearrange("p (c t) -> p c t", t=128), zR[:, cV + c_lo + c0:cV + c_lo + c0 + cn, tsl],
                   ALU.mult, [pb, zrb, ft2], [ft2])
            TT(dve, f_t2[:], f_t2[:], bc3(pvc("lnb", c_lo, NHH), 128), ALU.add, [ft2, pvb], [ft2])
            TT(dve, f_gb[:], f_t2[:], f_g[:], ALU.mult, [ft2, fg], [fgb])
            TT(pool, f_gw[:], f_g[:], bc3(pvc("lnw", c_lo, NHH), 128), ALU.mult, [fg, pvb], [fgw])
            armb, btmb, ktmb = tk("arm"), tk("btm"), tk("ktm")
            for hp in range(2):
                dve(lambda e, hp=hp: e.tensor_scalar(out=arm[:, :, hp, :, :].rearrange("p c a t -> p c (a t)"), in0=ar_b[:].rearrange("p c a t -> p c (a t)"),
                                                     scalar1=pmask[:, hp:hp + 1], scalar2=None, op0=ALU.mult), r=[arb, cst], w=[armb])
                act(lambda e, hp=hp: e.activation(out=btm[:, :, hp, :], in_=bt_b[:], func=AF.Copy, scale=pmask[:, hp:hp + 1]),
                    r=[btb_, cst], w=[btmb])
                act(lambda e, hp=hp: e.activation(out=ktm[:, :, hp, :], in_=kt_b[:], func=AF.Copy, scale=pmask[:, hp:hp + 1]),
                    r=[ktb, cst], w=[ktmb])
            chk(61)
            yield
            vtb, kkb, bbb = tk("Vtok"), tk("Ktok"), tk("Btok")
            for src, srcb, dst, dstb in ((v_b, vbb, Vtok, vtb), (kt_b, ktb, Ktok, kkb), (bt_b, btb_, Btok, bbb)):
                for c0 in range(0, NHH, 8):
                    cn = min(8, NHH - c0)
                    pt, pb = bank()
                    ptb = pt[:].bitcast(BF16)
                    for k in range(cn):
                        pe(lambda e, k=k, c0=c0, ptb=ptb, src=src: e.transpose(ptb[:, k * 128:(k + 1) * 128], src[:, c0 + k, :], ident_b[:]),
                           r=[srcb, cst], w=[pb])
                    act(lambda e, c0=c0, cn=cn, ptb=ptb, dst=dst: e.copy(out=dst[:, c0 * 128:(c0 + cn) * 128], in_=ptb[:, 0:cn * 128]),
                        r=[pb], w=[dstb])
            vtsb, ussb = tk("Vts"), tk("Uss")
            for s in range(2):
                dve(lambda e, s=s: e.tensor_scalar(out=Vts[:, s, :], in0=Vtok[:], scalar1=pmask[:, s:s + 1], scalar2=None, op0=ALU.mult),
                    r=[vtb, cst], w=[vtsb])
            chk(62)
            yield
            hstb, hbfb = tk("Hst"), tk("Hbf")
            xrb, akb = tk("XRB"), tk("AKR")
            ysbb = tk("Ysb")
            xb_ = [tk("Xp0"), tk("Xp1")]
            lb_ = [tk("Lp0"), tk("Lp1")]
            pb_ = [tk("Pp0"), tk("Pp1")]
            for hl in range(0, HB, 2):
                ptA, pbA = bank()
                ptB, pbB = bank()
                for d in range(2):
                    h = h0 + hl + d
                    cg, hp = h // 2, h % 2
                    cc = cg - c_lo
                    arv = ar_b[:, cc, :, :].rearrange("p a t -> p (a t)")
                    pe(lambda e, ptA=ptA, d=d, cc=cc, hp=hp, arv=arv: e.matmul(ptA[:, d * 256:(d + 1) * 256], lhsT=btm[:, cc, hp, :], rhs=arv,
                                                                             start=True, stop=True), r=[btmb, arb], w=[pbA])
                    pe(lambda e, ptB=ptB, d=d, cc=cc, hp=hp, arv=arv: e.matmul(ptB[:, d * 256:(d + 1) * 256], lhsT=ktm[:, cc, hp, :], rhs=arv,
                                                                             start=True, stop=True), r=[ktmb, arb], w=[pbB])
                TT(dve, XRB[:, hl:hl + 2, :, :].rearrange("p h a t -> p h (a t)"), ptA[:, 0:512].rearrange("p (h x) -> p h x", x=256),
                   m_xr[:].unsqueeze(1).to_broadcast([128, 2, 256]), ALU.mult, [pbA, cst], [xrb])
                TT(dve, AKR[:, hl:hl + 2, :, :].rearrange("p h a t -> p h (a t)"), ptB[:, 0:512].rearrange("p (h x) -> p h x", x=256),
                   m_xr[:].unsqueeze(1).to_broadcast([128, 2, 256]), ALU.mult, [pbB, cst], [akb])
            for hl0 in range(0, HB, 4):
                hn = min(4, HB - hl0)
                pt, pb = bank()
                for d in range(hn):
                    h = h0 + hl0 + d
                    cg, hp = h // 2, h % 2
                    cc = cg - c_lo
                    pe(lambda e, pt=pt, d=d, cc=cc, hp=hp: e.matmul(pt[:, d * 128:(d + 1) * 128], lhsT=arm[:, cc, hp, 0, :], rhs=bt_b[:, cc, :],
                                                                    start=True, stop=True), r=[armb, btb_], w=[pb])
                TT(dve, Lp[0][:, hl0:hl0 + hn, :], pt[:, 0:hn * 128].rearrange("p (h t) -> p h t", t=128),
                   m_sl[:].unsqueeze(1).to_broadcast([128, hn, 128]), ALU.mult, [pb, cst], [lb_[0]])
            chk(63)
            yield
            TT(pool, Pp[0][:], imx[:].unsqueeze(1).to_broadcast([128, HB, 128]), XRB[:, :, 0, :], ALU.subtract, [cst, xrb], [pb_[0]])

            def Xk(k, hl):
                return XRB[:, hl, 0, :] if k == 0 else Xp[k % 2][:, hl, :]

            def Xkb(k):
                return xrb if k == 0 else xb_[k % 2]
            for k in range(1, 6):
                pr, cu = (k - 1) % 2, k % 2
                for hl0 in range(0, HB, 4):
                    hn = min(4, HB - hl0)
                    pt, pb = bank()
                    for d in range(hn):
                        hl = hl0 + d
                        pe(lambda e, pt=pt, d=d, hl=hl, k=k, pr=pr: e.matmul(pt[:, d * 128:(d + 1) * 128], lhsT=Xk(k - 1, hl), rhs=Lp[pr][:, hl, :],
                                                                             start=True, stop=True), r=[Xkb(k - 1), lb_[pr]], w=[pb])
                    act(lambda e, pt=pt, hl0=hl0, hn=hn, cu=cu: e.copy(out=Lp[cu][:, hl0:hl0 + hn, :].rearrange("p h t -> p (h t)"), in_=pt[:, 0:hn * 128]),
                        r=[pb], w=[lb_[cu]])
                    if k <= 4:
                        pt2, pb2 = bank()
                        for d in range(hn):
                            hl = hl0 + d
                            pe(lambda e, pt2=pt2, d=d, hl=hl, k=k, pr=pr: e.matmul(pt2[:, d * 128:(d + 1) * 128], lhsT=Lp[pr][:, hl, :], rhs=Xk(k - 1, hl),
                                                                                   start=True, stop=True), r=[Xkb(k - 1), lb_[pr]], w=[pb2])
                        dve(lambda e, pt2=pt2, hl0=hl0, hn=hn, cu=cu: e.tensor_copy(out=Xp[cu][:, hl0:hl0 + hn, :].rearrange("p h t -> p (h t)"),
                                                                                    in_=pt2[:, 0:hn * 128]), r=[pb2], w=[xb_[cu]])
                for hl0 in range(0, HB, 4):
                    hn = min(4, HB - hl0)
                    pt, pb = bank()
                    for d in range(hn):
                        hl = hl0 + d
                        pe(lambda e, pt=pt, d=d, hl=hl, cu=cu, pr=pr: e.matmul(pt[:, d * 128:(d + 1) * 128], lhsT=Lp[cu][:, hl, :], rhs=Pp[pr][:, hl, :],
                                                                               start=True, stop=True), r=[lb_[cu], pb_[pr]], w=[pb])
                    TT(dve, Pp[cu][:, hl0:hl0 + hn, :].rearrange("p h t -> p (h t)"), pt[:, 0:hn * 128],
                       Pp[pr][:, hl0:hl0 + hn, :].rearrange("p h t -> p (h t)"), ALU.add, [pb, pb_[pr]], [pb_[cu]])
            chk(64)
            yield
            TTt = Pp[1]
            ttb = pb_[1]
            wsb_, usb_ = tk("Wsb"), tk("Usb")
            pt, pb = bank()
            for hl in range(HB):
                h = h0 + hl
                cg, hp = h // 2, h % 2
                cc = cg - c_lo
                psl = slice(hp * 64, hp * 64 + 64)
                for s in range(2):
                    ssl = slice(s * 64, s * 64 + 64)
                    pe(lambda e, pt=pt, hl=hl, cc=cc, cg=cg, hp=hp, s=s, ssl=ssl: e.matmul(pt[ssl, hl * 64:(hl + 1) * 64], lhsT=arm[:, cc, hp, 0, ssl],
                                                                                  rhs=Hbf[:, s, cg, :], start=True, stop=False),
                       r=[armb, hbfb], w=[pb])
                    pe(lambda e, pt=pt, hl=hl, h=h, s=s, ssl=ssl: e.matmul(pt[ssl, hl * 64:(hl + 1) * 64], lhsT=AKR[:, hl, 0, ssl],
                                                                         rhs=Vtok[:, hl * 64:(hl + 1) * 64], start=False, stop=True),
                       r=[akb, vtb], w=[pb])
            act(lambda e, pt=pt: e.copy(out=Wsb[:].rearrange("p h i -> p (h i)"), in_=pt[:, 0:HB * 64]), r=[pb], w=[wsb_])
            pt, pb = bank()
            for hl in range(HB):
                pe(lambda e, pt=pt, hl=hl: e.matmul(pt[:, hl * 64:(hl + 1) * 64], lhsT=TTt[:, hl, :], rhs=Wsb[:, hl, :], start=True, stop=True),
                   r=[ttb, wsb_], w=[pb])
            act(lambda e, pt=pt: e.activation(out=Usb[:].rearrange("p h i -> p (h i)"), in_=pt[:, 0:HB * 64], func=AF.Copy, scale=-1.0),
                r=[pb], w=[usb_])
            for s in range(2):
                dve(lambda e, s=s: e.tensor_scalar(out=Uss[:, s, :], in0=Usb[:].rearrange("p h i -> p (h i)"), scalar1=pmask[:, s:s + 1],
                                                   scalar2=None, op0=ALU.mult), r=[usb_, cst], w=[ussb])
            chk(65)
            yield
            pt, pb = bank()
            for hl in range(HB):
                h = h0 + hl
                cg, hp = h // 2, h % 2
                cc = cg - c_lo
                psl = slice(hp * 64, hp * 64 + 64)
                for s in range(2):
                    ssl = slice(s * 64, s * 64 + 64)
                    o = pt[ssl, hl * 64:(hl + 1) * 64]
                    pe(lambda e, o=o, cc=cc, cg=cg, hp=hp, s=s, ssl=ssl: e.matmul(o, lhsT=arm[:, cc, hp, 1, ssl], rhs=Hbf[:, s, cg, :], start=True, stop=False),
                       r=[armb, hbfb], w=[pb])
                    pe(lambda e, o=o, hl=hl, ssl=ssl: e.matmul(o, lhsT=XRB[:, hl, 1, ssl], rhs=Usb[:, hl, :], start=False, stop=False),
                       r=[xrb, usb_], w=[pb])
                    pe(lambda e, o=o, hl=hl, h=h, ssl=ssl: e.matmul(o, lhsT=AKR[:, hl, 1, ssl], rhs=Vtok[:, hl * 64:(hl + 1) * 64], start=False, stop=True),
                       r=[akb, vtb], w=[pb])
            act(lambda e, pt=pt, h0=h0: e.copy(out=Ysb[:, 0:HB, :].rearrange("p h i -> p (h i)"), in_=pt[:, 0:HB * 64]), r=[pb], w=[ysbb])
            chk(66)
            yield
            ncl = HB // 2
            cl0 = h0 // 2
            cll = 0
            for s in range(2):
                ssl = slice(s * 64, s * 64 + 64)
                pt, pb = bank()
                for hl in range(HB):
                    h = h0 + hl
                    cg, hp = h // 2, h % 2
                    cc = cg - c_lo
                    o = pt[hp * 64:hp * 64 + 64, (hl // 2) * 64:(hl // 2) * 64 + 64]
                    pe(lambda e, o=o, h=h, hl=hl, s=s: e.matmul(o, lhsT=Btok[:, hl * 64:(hl + 1) * 64], rhs=Uss[:, s, hl * 64:(hl + 1) * 64], start=True, stop=False),
                       r=[bbb, ussb], w=[pb])
                    pe(lambda e, hl=hl, o=o, h=h, s=s: e.matmul(o, lhsT=Ktok[:, hl * 64:(hl + 1) * 64], rhs=Vts[:, s, hl * 64:(hl + 1) * 64], start=False, stop=True),
                       r=[kkb, vtsb], w=[pb])
                htb = tk("Htmp")
                TT(dve, Htmp[:, 0:ncl, :], pt[:, 0:ncl * 64].rearrange("p (c i) -> p c i", i=64), Hst[:, s, cl0:cl0 + ncl, :], ALU.add,
                   [pb, hstb], [htb])
                TT(dve, Hst[:, s, cl0:cl0 + ncl, :], Htmp[:, 0:ncl, :], f_ep[:, 0:ncl, s * 64 + 63:s * 64 + 64].to_broadcast([128, ncl, 64]),
                   ALU.mult, [htb, fep], [hstb])
                act(lambda e, s=s, cl0=cl0, ncl=ncl: e.copy(out=Hbf[:, s, cl0:cl0 + ncl, :], in_=Hst[:, s, cl0:cl0 + ncl, :]), r=[hstb], w=[hbfb])
            chk(67)
            yield
            gsb, ysq_b = tk("gst"), tk("Ysq")
            dve(lambda e: e.tensor_reduce(out=gst[:, 0, :], in_=Ysb[:], axis=AX.X, op=ALU.add), r=[ysbb], w=[gsb])
            TT(pool, Ysq[:], Ysb[:], Ysb[:], ALU.mult, [ysbb], [ysq_b])
            dve(lambda e: e.tensor_reduce(out=gst[:, 1, :], in_=Ysq[:], axis=AX.X, op=ALU.add), r=[ysq_b], w=[gsb])
            dve(lambda e: e.tensor_scalar(out=gst[:, 0, :], in0=gst[:, 0, :], scalar1=1.0 / 64, scalar2=None, op0=ALU.mult), r=[gsb], w=[gsb])
            TT(dve, gst[:, 2, :], gst[:, 0, :], gst[:, 0, :], ALU.mult, [gsb], [gsb])
            dve(lambda e: e.scalar_tensor_tensor(out=gst[:, 3, :], in0=gst[:, 1, :], scalar=1.0 / 64, in1=gst[:, 2, :], op0=ALU.mult, op1=ALU.subtract),
                r=[gsb], w=[gsb])
            rstd(gst[:, 3, :], gst[:, 3, :], 1.0, LNX_I, [gsb], [gsb])
            TT(dve, Ysq[:], Ysb[:], bc3(gst[:, 0, :], 64), ALU.subtract, [ysbb, gsb, ysq_b], [ysq_b])
            TT(dve, Ysq[:], Ysq[:], bc3(gst[:, 3, :], 64), ALU.mult, [ysq_b, gsb], [ysq_b])
            yfl = Ysq[:].rearrange("p h i -> p (h i)")
            for c0 in range(0, NHH, 4):
                cn = min(4, NHH - c0)
                pt, pb = bank()
                for k in range(cn):
                    pe(lambda e, k=k, c0=c0, pt=pt: e.transpose(pt[:, k * 128:(k + 1) * 128], yfl[:, (c0 + k) * 128:(c0 + k + 1) * 128], ident_f[:]),
                       r=[ysq_b, cst], w=[pb])
                TT(dve, f_t1[:, c0:c0 + cn, :], pt[:, 0:cn * 128].rearrange("p (c t) -> p c t", t=128), f_gw[:, c0:c0 + cn, :], ALU.mult,
                   [pb, fgw, ft1], [ft1])
                TT(dve, yT[:, c_lo + c0:c_lo + c0 + cn, tsl], f_t1[:, c0:c0 + cn, :], f_gb[:, c0:c0 + cn, :], ALU.add, [ft1, fgb], [ytb])


        def m1_gen(nG, x_rows):
            Tt = 128 * nG
            chk(0)
            for g in range(nG):
                for s in range(2):
                    P.dma("sp", xt[s * 64:(s + 1) * 64, g, :], x_rows[g][s], writes=[xb])
            for g in range(nG):
                norm_to_hT(g, "nmix")
            yield
            chk(1)
            yield from in_proj(c.BCH, NCHA, Tt, zS, zsb)
            chk(2)

        def m2_gen(nG, first_prompt):
            Tt = 128 * nG
            ssd_prep(nG, first_prompt)
            yield
            chk(3)

            def ssd_all():
                for g in range(nG):
                    yield from ssd_step(g)
            ga, gb = ssd_all(), in_proj(c.ACH, 0, Tt, zR, zrb)
            while ga is not None or gb is not None:
                if ga is not None:
                    try:
                        next(ga)
                    except StopIteration:
                        ga = None
                if gb is not None:
                    try:
                        next(gb)
                        next(gb)
                    except StopIteration:
                        gb = None
                yield
            chk(4)
            chk(45)
            rwkv_prep(nG)
            yield
            chk(5)
            fence()
            yield

        def m3_gen(nG):
            for g in range(nG):
                for hb_ in range(NBH):
                    yield from rwkv_step(g, hb_)
            chk(6)
            fence()
            yield

        def dense_gen(nG, x_rows, y_rows, tidx):
            Tt = 128 * nG
            for g in range(nG):
                for s in range(2):
                    P.dma("sp", xt[s * 64:(s + 1) * 64, g, :], x_rows[g][s], writes=[xb])

            def ev_res(g, ng, pt, pb):
                TT(dve, xt[:, g, ng * NW:(ng + 1) * NW], xt[:, g, ng * NW:(ng + 1) * NW], pt[:, 0:NW], ALU.add, [xb, pb], [xb])
            yield from dense_tok(nG, wout_s, t_wout, KC // c.KPG_OUT, c.KPG_OUT, lambda k, g: (yT[:, k, g * 128:(g + 1) * 128], ytb), ev_res)
            for g in range(nG):
                P.dma("sp", x1s[tidx, g], xt[:, g, :], reads=[xb], writes=[x1b[tidx]])
            yield

        return m1_gen, m2_gen, m3_gen, dense_gen

    phases = [make_phases(i) for i in range(NPAR)]

    def interleave(ga, gb, fa=False, fb=False):
        while ga is not None or gb is not None:
            if ga is not None:
                extra_reads[:] = [AR] if fa else []
                try:
                    next(ga)
                except StopIteration:
                    ga = None
            if gb is not None:
                extra_reads[:] = [AR] if fb else []
                try:
                    next(gb)
                except StopIteration:
                    gb = None
        extra_reads[:] = []

    stb = [tk("carry"), tk("Hst"), tk("Hbf"), tk("Sst"), tk("Sbf"), tk("convst")]
    pool(lambda e: e.memset(zS_raw[:], 0.0), w=[tk("zS")])
    pool(lambda e: e.memset(zR_raw[:], 0.0), w=[tk("zR")])
    pool(lambda e: e.memset(carry[:], 0.0), w=[tk("carry")])
    pool(lambda e: e.memset(Hst[:], 0.0), w=[tk("Hst")])
    pool(lambda e: e.memset(Hbf[:], 0.0), w=[tk("Hbf")])
    pool(lambda e: e.memset(Sst[:], 0.0), w=[tk("Sst")])
    pool(lambda e: e.memset(Sbf[:], 0.0), w=[tk("Sbf")])
    pool(lambda e: e.memset(convst[:], 0.0), w=[tk("convst")])

    def x_of(step, s):
        if step == 0:
            return metac
        return xp[s, (step - 1) * 64:step * 64, :]

    def y_of(step, s):
        if step == 0:
            return None
        return yp[s, (step - 1) * 64:step * 64, :]

    def dump_states(i):
        P.dma("sp", o_shift[i], carry[:], reads=[tk("carry")])
        P.dma("sp", o_wkv[i], Hst[:], reads=[tk("Hst")])
        P.dma("sp", o_conv[i], convst[:], reads=[tk("convst")])
        P.dma("sp", o_ssm[i], Sst[:], reads=[tk("Sst")])

    def swap_states():
        dump_states(0)
        P.dma("sp", carry[:], st_shift, writes=[tk("carry")])
        P.dma("sp", Hst[:], st_wkv, writes=[tk("Hst")])
        P.dma("sp", convst[:], st_conv, writes=[tk("convst")])
        P.dma("sp", Sst[:], st_ssm, writes=[tk("Sst")])
        act(lambda e: e.copy(out=Hbf[:], in_=Hst[:]), r=[tk("Hst")], w=[tk("Hbf")])
        act(lambda e: e.copy(out=Sbf[:], in_=Sst[:]), r=[tk("Sst")], w=[tk("Sbf")])

    tiles = []
    for mt in range(c.NSTEP_P // G):
        steps = [mt * G + g for g in range(G)]
        tiles.append((G, [[x_of(st, s) for s in range(2)] for st in steps], [[y_of(st, s) for s in range(2)] for st in steps], mt == 0, False))
    tiles.append((1, [[xs[0], xs[1]]], [[ys[0], ys[1]]], False, True))
    NT = len(tiles)

    def mk_m1(t):
        nG, xr, yr, fp, smp = tiles[t]
        return phases[t % NPAR][0](nG, xr)

    def mk_m2(t):
        nG, xr, yr, fp, smp = tiles[t]
        if smp:
            swap_states()
        return phases[t % NPAR][1](nG, fp)

    def mk_m3(t):
        return phases[t % NPAR][2](tiles[t][0])

    def mk_dense(t):
        nG, xr, yr, fp, smp = tiles[t]
        return phases[t % NPAR][3](nG, xr, yr, t)

    try:
        if NPAR == 1:
            for t in range(NT):
                for mk, fl_ in ((mk_m1, False), (mk_m2, True), (mk_m3, True), (mk_dense, False)):
                    interleave(mk(t), None, fl_, False)
        else:
            interleave(mk_m1(0), None)
            interleave(mk_m2(0), None, True, False)
            for t in range(NT):
                interleave(mk_m3(t), mk_m1(t + 1) if t + 1 < NT else None, True, False)
                issue_pending(1)
                interleave(mk_dense(t), mk_m2(t + 1) if t + 1 < NT else None, False, True)
    except _Stop:
        P.finish()
        return nc, P
    dump_states(1)
    issue_pending(len(pending))
    P.barrier()
    rts = []
    for t in range(NT):
        nG, xr, yr, fp, smp = tiles[t]
        for g in range(nG):
            if yr[g][0] is not None or yr[g][1] is not None:
                rts.append((t, g, yr[g]))
    xFb = [Buf("xtF0"), Buf("xtF1")]
    hFb, aFb, svb, xnb = Buf("hTF"), Buf("actTF"), Buf("svF"), Buf("xnF")
    sqb = xnb
    smallF = small_all[0]
    ssb, rsb2 = Buf("ssF"), Buf("rstdF")
    ngrp = (len(rts) + GF - 1) // GF
    for gi in range(ngrp):
        grp = rts[gi * GF:(gi + 1) * GF]
        nj = len(grp)
        TFg = 128 * nj
        xf, xfb = xtF[gi % 2], xFb[gi % 2]
        for j, (t, g, yr) in enumerate(grp):
            rd = list(x1b) if gi < 2 else [x1b[t]]
            P.dma("sp", xf[:, j, :], x1s[t, g], reads=rd, writes=[xfb])
        for j in range(nj):
            act(lambda e, j=j, xf=xf: e.activation(out=sqF[:], in_=xf[:, j, :], func=AF.Square, accum_out=smallF[:, 4:5]), r=[xfb], w=[sqb, ssb])
            rstd(smallF[:, 4:5], smallF[:, 5:6], 1.0 / D, EPS_I, [ssb], [rsb2])
            act(lambda e, j=j, xf=xf: e.activation(out=xnF[:], in_=xf[:, j, :], func=AF.Copy, scale=smallF[:, 5:6]), r=[xfb, rsb2], w=[xnb])
            for k0 in range(0, KC, 4):
                kn = min(4, KC - k0)
                pt, pb = bank()
                ptb = pt[:].bitcast(BF16)
                for k in range(kn):
                    pe(lambda e, k=k, k0=k0, ptb=ptb: e.transpose(ptb[:, k * 128:(k + 1) * 128], xnF[:, (k0 + k) * 128:(k0 + k + 1) * 128], ident_b[:]),
                       r=[xnb, cst], w=[pb])
                TT(dve, hTF[:, k0:k0 + kn, j * 128:(j + 1) * 128], ptb[:, 0:kn * 128].rearrange("p (k t) -> p k t", t=128),
                   bc3(pvc("nffn", k0, kn), 128), ALU.mult, [pb, pvb], [hFb])
        for f in range(NF):
            wv, wb = load_w(wgu_s[f].rearrange("p a k n -> p (a k n)"), 2 * KC * 128, find(t_wgu, f))
            wv4 = wv.rearrange("p (a k n) -> p a k n", a=2, n=128)
            pg, pgb = bank()
            pu, pub = bank()
            for k in range(KC):
                pe(lambda e, k=k, pg=pg, wv4=wv4, TFg=TFg: e.matmul(pg[:, 0:TFg], lhsT=wv4[:, 0, k, :], rhs=hTF[:, k, 0:TFg], start=(k == 0), stop=(k == KC - 1)),
                   r=[wb, hFb], w=[pgb])
            for k in range(KC):
                pe(lambda e, k=k, pu=pu, wv4=wv4, TFg=TFg: e.matmul(pu[:, 0:TFg], lhsT=wv4[:, 1, k, :], rhs=hTF[:, k, 0:TFg], start=(k == 0), stop=(k == KC - 1)),
                   r=[wb, hFb], w=[pub])
            act(lambda e, pg=pg, TFg=TFg: e.activation(out=svF[:, 0:TFg], in_=pg[:, 0:TFg], func=AF.Silu), r=[pgb], w=[svb])
            TT(dve, actTF[:, f, 0:TFg], svF[:, 0:TFg], pu[:, 0:TFg], ALU.mult, [svb, pub], [aFb])
        nkg, kpg = NF // c.KPG_DN, c.KPG_DN
        for ng in range(c.NNG):
            banks = [bank() for _ in range(nj)]
            for kg in range(nkg):
                wv, wb = load_w(wdn_s[ng, kg].rearrange("p k n -> p (k n)"), kpg * NW, find(t_wdn, ng * nkg + kg))
                wv3 = wv.rearrange("p (k n) -> p k n", n=NW)
                for j in range(nj):
                    pt, pb = banks[j]
                    for kk in range(kpg):
                        kabs = kg * kpg + kk
                        pe(lambda e, pt=pt, j=j, wv3=wv3, kk=kk, kabs=kabs: e.matmul(
                            pt[:, 0:NW], lhsT=actTF[:, kabs, j * 128:(j + 1) * 128], rhs=wv3[:, kk, :], start=(kabs == 0), stop=(kabs == nkg * kpg - 1)),
                           r=[wb, aFb], w=[pb])
            for j in range(nj):
                pt, pb = banks[j]
                TT(dve, xf[:, j, ng * NW:(ng + 1) * NW], xf[:, j, ng * NW:(ng + 1) * NW], pt[:, 0:NW], ALU.add, [xfb, pb], [xfb])
        for j, (t, g, yr) in enumerate(grp):
            act(lambda e, j=j, xf=xf: e.activation(out=sqF[:], in_=xf[:, j, :], func=AF.Square, accum_out=smallF[:, 6:7]), r=[xfb], w=[sqb, ssb])
            rstd(smallF[:, 6:7], smallF[:, 7:8], 1.0 / D, EPS_I, [ssb], [rsb2])
            dve(lambda e, j=j, xf=xf: e.scalar_tensor_tensor(out=xf[:, j, :], in0=xf[:, j, :], scalar=smallF[:, 7:8], in1=nfw_bc[:], op0=ALU.mult, op1=ALU.mult),
                r=[xfb, rsb2, tk("nfw")], w=[xfb])
            for s in range(2):
                if yr[s] is not None:
                    P.dma("sp", yr[s], xf[s * 64:(s + 1) * 64, j, :], reads=[xfb])
    P.finish()
    return nc, P


def _fm(vec, chunks):
    out = np.zeros((128, len(chunks)), np.float32)
    for i, (c0, n) in enumerate(chunks):
        out[:n, i] = vec[c0:c0 + n]
    return out


def _fm_even(vec):
    return np.ascontiguousarray(np.asarray(vec, np.float32).reshape(-1, 128).T)


def host_weights(c, I):
    f = lambda k: np.asarray(I[k], np.float32)
    pvo, NPV = pv_layout(c)
    pv = np.zeros((128, NPV), np.float32)

    def put(name, arr):
        o, w = pvo[name]
        assert arr.shape == (128, w), (name, arr.shape, w)
        pv[:, o:o + w] = arr
    put("mu", _fm(f("rwkv_mu")[0], c.ACH))
    put("w0", _fm_even(f("rwkv_w0")[0]))
    put("a0", _fm_even(f("rwkv_a0")[0]))
    put("kk", _fm_even(f("rwkv_k_k")[0]))
    put("ka", _fm_even(f("rwkv_k_a")[0]))
    put("rk", _fm_even(f("rwkv_r_k")[0].reshape(-1)))
    put("lnw", _fm_even(f("rwkv_lnx_w")[0]))
    put("lnb", _fm_even(f("rwkv_lnx_b")[0]))
    cw = f("ssm_conv_w")[0]
    put("convw", np.ascontiguousarray(cw.reshape(c.NXBC, 128, 4).transpose(1, 0, 2)).reshape(128, c.NXBC * 4))
    put("convb", _fm_even(f("ssm_conv_b")[0]))
    put("snw", _fm_even(f("ssm_norm_w")[0]))
    put("dskip", _fm_even(np.repeat(f("ssm_D")[0], 64)))
    dtb = np.zeros((128, 1), np.float32)
    dtb[:c.NB, 0] = f("ssm_dt_bias")[0]
    put("dtb", dtb)
    put("nmix", _fm_even(f("norm_mix_w")[0]))
    put("nffn", _fm_even(f("norm_ffn_w")[0]))
    lowr = np.zeros((128, 4, c.AW), np.float32)
    lowr[0:64, 0] = f("rwkv_w_up")[0]
    lowr[64:128, 3] = f("rwkv_a_up")[0]
    gu = f("rwkv_g_up")[0]
    lowr[:, 1] = gu[0:128]
    lowr[0:32, 2] = gu[128:160]
    w_in = f("w_in")[0]
    KC = c.KC
    win_h = np.zeros((c.NCHIN, 128, KC, 128), np.float32)
    for ci, (c0, n) in enumerate(c.ACH + c.BCH):
        win_h[ci, :, :, :n] = w_in[:, c0:c0 + n].reshape(KC, 128, n).transpose(1, 0, 2)
    NW = c.NW

    def tokw(w, kpg):
        K, N = w.shape
        return np.ascontiguousarray(w.reshape(K // (128 * kpg), kpg, 128, N // NW, NW).transpose(3, 0, 2, 1, 4))
    wout_h = tokw(f("w_out")[0], c.KPG_OUT)
    wdn_h = tokw(f("ffn_w_down")[0], c.KPG_DN)
    wg = f("ffn_w_gate")[0].reshape(KC, 128, c.NF, 128)
    wu = f("ffn_w_up")[0].reshape(KC, 128, c.NF, 128)
    wgu_h = np.ascontiguousarray(np.stack([wg, wu], 0).transpose(3, 2, 0, 1, 4))
    metac = np.zeros((64, c.D), np.float32)
    metac[48:] = f("meta_tokens")
    return dict(pvec=pv, alog=f("ssm_A_log")[0][None, :].copy(), nfw=f("norm_final_w")[None, :].copy(), lowr=lowr,
                win_h=win_h, wout_h=wout_h, wgu_h=wgu_h, wdn_h=wdn_h, metac=metac)


def host_core_inputs(c, I, core):
    f = lambda k: np.asarray(I[k], np.float32)
    sl = slice(2 * core, 2 * core + 2)
    sh = f("state_rwkv_shift")[0][sl]
    st_shift = np.stack([_fm(sh[s], c.ACH) for s in range(2)], -1)
    wkv = f("state_rwkv_wkv")[0][sl]
    st_wkv = np.ascontiguousarray(wkv.reshape(2, c.NHP, 2, 64, 64).transpose(2, 4, 0, 1, 3).reshape(128, 2, c.NHP, 64))
    cv = f("state_ssm_conv")[0][sl]
    st_conv = np.ascontiguousarray(cv.reshape(2, 3, c.NXBC, 128).transpose(3, 2, 0, 1))
    sm = f("state_ssm")[0][sl]
    st_ssm = np.ascontiguousarray(sm.reshape(2, c.NB * 64, 128).transpose(2, 0, 1))
    return dict(xp=np.ascontiguousarray(f("x_prompt")[sl]), xs=np.ascontiguousarray(f("x_sample")[sl]),
                st_shift=np.ascontiguousarray(st_shift), st_wkv=st_wkv, st_conv=st_conv, st_ssm=st_ssm)


def host_unpack(c, r, pfx):
    sh = r[pfx + "_shift"]
    shift = np.zeros((2, c.ACOLS), np.float32)
    for ci, (c0, n) in enumerate(c.ACH):
        shift[:, c0:c0 + n] = sh[:n, ci, :].T
    wk = r[pfx + "_wkv"].reshape(2, 64, 2, c.NHP, 64)
    wkv = np.ascontiguousarray(wk.transpose(2, 3, 0, 4, 1)).reshape(2, c.NA, 64, 64)
    cv = r[pfx + "_conv"]
    conv = np.ascontiguousarray(cv.transpose(2, 3, 1, 0)).reshape(2, 3, c.CONVD)
    sm = r[pfx + "_ssm"]
    ssm = np.ascontiguousarray(sm.transpose(1, 2, 0)).reshape(2, c.NB, 64, 128)
    return shift, wkv, conv, ssm


_CACHE = {}


def run(c, I, runner=None):
    key = (c.D, c.SEQ, c.DFF, c.G)
    if key not in _CACHE:
        _CACHE[key] = build_program(c)[0]
    nc = _CACHE[key]
    W = host_weights(c, I)
    in_maps = []
    for core in range(c.NCORES):
        m = dict(W)
        m.update(host_core_inputs(c, I, core))
        in_maps.append(m)
    if runner is None:
        res = run_bass_kernel_spmd(nc, in_maps, core_ids=list(range(c.NCORES))).results
    else:
        res = runner(nc, in_maps)
    yp = np.concatenate([r["yp"] for r in res], 0)
    ys = np.concatenate([r["ys"] for r in res], 0)
    outs = [yp, ys]
    for pfx in ("p", "s"):
        parts = [host_unpack(c, r, pfx) for r in res]
        for j in range(4):
            outs.append(np.concatenate([p[j] for p in parts], 0)[None])
    return tuple(np.ascontiguousarray(o, dtype=np.float32) for o in outs)


def kernel(**inputs):
    return run(FULL, inputs)
```
